# Optimizing a Trainium2 kernel written in Bass

```python
import jax, jax.numpy as jnp
from jax import lax
import numpy as np

D_MODEL = 1024
BATCH = 4
SEQ = 4096
DEPTH = 4

PLE_DIM = 256
GDN_HEADS = 4
GDN_DK = 128
GDN_DV = 128
CONV_WIDTH = 4
MLSTM_HEADS = 4
MLSTM_DQK = 64
MLSTM_DV = 128
CHUNK = 64
D_FF = -(-(8 * D_MODEL) // (3 * 256)) * 256
GDN_QK = GDN_HEADS * GDN_DK
GDN_V = GDN_HEADS * GDN_DV
ML_QK = MLSTM_HEADS * MLSTM_DQK
ML_V = MLSTM_HEADS * MLSTM_DV
IN_SIZES = (GDN_QK, GDN_QK, GDN_V, GDN_V, GDN_HEADS, GDN_HEADS,
            ML_QK, ML_QK, ML_V, ML_V, MLSTM_HEADS, MLSTM_HEADS,
            D_MODEL, D_MODEL)
IN_WIDTH = 2 * GDN_QK + 2 * GDN_V + 2 * GDN_HEADS + 2 * ML_QK + 2 * ML_V + 2 * MLSTM_HEADS + 2 * D_MODEL
CONV_CH = 2 * GDN_QK + GDN_V
NORM_EPS = 1e-6

kernel_name = 'hybrid_gdn_mlstm_ple'


def _rmsnorm(x, gain):
    xf = x.astype(jnp.float32)
    y = xf * lax.rsqrt(jnp.mean(xf * xf, axis=-1, keepdims=True) + NORM_EPS)
    return (y * gain.astype(jnp.float32)).astype(x.dtype)


def _l2norm(x):
    return x * lax.rsqrt(jnp.sum(x * x, axis=-1, keepdims=True) + NORM_EPS)


def _causal_conv(x, w):
    c = x.shape[-1]
    return lax.conv_general_dilated(x, w[:, None, :].astype(x.dtype), window_strides=(1,),
                                    padding=[(CONV_WIDTH - 1, 0)],
                                    dimension_numbers=('NWC', 'WIO', 'NWC'),
                                    feature_group_count=c)


def _chunk_seq(t):
    b, s, h, d = t.shape
    return t.reshape(b, s // CHUNK, CHUNK, h, d).transpose(1, 0, 3, 2, 4)


def _chunk_gate(t):
    b, s, h = t.shape
    return t.reshape(b, s // CHUNK, CHUNK, h).transpose(1, 0, 3, 2)


def _unchunk(t):
    n, b, h, c, d = t.shape
    return t.transpose(1, 0, 3, 2, 4).reshape(b, n * c, h, d)


def gated_delta_rule(q, k, v, g, beta):
    dk = q.shape[-1]
    dv = v.shape[-1]
    q = _l2norm(q) * (dk ** -0.5)
    k = _l2norm(k)
    qc, kc, vc = _chunk_seq(q), _chunk_seq(k), _chunk_seq(v)
    gc = jnp.cumsum(_chunk_gate(g), axis=-1)
    bc = _chunk_gate(beta)
    incl = jnp.tril(jnp.ones((CHUNK, CHUNK), dtype=bool))
    strict = jnp.tril(jnp.ones((CHUNK, CHUNK), dtype=bool), -1)
    decay = jnp.exp(jnp.where(incl, gc[..., :, None] - gc[..., None, :], -jnp.inf))
    k_beta = kc * bc[..., None]
    m = jnp.where(strict, jnp.einsum('nbhcd,nbhsd->nbhcs', k_beta, kc) * decay, 0.0)
    a = m + jnp.eye(CHUNK, dtype=m.dtype)
    rhs = jnp.concatenate([vc * bc[..., None], k_beta * jnp.exp(gc)[..., None]], axis=-1)
    sol = lax.linalg.triangular_solve(a, rhs, left_side=True, lower=True, unit_diagonal=True)
    u, w = sol[..., :dv], sol[..., dv:]
    attn = jnp.einsum('nbhcd,nbhsd->nbhcs', qc, kc) * decay
    g_last = gc[..., -1]

    def step(state, inp):
        q_i, k_i, u_i, w_i, g_i, gl_i, attn_i = inp
        v_new = u_i - jnp.einsum('bhcd,bhde->bhce', w_i, state)
        o = (jnp.einsum('bhcd,bhde->bhce', q_i * jnp.exp(g_i)[..., None], state)
             + jnp.einsum('bhcs,bhse->bhce', attn_i, v_new))
        k_dec = k_i * jnp.exp(gl_i[..., None] - g_i)[..., None]
        state = state * jnp.exp(gl_i)[..., None, None] + jnp.einsum('bhcd,bhce->bhde', k_dec, v_new)
        return state, o

    init = jnp.zeros(qc.shape[1:3] + (dk, dv), jnp.float32)
    _, out = lax.scan(step, init, (qc, kc, u, w, gc, g_last, attn))
    return _unchunk(out)


def mlstm_chunkwise(q, k, v, i_pre, f_pre):
    dqk = q.shape[-1]
    qc = _chunk_seq(q * (dqk ** -0.5))
    kc, vc = _chunk_seq(k), _chunk_seq(v)
    ic = _chunk_gate(i_pre)
    bc = jnp.cumsum(_chunk_gate(jax.nn.log_sigmoid(f_pre)), axis=-1)
    incl = jnp.tril(jnp.ones((CHUNK, CHUNK), dtype=bool))

    def step(carry, inp):
        c_bar, n_bar, m = carry
        q_i, k_i, v_i, i_i, b_i = inp
        d_log = jnp.where(incl, b_i[..., :, None] - b_i[..., None, :] + i_i[..., None, :], -jnp.inf)
        m_inter = b_i + m[..., None]
        m_t = jnp.maximum(m_inter, jnp.max(d_log, axis=-1))
        w_inter = jnp.exp(m_inter - m_t)
        s = jnp.einsum('bhcd,bhsd->bhcs', q_i, k_i) * jnp.exp(d_log - m_t[..., None])
        num = (w_inter[..., None] * jnp.einsum('bhcd,bhde->bhce', q_i, c_bar)
               + jnp.einsum('bhcs,bhse->bhce', s, v_i))
        den = w_inter * jnp.einsum('bhcd,bhd->bhc', q_i, n_bar) + jnp.sum(s, axis=-1)
        h = num / jnp.maximum(jnp.abs(den), jnp.exp(-m_t))[..., None]
        b_last = b_i[..., -1]
        a_i = b_last[..., None] - b_i + i_i
        m_new = jnp.maximum(b_last + m, jnp.max(a_i, axis=-1))
        scale_prev = jnp.exp(b_last + m - m_new)
        k_w = k_i * jnp.exp(a_i - m_new[..., None])[..., None]
        c_bar = scale_prev[..., None, None] * c_bar + jnp.einsum('bhcd,bhce->bhde', k_w, v_i)
        n_bar = scale_prev[..., None] * n_bar + jnp.sum(k_w, axis=-2)
        return (c_bar, n_bar, m_new), h

    nb, bb, hh = qc.shape[0], qc.shape[1], qc.shape[2]
    init = (jnp.zeros((bb, hh, dqk, vc.shape[-1]), jnp.float32),
            jnp.zeros((bb, hh, dqk), jnp.float32),
            jnp.zeros((bb, hh), jnp.float32))
    _, out = lax.scan(step, init, (qc, kc, vc, ic, bc))
    return _unchunk(out)


def _head_rmsnorm(x, gain):
    return x * lax.rsqrt(jnp.mean(x * x, axis=-1, keepdims=True) + NORM_EPS) * gain.astype(jnp.float32)


def _layer(x, p_i, g_mix, w_in, conv_w, a_log, dt_bias, gdn_norm, ml_i_bias, ml_f_bias, ml_norm,
           w_branch_a, w_branch_b, w_out, g_ffn, w1, w3, w2, g_ple, w_ple_gate, w_ple):
    bsz, seq, _ = x.shape
    dt = x.dtype
    h = _rmsnorm(x, g_mix)
    proj = h @ w_in
    idx = [int(v) for v in np.cumsum(IN_SIZES)[:-1]]
    (gq, gk, gv, gz, gbeta, galpha, mq, mk, mv, mo, mi, mf, gate_a, gate_b) = jnp.split(proj, idx, axis=-1)

    qkv = jax.nn.silu(_causal_conv(jnp.concatenate([gq, gk, gv], axis=-1), conv_w))
    gq, gk, gv = jnp.split(qkv.astype(jnp.float32), [GDN_QK, 2 * GDN_QK], axis=-1)
    beta = jax.nn.sigmoid(gbeta.astype(jnp.float32))
    g = -jnp.exp(a_log.astype(jnp.float32)) * jax.nn.softplus(galpha.astype(jnp.float32) + dt_bias.astype(jnp.float32))
    o_a = gated_delta_rule(gq.reshape(bsz, seq, GDN_HEADS, GDN_DK),
                           gk.reshape(bsz, seq, GDN_HEADS, GDN_DK),
                           gv.reshape(bsz, seq, GDN_HEADS, GDN_DV), g, beta)
    o_a = _head_rmsnorm(o_a, gdn_norm) * jax.nn.silu(gz.astype(jnp.float32)).reshape(bsz, seq, GDN_HEADS, GDN_DV)
    y_a = o_a.reshape(bsz, seq, GDN_V).astype(dt) @ w_branch_a

    o_b = mlstm_chunkwise(mq.astype(jnp.float32).reshape(bsz, seq, MLSTM_HEADS, MLSTM_DQK),
                          mk.astype(jnp.float32).reshape(bsz, seq, MLSTM_HEADS, MLSTM_DQK),
                          mv.astype(jnp.float32).reshape(bsz, seq, MLSTM_HEADS, MLSTM_DV),
                          mi.astype(jnp.float32) + ml_i_bias.astype(jnp.float32),
                          mf.astype(jnp.float32) + ml_f_bias.astype(jnp.float32))
    o_b = _head_rmsnorm(o_b, ml_norm) * jax.nn.sigmoid(mo.astype(jnp.float32)).reshape(bsz, seq, MLSTM_HEADS, MLSTM_DV)
    y_b = o_b.reshape(bsz, seq, ML_V).astype(dt) @ w_branch_b

    mixed = jax.nn.sigmoid(gate_a) * y_a + jax.nn.sigmoid(gate_b) * y_b
    x = x + mixed @ w_out

    h = _rmsnorm(x, g_ffn)
    x = x + (jax.nn.silu(h @ w1) * (h @ w3)) @ w2

    gate = jax.nn.sigmoid(_rmsnorm(x, g_ple) @ w_ple_gate)
    x = x + gate * (p_i @ w_ple)
    return x


def setup_inputs(seed: int = 0) -> dict:
    key = jax.random.key(seed)
    ks = jax.random.split(key, 24)
    f32 = jnp.float32
    L, D = DEPTH, D_MODEL

    def nrm(k, shape, scale):
        return jax.random.normal(k, shape, f32) * scale

    def gain(k, shape):
        return 1.0 + 0.05 * jax.random.normal(k, shape, f32)

    dt0 = jnp.exp(jax.random.uniform(ks[5], (L, GDN_HEADS), f32, np.log(1e-3), np.log(1e-1)))
    return {
        'x': jax.random.normal(ks[0], (BATCH, SEQ, D), f32),
        'p': jax.random.normal(ks[1], (DEPTH, BATCH, SEQ, PLE_DIM), f32),
        'g_mix': gain(ks[2], (L, D)),
        'w_in': nrm(ks[3], (L, D, IN_WIDTH), D ** -0.5),
        'conv_w': nrm(ks[4], (L, CONV_WIDTH, CONV_CH), CONV_WIDTH ** -0.5),
        'a_log': jnp.log(jax.random.uniform(ks[6], (L, GDN_HEADS), f32, 1.0, 16.0)),
        'dt_bias': dt0 + jnp.log(-jnp.expm1(-dt0)),
        'gdn_norm': gain(ks[7], (L, GDN_DV)),
        'ml_i_bias': nrm(ks[8], (L, MLSTM_HEADS), 0.1),
        'ml_f_bias': jnp.linspace(3.0, 6.0, MLSTM_HEADS, dtype=f32)[None, :] + nrm(ks[9], (L, MLSTM_HEADS), 0.1),
        'ml_norm': gain(ks[10], (L, MLSTM_HEADS, MLSTM_DV)),
        'w_branch_a': nrm(ks[11], (L, GDN_V, D), GDN_V ** -0.5),
        'w_branch_b': nrm(ks[12], (L, ML_V, D), ML_V ** -0.5),
        'w_out': nrm(ks[13], (L, D, D), D ** -0.5),
        'g_ffn': gain(ks[14], (L, D)),
        'w1': nrm(ks[15], (L, D, D_FF), D ** -0.5),
        'w3': nrm(ks[16], (L, D, D_FF), D ** -0.5),
        'w2': nrm(ks[17], (L, D_FF, D), D_FF ** -0.5),
        'g_ple': gain(ks[18], (L, D)),
        'w_ple_gate': nrm(ks[19], (L, D, D), D ** -0.5),
        'w_ple': nrm(ks[20], (L, PLE_DIM, D), PLE_DIM ** -0.5),
        'g_final': gain(ks[21], (D,)),
    }


def reference(x, p, g_mix, w_in, conv_w, a_log, dt_bias, gdn_norm, ml_i_bias, ml_f_bias, ml_norm,
              w_branch_a, w_branch_b, w_out, g_ffn, w1, w3, w2, g_ple, w_ple_gate, w_ple, g_final):
    for i in range(DEPTH):
        x = _layer(x, p[i], g_mix[i], w_in[i], conv_w[i], a_log[i], dt_bias[i], gdn_norm[i],
                   ml_i_bias[i], ml_f_bias[i], ml_norm[i], w_branch_a[i], w_branch_b[i], w_out[i],
                   g_ffn[i], w1[i], w3[i], w2[i], g_ple[i], w_ple_gate[i], w_ple[i])
    return _rmsnorm(x, g_final)
```

```python
import contextlib
import numpy as np
import concourse.bass as bass
import concourse.mybir as mybir
from concourse.bass_utils import run_bass_kernel_spmd

F32 = mybir.dt.float32
BF16 = mybir.dt.bfloat16
AF = mybir.ActivationFunctionType
ALU = mybir.AluOpType

ENGS = ("pe", "act", "dve", "pool", "sp")
EPOCH = 30000
NEPOCH = 8
NDMA_SEM = 8


class Op:
    __slots__ = ("eng", "fn", "deps", "is_dma", "milestone", "dma_slot", "dma_cnt", "needed",
                 "dma_prev", "final", "phase")

    def __init__(self, eng, fn, is_dma, phase):
        self.eng = eng
        self.fn = fn
        self.deps = []
        self.is_dma = is_dma
        self.milestone = None
        self.needed = False
        self.dma_slot = None
        self.dma_cnt = None
        self.dma_prev = None
        self.final = False
        self.phase = phase


class Prog:
    def __init__(self, nc, stack):
        self.nc = nc
        self.phase = 0
        self.pending = {e: [] for e in ENGS}
        self.last_writer = {}
        self.readers = {}
        self.dma_count = {e: 0 for e in ENGS}
        self.dma_chain = {e: [None] * NDMA_SEM for e in ENGS}
        self.ms_count = {e: 0 for e in ENGS}
        self.last_op = {e: None for e in ENGS}
        self.barrier = {e: [] for e in ENGS}
        self.known = {e: {} for e in ENGS}
        self.sems = {e: [stack.enter_context(nc.semaphore(f"s_{e}_{i}")) for i in range(NEPOCH)]
                     for e in ("pe", "act", "dve", "pool")}
        self.dsems = {e: [stack.enter_context(nc.semaphore(f"d_{e}_{i}")) for i in range(NDMA_SEM)]
                      for e in ("sp", "pool", "act")}
        self.n_ops = 0

    def op(self, eng, fn, reads=(), writes=(), dma=False):
        o = Op(eng, fn, dma, self.phase)
        deps = set()
        for b in reads:
            for w in self.last_writer.get(b, ()):
                deps.add(w)
        for b in writes:
            for w in self.last_writer.get(b, ()):
                deps.add(w)
            for r in self.readers.get(b, ()):
                deps.add(r)
        for d in deps:
            if d.phase < self.phase:
                continue
            if d.eng == "pe" and eng == "pe" and not d.is_dma and not dma:
                continue
            o.deps.append(d)
            d.needed = True
        if self.barrier[eng]:
            o.deps.extend(self.barrier[eng])
            self.barrier[eng] = []
        for b in reads:
            self.readers.setdefault(b, []).append(o)
        for b in writes:
            prev = self.last_writer.get(b, [])
            if dma and prev and all(w.is_dma for w in prev) and not self.readers.get(b):
                self.last_writer[b] = prev + [o]
            else:
                self.last_writer[b] = [o]
            self.readers[b] = []
        if dma:
            n = self.dma_count[eng]
            self.dma_count[eng] = n + 1
            slot = n % NDMA_SEM
            o.dma_slot = slot
            o.dma_cnt = n // NDMA_SEM + 1
            o.dma_prev = self.dma_chain[eng][slot]
            self.dma_chain[eng][slot] = o
            o.needed = True
        else:
            self.last_op[eng] = o
        self.pending[eng].append(o)
        self.n_ops += 1
        return o

    def _wait_for(self, e, engobj, d):
        known = self.known[e]
        if d.is_dma:
            key = ("d", d.eng, d.dma_slot)
            val = d.dma_cnt * 16
            sem = self.dsems[d.eng][d.dma_slot]
        else:
            ep = (d.milestone - 1) // EPOCH
            key = ("c", d.eng, ep)
            val = d.milestone - ep * EPOCH
            sem = self.sems[d.eng][ep]
        if known.get(key, 0) >= val:
            return
        engobj.wait_ge(sem, val)
        known[key] = val

    def emit(self, final=False):
        nc = self.nc
        tails = []
        for e in ENGS:
            o = self.last_op[e]
            if o is not None:
                o.needed = True
                tails.append(o)
            for d in self.dma_chain[e]:
                if d is not None:
                    tails.append(d)
        for e in ENGS:
            for o in self.pending[e]:
                if not o.is_dma and o.needed:
                    self.ms_count[e] += 1
                    o.milestone = self.ms_count[e]
            assert self.ms_count[e] < EPOCH * NEPOCH, (e, self.ms_count[e])
        finals = [o for e in ENGS for o in self.pending[e] if o.final]

        def run(e, engobj):
            for o in self.pending[e]:
                for d in o.deps:
                    self._wait_for(e, engobj, d)
                if o.is_dma and o.dma_prev is not None:
                    self._wait_for(e, engobj, o.dma_prev)
                ins = o.fn(engobj)
                if o.is_dma:
                    ins.then_inc(self.dsems[e][o.dma_slot], 16)
                elif o.needed:
                    ep = (o.milestone - 1) // EPOCH
                    ins.then_inc(self.sems[e][ep], 1)
            if e == "sp" and final:
                for o in finals:
                    self._wait_for(e, engobj, o)

        with nc.Block() as block:
            block.tensor(lambda eng: run("pe", eng))
            block.scalar(lambda eng: run("act", eng))
            block.vector(lambda eng: run("dve", eng))
            block.gpsimd(lambda eng: run("pool", eng))
            block.sync(lambda eng: run("sp", eng))
        self.pending = {e: [] for e in ENGS}
        self.phase += 1
        for e in ENGS:
            self.barrier[e] = list(tails)


D = 1024
KD = 8
TT = 256
CH = 64
IN_W = 5648
D_FF = 2816
NFF = 22
EPS = 1e-6
GQ, GK, GV, GZ, GB, GA = 0, 512, 1024, 1536, 2048, 2052
MQ, MK, MV, MO, MI, MF = 2056, 2312, 2568, 3080, 3592, 3596
GTA, GTB = 3600, 4624
W1A = 3600
V_GMIX, V_GFFN, V_GPLE, V_CONV, V_GDNN, V_MLN, V_ALOG, V_DTB, V_IB, V_FB, NV = 0, 8, 16, 24, 72, 73, 77, 81, 85, 89, 93
C_ID, C_U, C_SL, C_MLN, C_MS01, C_MUN, C_ONE, NCONST = 0, 128, 192, 256, 320, 384, 448, 576
NEG = -30000.0


def make_consts():
    c = np.zeros((128, NCONST), np.float32)
    c[:, C_ID:C_ID + 128] = np.eye(128, dtype=np.float32)
    i = np.arange(64)
    c[:64, C_U:C_U + 64] = (i[:, None] <= i[None, :])
    c[:64, C_SL:C_SL + 64] = (i[:, None] > i[None, :])
    c[:64, C_MLN:C_MLN + 64] = np.where(i[None, :] <= i[:, None], 0.0, NEG)
    c[:64, C_MS01:C_MS01 + 64] = (i[None, :] < i[:, None])
    c[:64, C_MUN:C_MUN + 64] = np.where(i[:, None] <= i[None, :], 0.0, NEG)
    c[:, C_ONE:C_ONE + 128] = 1.0
    return c


class Builder:
    def __init__(self, S, L, last_final=True, x_in_name="xT", dbg=False):
        self.S, self.L = S, L
        self.NT = S // TT
        nc = self.nc = bass.Bass("TRN2", target_bir_lowering=False)
        self.stack = contextlib.ExitStack()
        dr = lambda name, shape, dt=F32, kind="ExternalInput": nc.dram_tensor(name, shape, dt, kind=kind).ap()
        self.xT = dr("xT", [D, S])
        self.pT = dr("pT", [L, 256, S])
        self.vec = dr("vec", [128, L * NV])
        self.gfin = dr("gfin", [128, KD])
        self.cst = dr("cst", [128, NCONST])
        self.w_in = dr("w_in", [L, D, IN_W])
        self.w_ba = dr("w_branch_a", [L, 512, D])
        self.w_bb = dr("w_branch_b", [L, 512, D])
        self.w_out = dr("w_out", [L, D, D])
        self.w1 = dr("w1", [L, D, D_FF])
        self.w3 = dr("w3", [L, D, D_FF])
        self.w2 = dr("w2", [L, D_FF, D])
        self.w_pg = dr("w_ple_gate", [L, D, D])
        self.w_ple = dr("w_ple", [L, 256, D])
        self.outT = dr("outT", [D, S], kind="ExternalOutput")
        self.xa = nc.dram_tensor("xa_s", [D, S], F32).ap()
        self.xb = nc.dram_tensor("xb_s", [D, S], F32).ap()
        self.oa_d = nc.dram_tensor("oa_s", [512, S], BF16).ap()
        self.ob_d = nc.dram_tensor("ob_s", [512, S], BF16).ap()
        self.dbg = {}
        self.want_dbg = dbg

    BAD_LO, BAD_HI = 176000, 180288

    def sb(self, st, name, shape, dt):
        nm = f"{name}_{self.P.phase}"
        cm = self.nc.sbuf_tensor(nm, shape, dt)
        t = cm.__enter__()
        addr = int(self.nc.lookup_mloc(t).addr)
        nbytes = int(np.prod(shape[1:])) * (2 if dt == BF16 else 4)
        if addr < self.BAD_HI and addr + nbytes > self.BAD_LO:
            cm.__exit__(None, None, None)
            st.enter_context(self.nc.sbuf_tensor(f"hole_{nm}", [128, (self.BAD_HI - addr + 3) // 4], F32))
            return st.enter_context(self.nc.sbuf_tensor(nm + "_r", shape, dt))
        st.push(cm)
        return t

    def ps(self, st, name, shape, dt):
        return st.enter_context(self.nc.psum_tensor(f"{name}_{self.P.phase}", shape, dt))

    def mm(self, out, lhsT, rhs, start, stop, reads, writes):
        return self.P.op("pe", lambda e: e.matmul(out, lhsT=lhsT, rhs=rhs, start=start, stop=stop), reads, writes)

    def tr(self, out, in_, ident, reads, writes):
        return self.P.op("pe", lambda e: e.transpose(out=out, in_=in_, identity=ident), reads, writes)

    def act(self, out, in_, func, reads, writes, bias=None, scale=None, accum=None):
        kw = {}
        if bias is not None:
            kw["bias"] = bias
        if scale is not None:
            kw["scale"] = scale
        if accum is not None:
            kw["accum_out"] = accum
        return self.P.op("act", lambda e: e.activation(out=out, in_=in_, func=func, **kw), reads, writes)

    def tt(self, eng, out, in0, in1, op, reads, writes):
        return self.P.op(eng, lambda e: e.tensor_tensor(out=out, in0=in0, in1=in1, op=op), reads, writes)

    def ts(self, eng, out, in0, s1, s2, op0, op1, reads, writes):
        if op1 is None:
            return self.P.op(eng, lambda e: e.tensor_scalar(out=out, in0=in0, scalar1=s1, scalar2=None, op0=op0), reads, writes)
        return self.P.op(eng, lambda e: e.tensor_scalar(out=out, in0=in0, scalar1=s1, scalar2=s2, op0=op0, op1=op1), reads, writes)

    def stt(self, eng, out, in0, scalar, in1, op0, op1, reads, writes):
        return self.P.op(eng, lambda e: e.scalar_tensor_tensor(out=out, in0=in0, scalar=scalar, in1=in1, op0=op0, op1=op1), reads, writes)

    def cp(self, eng, out, in_, reads, writes):
        if eng == "act":
            return self.P.op("act", lambda e: e.copy(out=out, in_=in_), reads, writes)
        return self.P.op(eng, lambda e: e.tensor_copy(out=out, in_=in_), reads, writes)

    def dma(self, eng, out, in_, reads, writes, final=False):
        o = self.P.op(eng, lambda e: e.dma_start(out=out, in_=in_), reads, writes, dma=True)
        o.final = final
        return o

    def cut(self, k):
        import os
        return int(os.environ.get("MK_CUT", "99")) <= k

    def tap(self, name, src_ap, shape, reads):
        if not self.want_dbg or name in self.dbg:
            return
        t = self.nc.dram_tensor("dbg_" + name, list(shape), F32, kind="ExternalOutput").ap()
        self.dbg[name] = t
        self.dma("sp", t, src_ap, reads, [], final=True)

    def build(self):
        nc = self.nc
        with self.stack as top:
            self.P = P = Prog(nc, top)
            self.cf = self.sb(top, "cf", [128, NCONST], F32)
            self.identb = self.sb(top, "identb", [128, 128], BF16)
            self.onesb = self.sb(top, "onesb", [128, 128], BF16)
            self.vecs = self.sb(top, "vecs", [128, self.L * NV], F32)
            self.g32 = self.sb(top, "g32", [128, self.L * 24 + 8], F32)
            self.posa = self.sb(top, "posa", [128, self.L * 4], F32)
            self.nega = self.sb(top, "nega", [128, self.L * 4], F32)
            self.gfs = self.sb(top, "gfs", [128, KD], F32)
            self.dma("sp", self.cf[:], self.cst, [], ["cf"])
            self.dma("sp", self.vecs[:], self.vec, [], ["vecs"])
            self.dma("sp", self.gfs[:], self.gfin, [], ["gfs"])
            self.cp("dve", self.identb[:], self.cf[:, C_ID:C_ID + 128], ["cf"], ["identb"])
            self.cp("dve", self.onesb[:], self.cf[:, C_ONE:C_ONE + 128], ["cf"], ["onesb"])
            for l in range(self.L):
                self.ts("dve", self.g32[:, l * 24:(l + 1) * 24], self.vecs[:, l * NV:l * NV + 24], 32.0, None,
                        ALU.mult, None, ["vecs"], ["g32"])
                self.act(self.posa[:, l * 4:(l + 1) * 4], self.vecs[:, l * NV + V_ALOG:l * NV + V_ALOG + 4], AF.Exp,
                         ["vecs"], ["posa"])
            self.ts("dve", self.g32[:, self.L * 24:self.L * 24 + 8], self.gfs[:], 32.0, None, ALU.mult, None, ["gfs"], ["g32"])
            self.ts("dve", self.nega[:], self.posa[:], -1.0, None, ALU.mult, None, ["posa"], ["nega"])
            for l in range(self.L):
                src = self.xT if l == 0 else self.xb
                last = (l == self.L - 1)
                import os
                stop = int(os.environ.get("MK_STOP", "99"))
                with contextlib.ExitStack() as st:
                    if stop >= 1:
                        self.phase_1a(st, l, src)
                    P.emit()
                with contextlib.ExitStack() as st:
                    if stop >= 2:
                        self.phase_1b(st, l, src)
                    P.emit()
                with contextlib.ExitStack() as st:
                    if stop >= 3:
                        self.phase_2(st, l, last)
                    P.emit(final=last)
        return nc

    def load_w(self, dst, src2d, kc_n, ncols, key, col0=0, split=2):
        step = (ncols + split - 1) // split
        for kc in range(kc_n):
            for c0 in range(0, ncols, step):
                c1 = min(ncols, c0 + step)
                self.dma("pool", dst[:, kc, c0:c1], src2d[kc * 128:(kc + 1) * 128, col0 + c0:col0 + c1], [], [key])

    def rmsnorm(self, xin, sq, hT, rstd, pss, gcol, keys):
        kx, ksq, kh, krs, kps = keys
        self.act(sq[:], xin[:], AF.Square, [kx], [ksq])
        for kc in range(KD):
            self.mm(pss, self.onesb[:], sq[:, kc, :], kc == 0, kc == KD - 1, [ksq, "onesb"], [kps])
        self.act(rstd[:], pss, AF.Ln, [kps], [krs], bias=float(D * EPS))
        self.act(rstd[:], rstd[:], AF.Exp, [krs], [krs], scale=-0.5)
        for kc in range(KD):
            self.stt("dve", hT[:, kc, :], xin[:, kc, :], self.g32[:, gcol + kc:gcol + kc + 1], rstd[:],
                     ALU.mult, ALU.mult, [kx, krs, "g32"], [kh])

    def phase_1a(self, st, l, xsrc):
        P, S = self.P, self.S
        sb, ps = self.sb, self.ps
        vb = l * NV
        W = sb(st, "W1a", [128, KD, W1A], BF16)
        self.load_w(W, self.w_in[l], KD, W1A, "W1a", split=3)
        if self.cut(0):
            return
        xin = sb(st, "xin", [128, KD, TT], F32)
        sq = sb(st, "sq", [128, KD, TT], BF16)
        hT = sb(st, "hT", [128, KD, TT], BF16)
        rstd = sb(st, "rstd", [128, TT], F32)
        raw = [sb(st, f"raw{i}", [128, TT + 3], F32) for i in range(2)]
        halo = sb(st, "halo", [128, 12, 3], F32)
        acc = [sb(st, f"acc{i}", [128, TT], F32) for i in range(2)]
        sl = [sb(st, f"sl{i}", [128, TT], F32) for i in range(2)]
        sqk = [sb(st, f"sqk{i}", [128, TT], BF16) for i in range(2)]
        rn = [sb(st, f"rn{i}", [128, TT], F32) for i in range(2)]
        gqT = sb(st, "gqT", [128, 4, TT], BF16)
        gkT = sb(st, "gkT", [128, 4, TT], BF16)
        gvT = sb(st, "gvT", [128, 4, TT], BF16)
        gzT = sb(st, "gzT", [128, 4, TT], BF16)
        mqT = sb(st, "mqT", [64, 4, TT], BF16)
        mkT = sb(st, "mkT", [64, 4, TT], BF16)
        mvT = sb(st, "mvT", [128, 4, TT], BF16)
        moT = sb(st, "moT", [128, 4, TT], BF16)
        ztmp = [sb(st, f"ztmp{i}", [128, TT], F32) for i in range(2)]
        gv_tok = sb(st, "gv_tok", [64, 4, 4, 128], BF16)
        gk_tok = sb(st, "gk_tok", [64, 4, 4, 128], BF16)
        mv_tok = sb(st, "mv_tok", [64, 4, 4, 128], BF16)
        mk_tok = sb(st, "mk_tok", [64, 4, 4, 64], BF16)
        oaT = sb(st, "oaT", [128, 4, TT], BF16)
        obT = sb(st, "obT", [128, 4, TT], BF16)
        Sg = sb(st, "Sg", [128, 4, 128], F32)
        Sgb = sb(st, "Sgb", [128, 4, 128], BF16)
        Cm = sb(st, "Cm", [64, 4, 128], F32)
        Cmb = sb(st, "Cmb", [64, 4, 128], BF16)
        nm = sb(st, "nm", [64, 4], F32)
        nmb = sb(st, "nmb", [64, 4], BF16)
        T = {}
        def tmp(name, shape, dt):
            T[name] = sb(st, name, shape, dt)
            return T[name]
        tmp("g_at", [64, 4], F32); tmp("beta", [64, 4], F32); tmp("g_ea", [64, 4], F32); tmp("g_sp", [64, 4], F32)
        tmp("g_g", [64, 4], F32); tmp("g_ng", [64, 4], F32); tmp("Gdn", [64, 4, 64], F32); tmp("gbc", [64, 4, 64], F32)
        tmp("e8", [64, 8], F32); tmp("egl", [128, 4], F32); tmp("bg", [64, 4], F32)
        tmp("Ef", [64, 4, 64], F32); tmp("dtm", [64, 4, 64], F32); tmp("T1", [64, 4, 64], F32); tmp("T2", [64, 4, 64], F32)
        tmp("attn", [64, 256], BF16); tmp("NAT", [64, 512], BF16)
        for j in range(6):
            tmp(f"NM{j}", [64, 512], BF16)
            tmp(f"IpN{j}", [64, 4, 64], BF16)
        tmp("X0", [64, 4, 256], BF16); tmp("X1", [64, 4, 256], BF16)
        tmp("u", [64, 512], F32); tmp("wT", [128, 256], BF16); tmp("vnew", [64, 512], BF16)
        tmp("qs", [64, 4, 128], F32); tmp("o", [64, 4, 128], F32); tmp("kdec", [64, 4, 128], BF16)
        tmp("junk", [64, 128], F32); tmp("ssq", [64, 4], F32); tmp("rs", [64, 4], F32); tmp("on", [64, 4, 128], BF16)
        tmp("m_ig", [64, 4], F32); tmp("m_fg", [64, 4], F32); tmp("m_ef", [64, 4], F32); tmp("m_sp", [64, 4], F32)
        tmp("m_lf", [64, 4], F32); tmp("Ld", [64, 4, 64], F32); tmp("nlb", [64, 4, 64], F32); tmp("ibc", [64, 4, 64], F32)
        tmp("m_eb", [64, 4], F32); tmp("m_ebl", [64, 4], F32); tmp("m_a", [64, 4], F32); tmp("m_ea", [64, 4], F32)
        tmp("sT", [64, 256], BF16); tmp("nume", [64, 4, 128], F32); tmp("num", [64, 4, 128], F32)
        tmp("den", [64, 4], F32); tmp("dene", [64, 4], F32); tmp("rr", [64, 4], F32); tmp("sc", [64, 4], F32)
        tmp("hn", [64, 4, 128], BF16); tmp("kw", [64, 4, 64], BF16)
        ppb = [ps(st, f"ppb{i}", [128, 512], F32) for i in range(2)]
        pp = [ppb[i][:, 0:256] for i in range(2)]
        ptrb = ps(st, "ptrb", [128, 1024], BF16)
        ptr = [ptrb[:, i * 512:(i + 1) * 512] for i in range(2)]
        psm = ps(st, "psm", [128, 512], F32)
        pA = ps(st, "pA", [128, 512], F32)
        pB = ps(st, "pB", [128, 512], F32)
        pCD = ps(st, "pCD", [128, 1024], F32)
        cf = self.cf
        Uf = cf[0:64, C_U:C_U + 64]
        SLf = cf[0:64, C_SL:C_SL + 64]
        ones64 = cf[0:64, C_ONE:C_ONE + 64]
        ones64x128 = cf[0:64, C_ONE:C_ONE + 128]
        idb64 = self.identb[0:64, 0:64]
        SQ128 = float(np.sqrt(128.0))

        def bc_h(ap64):
            return ap64[:, None, :].broadcast_to([64, 4, 64])

        def bc_last(ap, n):
            return ap[:, :, None].broadcast_to([64, 4, n])

        self.P.op("pool", lambda e: e.memset(Sg[:], 0.0), [], ["Sg"])
        self.P.op("pool", lambda e: e.memset(Sgb[:], 0.0), [], ["Sgb"])
        self.P.op("pool", lambda e: e.memset(Cm[:], 0.0), [], ["Cm"])
        self.P.op("pool", lambda e: e.memset(Cmb[:], 0.0), [], ["Cmb"])
        self.P.op("pool", lambda e: e.memset(nm[:], 0.0), [], ["nm"])
        self.P.op("pool", lambda e: e.memset(nmb[:], 0.0), [], ["nmb"])
        self.P.op("pool", lambda e: e.memset(halo[:], 0.0), [], ["halo"])

        ppi = [0]

        def proj(col0, ncols=128):
            i = ppi[0] % 2
            ppi[0] += 1
            for kc in range(KD):
                self.mm(pp[i][0:ncols, :], W[:, kc, col0:col0 + ncols], hT[:, kc, :], kc == 0, kc == KD - 1,
                        ["W1a", "hT"], [f"pp{i}"])
            return pp[i], f"pp{i}"

        for tt in range(self.NT):
            t0 = tt * TT
            self.dma("sp", xin[:], xsrc[:, t0:t0 + TT].rearrange("(k p) t -> p k t", p=128), [], ["xin"])
            self.rmsnorm(xin, sq, hT, rstd, psm[:, 256:512], l * 24 + 0, ("xin", "sq", "hT", "rstd", "psm"))
            if self.cut(1):
                return
            import os
            ncc = int(os.environ.get("MK_NCC", "12")); stg = int(os.environ.get("MK_ST", "9"))
            for cc in range(ncc):
                pt, pk = proj(cc * 128)
                r = raw[cc % 2]; rk = f"raw{cc % 2}"
                a = acc[cc % 2]; ak = f"acc{cc % 2}"
                self.cp("pool", r[:, 0:3], halo[:, cc, :], ["halo"], [rk])
                self.cp("act", r[:, 3:TT + 3], pt[:], [pk], [rk])
                self.cp("pool", halo[:, cc, :], r[:, TT:TT + 3], [rk], ["halo"])
                if stg < 2:
                    continue
                wc = lambda j: self.vecs[:, vb + V_CONV + cc * 4 + j: vb + V_CONV + cc * 4 + j + 1]
                self.ts("dve", a[:], r[:, 3:TT + 3], wc(3), None, ALU.mult, None, [rk, "vecs"], [ak])
                for j in (2, 1, 0):
                    self.stt("dve", a[:], r[:, j:j + TT], wc(j), a[:], ALU.mult, ALU.add, [rk, ak, "vecs"], [ak])
                kind, hh = cc // 4, cc % 4
                if stg < 3:
                    continue
                if kind == 2:
                    self.act(gvT[:, hh, :], a[:], AF.Silu, [ak], ["gvT"])
                else:
                    s_ = sl[cc % 2]; sk = f"sl{cc % 2}"
                    q_ = sqk[cc % 2]; qk_ = f"sqk{cc % 2}"
                    r_ = rn[cc % 2]; rnk = f"rn{cc % 2}"
                    self.act(s_[:], a[:], AF.Silu, [ak], [sk])
                    self.act(q_[:], s_[:], AF.Square, [sk], [qk_])
                    self.mm(psm[:, 256:512], self.onesb[:], q_[:], True, True, [qk_, "onesb"], ["psm"])
                    self.act(r_[:], psm[:, 256:512], AF.Ln, ["psm"], [rnk], bias=float(EPS))
                    self.act(r_[:], r_[:], AF.Exp, [rnk], [rnk], scale=-0.5)
                    dst, dk_ = (gqT, "gqT") if kind == 0 else (gkT, "gkT")
                    if kind == 0:
                        self.stt("dve", dst[:, hh, :], s_[:], float(128.0 ** -0.5), r_[:], ALU.mult, ALU.mult, [sk, rnk], [dk_])
                    else:
                        self.tt("dve", dst[:, hh, :], s_[:], r_[:], ALU.mult, [sk, rnk], [dk_])
            if self.cut(2):
                return
            for hh in range(4):
                pt, pk = proj(GZ + hh * 128)
                z = ztmp[hh % 2]; zk = f"ztmp{hh % 2}"
                self.act(z[:], pt[:], AF.Silu, [pk], [zk])
                self.ts("dve", gzT[:, hh, :], z[:], self.vecs[:, vb + V_GDNN:vb + V_GDNN + 1], None, ALU.mult, None,
                        [zk, "vecs"], ["gzT"])
            for hh in range(4):
                pt, pk = proj(MQ + hh * 64, 64)
                self.act(mqT[:, hh, :], pt[0:64, :], AF.Copy, [pk], ["mqT"], scale=0.125)
                pt, pk = proj(MK + hh * 64, 64)
                self.cp("act", mkT[:, hh, :], pt[0:64, :], [pk], ["mkT"])
            for hh in range(4):
                pt, pk = proj(MV + hh * 128)
                self.cp("act", mvT[:, hh, :], pt[:], [pk], ["mvT"])
                pt, pk = proj(MO + hh * 128)
                z = ztmp[hh % 2]; zk = f"ztmp{hh % 2}"
                self.act(z[:], pt[:], AF.Sigmoid, [pk], [zk])
                self.ts("dve", moT[:, hh, :], z[:], self.vecs[:, vb + V_MLN + hh:vb + V_MLN + hh + 1], None, ALU.mult, None,
                        [zk, "vecs"], ["moT"])
            if self.cut(3):
                return
            for c in range(4):
                cs = slice(c * 64, (c + 1) * 64)
                for (srcT, sk, dst, dk_) in ((gvT, "gvT", gv_tok, "gv_tok"), (gkT, "gkT", gk_tok, "gk_tok"), (mvT, "mvT", mv_tok, "mv_tok")):
                    p_ = ptr[c % 2]; pk = "ptr"
                    for hh in range(4):
                        self.tr(p_[0:64, hh * 128:(hh + 1) * 128], srcT[:, hh, cs], self.identb[:], [sk, "identb"], [pk])
                    self.cp("act", dst[:, c, :, :], p_[0:64, :].rearrange("p (h e) -> p h e", h=4), [pk], [dk_])
                p_ = ptr[c % 2]; pk = "ptr"
                for hh in range(4):
                    self.tr(p_[0:64, hh * 64:(hh + 1) * 64], mkT[:, hh, cs], idb64, ["mkT", "identb"], [pk])
                self.cp("act", mk_tok[:, c, :, :], p_[0:64, 0:256].rearrange("p (h e) -> p h e", h=4), [pk], ["mk_tok"])

            if self.cut(4):
                return
            for c in range(4):
                cs = slice(c * 64, (c + 1) * 64)
                for kc in range(KD):
                    self.mm(psm[0:64, 0:8], hT[:, kc, cs], W[:, kc, GB:GB + 8], kc == 0, kc == KD - 1, ["hT", "W1a"], ["psm"])
                for kc in range(KD):
                    self.mm(psm[0:64, 8:16], hT[:, kc, cs], W[:, kc, MI:MI + 8], kc == 0, kc == KD - 1, ["hT", "W1a"], ["psm"])
                self.tt("dve", T["g_at"][:], psm[0:64, 4:8], self.vecs[0:64, vb + V_DTB:vb + V_DTB + 4], ALU.add, ["psm", "vecs"], ["g_at"])
                self.act(T["beta"][:], psm[0:64, 0:4], AF.Sigmoid, ["psm"], ["beta"])
                self.act(T["g_ea"][:], T["g_at"][:], AF.Exp, ["g_at"], ["g_ea"])
                self.act(T["g_sp"][:], T["g_ea"][:], AF.Ln, ["g_ea"], ["g_sp"], bias=1.0)
                self.tt("dve", T["g_g"][:], T["g_sp"][:], self.nega[0:64, l * 4:l * 4 + 4], ALU.mult, ["g_sp", "nega"], ["g_g"])
                self.tt("dve", T["g_ng"][:], T["g_sp"][:], self.posa[0:64, l * 4:l * 4 + 4], ALU.mult, ["g_sp", "posa"], ["g_ng"])
                self.tt("dve", T["Gdn"][:], bc_h(Uf), bc_last(T["g_ng"][:], 64), ALU.mult, ["g_ng", "cf"], ["Gdn"])
                self.cp("pool", T["gbc"][:], bc_last(T["g_g"][:], 64), ["g_g"], ["gbc"])
                self.mm(pA[0:64, 0:256], Uf, T["gbc"][:].rearrange("p h s -> p (h s)"), True, False, ["gbc", "cf"], ["pA"])
                self.mm(pA[0:64, 0:256], ones64, T["Gdn"][:].rearrange("p h s -> p (h s)"), False, True, ["Gdn", "cf"], ["pA"])
                self.mm(psm[0:64, 16:20], Uf, T["g_g"][:], True, True, ["g_g", "cf"], ["psm"])
                self.mm(psm[0:64, 20:24], SLf, T["g_g"][:], True, True, ["g_g", "cf"], ["psm"])
                self.mm(psm[:, 24:28], ones64x128, T["g_g"][:], True, True, ["g_g", "cf"], ["psm"])
                self.act(T["e8"][:], psm[0:64, 16:24], AF.Exp, ["psm"], ["e8"])
                self.act(T["egl"][:], psm[:, 24:28], AF.Exp, ["psm"], ["egl"])
                e_gc = T["e8"][:, 0:4]; e_rc = T["e8"][:, 4:8]
                self.tt("dve", T["bg"][:], T["beta"][:], e_gc, ALU.mult, ["beta", "e8"], ["bg"])
                self.tt("dve", T["Ef"][:], pA[0:64, 0:256].rearrange("p (h s) -> p h s", h=4), bc_h(cf[0:64, C_MLN:C_MLN + 64]), ALU.add,
                        ["pA", "cf"], ["Ef"])
                self.act(T["Ef"][:], T["Ef"][:], AF.Exp, ["Ef"], ["Ef"])
                for hh in range(4):
                    self.mm(pB[0:64, hh * 64:(hh + 1) * 64], gkT[:, hh, cs], gkT[:, hh, cs], True, True, ["gkT"], ["pB"])
                for hh in range(4):
                    self.mm(pB[0:64, 256 + hh * 64:256 + (hh + 1) * 64], gqT[:, hh, cs], gkT[:, hh, cs], True, True, ["gqT", "gkT"], ["pB"])
                self.tt("dve", T["T1"][:], pB[0:64, 0:256].rearrange("p (h s) -> p h s", h=4), T["Ef"][:], ALU.mult, ["pB", "Ef"], ["T1"])
                self.tt("dve", T["T2"][:], T["T1"][:], bc_last(T["beta"][:], 64), ALU.mult, ["T1", "beta"], ["T2"])
                M0 = T["NM0"][:, 256:512]
                self.tt("dve", M0.rearrange("p (h s) -> p h s", h=4), T["T2"][:], bc_h(cf[0:64, C_MS01:C_MS01 + 64]), ALU.mult,
                        ["T2", "cf"], ["NM0m"])
                self.tt("dve", T["attn"][:].rearrange("p (h s) -> p h s", h=4), pB[0:64, 256:512].rearrange("p (h s) -> p h s", h=4),
                        T["Ef"][:], ALU.mult, ["pB", "Ef"], ["attn"])
                p_ = ptr[0]; pk = "ptr"
                for hh in range(4):
                    self.tr(p_[0:64, hh * 64:(hh + 1) * 64], M0[:, hh * 64:(hh + 1) * 64], idb64, ["NM0m", "identb"], [pk])
                for hh in range(4):
                    self.tr(p_[0:64, 256 + hh * 64:256 + (hh + 1) * 64], T["attn"][:, hh * 64:(hh + 1) * 64], idb64, ["attn", "identb"], [pk])
                self.cp("act", T["NM0"][:, 0:256], p_[0:64, 0:256], [pk], ["NM0n"])
                self.cp("act", T["NAT"][:, 0:256], p_[0:64, 256:512], [pk], ["NAT"])
                attnT = T["NAT"][:, 0:256]
                self.tt("pool", T["IpN0"][:], bc_h(idb64), T["NM0"][:, 0:256].rearrange("p (h s) -> p h s", h=4), ALU.subtract,
                        ["NM0n", "identb"], ["IpN0"])
                for j in range(1, 6):
                    prv = T[f"NM{j - 1}"]; cur = T[f"NM{j}"]
                    pn, pm = f"NM{j - 1}n", f"NM{j - 1}m"
                    for hh in range(4):
                        hs = slice(hh * 64, (hh + 1) * 64)
                        hs2 = slice(256 + hh * 64, 256 + (hh + 1) * 64)
                        self.mm(pA[0:64, hs], prv[:, hs2], prv[:, hs], True, True, [pn, pm], ["pA"])
                        if j < 5:
                            self.mm(pA[0:64, hs2], prv[:, hs], prv[:, hs2], True, True, [pn, pm], ["pA"])
                    if j < 5:
                        self.cp("act" if j % 2 else "dve", cur[:, :], pA[0:64, :], ["pA"], [f"NM{j}n", f"NM{j}m"])
                    else:
                        self.cp("act", cur[:, 0:256], pA[0:64, 0:256], ["pA"], [f"NM{j}n"])
                    self.tt("pool", T[f"IpN{j}"][:], bc_h(idb64), cur[:, 0:256].rearrange("p (h s) -> p h s", h=4), ALU.add,
                            [f"NM{j}n", "identb"], [f"IpN{j}"])
                self.tt("dve", T["X0"][:, :, 0:128], gv_tok[:, c, :, :], bc_last(T["beta"][:], 128), ALU.mult, ["gv_tok", "beta"], ["X0"])
                self.tt("dve", T["X0"][:, :, 128:256], gk_tok[:, c, :, :], bc_last(T["bg"][:], 128), ALU.mult, ["gk_tok", "bg"], ["X0"])
                for j in range(5):
                    Xs = T[f"X{j % 2}"]; Xd = T[f"X{(j + 1) % 2}"]
                    for hh in range(4):
                        self.mm(pCD[0:64, hh * 256:(hh + 1) * 256], T[f"IpN{j}"][:, hh, :], Xs[:, hh, :], True, True,
                                [f"IpN{j}", f"X{j % 2}"], ["pCD"])
                    self.cp("act" if j % 2 else "dve", Xd[:].rearrange("p h n -> p (h n)"), pCD[0:64, :], ["pCD"], [f"X{(j + 1) % 2}"])
                X5 = T["X1"]
                for hh in range(4):
                    self.mm(pCD[0:64, hh * 128:(hh + 1) * 128], T["IpN5"][:, hh, :], X5[:, hh, 0:128], True, True, ["IpN5", "X1"], ["pCD"])
                for hh in range(4):
                    self.mm(pCD[:, 512 + hh * 64:512 + (hh + 1) * 64], X5[:, hh, 128:256], T["IpN5"][:, hh, :], True, True, ["IpN5", "X1"], ["pCD"])
                self.cp("dve", T["u"][:], pCD[0:64, 0:512], ["pCD"], ["u"])
                self.cp("act", T["wT"][:], pCD[:, 512:768], ["pCD"], ["wT"])
                for hh in range(4):
                    self.mm(pA[0:64, hh * 128:(hh + 1) * 128], T["wT"][:, hh * 64:(hh + 1) * 64], Sgb[:, hh, :], True, True, ["wT", "Sgb"], ["pA"])
                for hh in range(4):
                    self.mm(pB[0:64, hh * 128:(hh + 1) * 128], gqT[:, hh, cs], Sgb[:, hh, :], True, True, ["gqT", "Sgb"], ["pB"])
                self.tt("dve", T["vnew"][:], T["u"][:], pA[0:64, :], ALU.subtract, ["u", "pA"], ["vnew"])
                for hh in range(4):
                    self.mm(pCD[0:64, hh * 128:(hh + 1) * 128], attnT[:, hh * 64:(hh + 1) * 64], T["vnew"][:, hh * 128:(hh + 1) * 128], True, True,
                            ["NAT", "vnew"], ["pCD"])
                for hh in range(4):
                    self.act(T["qs"][:, hh, :], pB[0:64, hh * 128:(hh + 1) * 128], AF.Copy, ["pB", "e8"], ["qs"], scale=T["e8"][:, hh:hh + 1])
                self.tt("dve", T["o"][:], T["qs"][:], pCD[0:64, 0:512].rearrange("p (h e) -> p h e", h=4), ALU.add, ["qs", "pCD"], ["o"])
                self.tt("dve", T["kdec"][:], gk_tok[:, c, :, :], bc_last(e_rc, 128), ALU.mult, ["gk_tok", "e8"], ["kdec"])
                for hh in range(4):
                    self.mm(pCD[:, 512 + hh * 128:512 + (hh + 1) * 128], T["kdec"][:, hh, :], T["vnew"][:, hh * 128:(hh + 1) * 128], True, True,
                            ["kdec", "vnew"], ["pCD"])
                for hh in range(4):
                    self.stt("dve", Sg[:, hh, :], Sg[:, hh, :], T["egl"][:, hh:hh + 1], pCD[:, 512 + hh * 128:512 + (hh + 1) * 128],
                             ALU.mult, ALU.add, ["Sg", "egl", "pCD"], ["Sg"])
                self.cp("act", Sgb[:], Sg[:], ["Sg"], ["Sgb"])
                self.P.op("pool", lambda e: e.memset(T["ssq"][:], 0.0), [], ["ssq"])
                for hh in range(4):
                    self.act(T["junk"][:], T["o"][:, hh, :], AF.Square, ["o"], ["junk", "ssq"], accum=T["ssq"][:, hh:hh + 1])
                self.act(T["rs"][:], T["ssq"][:], AF.Ln, ["ssq"], ["rs"], bias=float(128 * EPS))
                self.act(T["rs"][:], T["rs"][:], AF.Exp, ["rs"], ["rs"], scale=-0.5)
                self.stt("dve", T["on"][:], T["o"][:], SQ128, bc_last(T["rs"][:], 128), ALU.mult, ALU.mult, ["o", "rs"], ["on"])
                p_ = ptr[1]; pk = "ptr"
                for hh in range(4):
                    self.tr(p_[:, hh * 64:(hh + 1) * 64], T["on"][:, hh, :], idb64, ["on", "identb"], [pk])
                self.tt("dve", oaT[:, :, cs], p_[:, 0:256].rearrange("p (h c) -> p h c", h=4), gzT[:, :, cs], ALU.mult, [pk, "gzT"], ["oaT"])

                if self.cut(5):
                    return
                self.tt("dve", T["m_ig"][:], psm[0:64, 8:12], self.vecs[0:64, vb + V_IB:vb + V_IB + 4], ALU.add, ["psm", "vecs"], ["m_ig"])
                self.tt("dve", T["m_fg"][:], psm[0:64, 12:16], self.vecs[0:64, vb + V_FB:vb + V_FB + 4], ALU.add, ["psm", "vecs"], ["m_fg"])
                self.act(T["m_ef"][:], T["m_fg"][:], AF.Exp, ["m_fg"], ["m_ef"], scale=-1.0)
                self.act(T["m_sp"][:], T["m_ef"][:], AF.Ln, ["m_ef"], ["m_sp"], bias=1.0)
                self.ts("dve", T["m_lf"][:], T["m_sp"][:], -1.0, None, ALU.mult, None, ["m_sp"], ["m_lf"])
                if int(os.environ.get("MK_MD", "99")) <= 1:
                    return
                self.tt("dve", T["Ld"][:], bc_h(Uf), bc_last(T["m_lf"][:], 64), ALU.mult, ["m_lf", "cf"], ["Ld"])
                self.cp("pool", T["nlb"][:], bc_last(T["m_sp"][:], 64), ["m_sp"], ["nlb"])
                self.cp("pool", T["ibc"][:], bc_last(T["m_ig"][:], 64), ["m_ig"], ["ibc"])
                if int(os.environ.get("MK_MD", "99")) <= 2:
                    return
                fl = lambda t_: t_[:].rearrange("p h s -> p (h s)")
                if os.environ.get("MK_E", "0") == "1":
                    self.mm(pA[0:64, 0:256], cf[0:64, C_ID:C_ID + 64], fl(T["ibc"]), True, False, ["ibc", "cf"], ["pA"])
                    self.mm(pA[0:64, 0:256], Uf, fl(T["nlb"]), False, False, ["nlb", "cf"], ["pA"])
                    self.mm(pA[0:64, 0:256], ones64, fl(T["Ld"]), False, True, ["Ld", "cf"], ["pA"])
                elif os.environ.get("MK_E", "0") == "2":
                    self.mm(pA[0:64, 0:256], ones64, fl(T["Ld"]), True, False, ["Ld", "cf"], ["pA"])
                    self.mm(pA[0:64, 0:256], Uf, fl(T["nlb"]), False, True, ["nlb", "cf"], ["pA"])
                else:
                    self.mm(pA[0:64, 0:256], ones64, fl(T["Ld"]), True, False, ["Ld", "cf"], ["pA"])
                    self.mm(pA[0:64, 0:256], Uf, fl(T["nlb"]), False, False, ["nlb", "cf"], ["pA"])
                    self.mm(pA[0:64, 0:256], cf[0:64, C_ID:C_ID + 64], fl(T["ibc"]), False, True, ["ibc", "cf"], ["pA"])
                if int(os.environ.get("MK_MD", "99")) <= 3:
                    return
                self.mm(psm[0:64, 32:36], Uf, T["m_lf"][:], True, True, ["m_lf", "cf"], ["psm"])
                self.mm(psm[0:64, 36:40], SLf, T["m_lf"][:], True, True, ["m_lf", "cf"], ["psm"])
                self.mm(psm[0:64, 40:44], ones64, T["m_lf"][:], True, True, ["m_lf", "cf"], ["psm"])
                if int(os.environ.get("MK_MD", "99")) <= 4:
                    return
                self.act(T["m_eb"][:], psm[0:64, 32:36], AF.Exp, ["psm"], ["m_eb"])
                self.act(T["m_ebl"][:], psm[0:64, 40:44], AF.Exp, ["psm"], ["m_ebl"])
                self.tt("dve", T["m_a"][:], psm[0:64, 36:40], T["m_ig"][:], ALU.add, ["psm", "m_ig"], ["m_a"])
                self.act(T["m_ea"][:], T["m_a"][:], AF.Exp, ["m_a"], ["m_ea"])
                if int(os.environ.get("MK_MD", "99")) <= 5:
                    return
                if os.environ.get("MK_X", "0") == "4":
                    self.act(T["dtm"][:], pA[0:64, 0:256].rearrange("p (h s) -> p h s", h=4), AF.Exp, ["pA"], ["dtm"])
                elif os.environ.get("MK_X", "0") == "1":
                    self.tt("dve", T["Ef"][:], pA[0:64, 0:256].rearrange("p (h s) -> p h s", h=4), bc_h(cf[0:64, C_MUN:C_MUN + 64]), ALU.add,
                            ["pA", "cf"], ["Ef"])
                else:
                    self.tt("dve", T["dtm"][:], pA[0:64, 0:256].rearrange("p (h s) -> p h s", h=4), bc_h(cf[0:64, C_MUN:C_MUN + 64]), ALU.add,
                            ["pA", "cf"], ["dtm"])
                if int(os.environ.get("MK_MD", "99")) <= 6:
                    return
                self.act(T["dtm"][:], T["dtm"][:], AF.Exp, ["dtm"], ["dtm"])
                if int(os.environ.get("MK_MC", "99")) <= 1:
                    return
                for hh in range(4):
                    self.mm(pB[0:64, hh * 64:(hh + 1) * 64], mkT[:, hh, cs], mqT[:, hh, cs], True, True, ["mkT", "mqT"], ["pB"])
                self.tt("dve", T["sT"][:].rearrange("p (h s) -> p h s", h=4), pB[0:64, 0:256].rearrange("p (h s) -> p h s", h=4), T["dtm"][:],
                        ALU.mult, ["pB", "dtm"], ["sT"])
                if int(os.environ.get("MK_MC", "99")) <= 2:
                    return
                for hh in range(4):
                    self.mm(pCD[0:64, hh * 128:(hh + 1) * 128], T["sT"][:, hh * 64:(hh + 1) * 64], mv_tok[:, c, hh, :], True, True, ["sT", "mv_tok"], ["pCD"])
                    self.mm(psm[0:64, 44 + hh:45 + hh], T["sT"][:, hh * 64:(hh + 1) * 64], self.onesb[0:64, 0:1], True, True, ["sT", "onesb"], ["psm"])
                    self.mm(pCD[0:64, 512 + hh * 128:512 + (hh + 1) * 128], mqT[:, hh, cs], Cmb[:, hh, :], True, True, ["mqT", "Cmb"], ["pCD"])
                    self.mm(psm[0:64, 48 + hh:49 + hh], mqT[:, hh, cs], nmb[:, hh:hh + 1], True, True, ["mqT", "nmb"], ["psm"])
                if int(os.environ.get("MK_MC", "99")) <= 3:
                    return
                for hh in range(4):
                    self.act(T["nume"][:, hh, :], pCD[0:64, 512 + hh * 128:512 + (hh + 1) * 128], AF.Copy, ["pCD", "m_eb"], ["nume"],
                             scale=T["m_eb"][:, hh:hh + 1])
                self.tt("dve", T["num"][:], T["nume"][:], pCD[0:64, 0:512].rearrange("p (h e) -> p h e", h=4), ALU.add, ["nume", "pCD"], ["num"])
                self.tt("dve", T["dene"][:], psm[0:64, 48:52], T["m_eb"][:], ALU.mult, ["psm", "m_eb"], ["dene"])
                self.tt("dve", T["den"][:], T["dene"][:], psm[0:64, 44:48], ALU.add, ["dene", "psm"], ["den"])
                self.stt("dve", T["dene"][:], T["den"][:], -1.0, T["den"][:], ALU.mult, ALU.max, ["den"], ["dene"])
                self.ts("dve", T["den"][:], T["dene"][:], 1.0, None, ALU.max, None, ["dene"], ["den"])
                self.P.op("dve", lambda e: e.reciprocal(out=T["rr"][:], in_=T["den"][:]), ["den"], ["rr"])
                self.P.op("pool", lambda e: e.memset(T["ssq"][:], 0.0), [], ["ssq"])
                for hh in range(4):
                    self.act(T["junk"][:], T["num"][:, hh, :], AF.Square, ["num"], ["junk", "ssq"], accum=T["ssq"][:, hh:hh + 1])
                self.tt("dve", T["sc"][:], T["rr"][:], T["rr"][:], ALU.mult, ["rr"], ["sc"])
                self.tt("dve", T["rs"][:], T["ssq"][:], T["sc"][:], ALU.mult, ["ssq", "sc"], ["rs"])
                self.ts("dve", T["rs"][:], T["rs"][:], 1.0 / 128.0, EPS, ALU.mult, ALU.add, ["rs"], ["rs"])
                self.act(T["rs"][:], T["rs"][:], AF.Ln, ["rs"], ["rs"])
                self.act(T["rs"][:], T["rs"][:], AF.Exp, ["rs"], ["rs"], scale=-0.5)
                self.tt("dve", T["sc"][:], T["rr"][:], T["rs"][:], ALU.mult, ["rr", "rs"], ["sc"])
                self.tt("dve", T["hn"][:], T["num"][:], bc_last(T["sc"][:], 128), ALU.mult, ["num", "sc"], ["hn"])
                p_ = ptr[1]; pk = "ptr"
                for hh in range(4):
                    self.tr(p_[:, 256 + hh * 64:256 + (hh + 1) * 64], T["hn"][:, hh, :], idb64, ["hn", "identb"], [pk])
                self.tt("dve", obT[:, :, cs], p_[:, 256:512].rearrange("p (h c) -> p h c", h=4), moT[:, :, cs], ALU.mult, [pk, "moT"], ["obT"])
                if int(os.environ.get("MK_MC", "99")) <= 4:
                    return
                self.tt("dve", T["kw"][:], mk_tok[:, c, :, :], bc_last(T["m_ea"][:], 64), ALU.mult, ["mk_tok", "m_ea"], ["kw"])
                for hh in range(4):
                    self.mm(pB[0:64, hh * 128:(hh + 1) * 128], T["kw"][:, hh, :], mv_tok[:, c, hh, :], True, True, ["kw", "mv_tok"], ["pB"])
                    self.mm(psm[0:64, 52 + hh:53 + hh], T["kw"][:, hh, :], self.onesb[0:64, 0:1], True, True, ["kw", "onesb"], ["psm"])
                if int(os.environ.get("MK_MC", "99")) <= 5:
                    return
                for hh in range(4):
                    self.stt("dve", Cm[:, hh, :], Cm[:, hh, :], T["m_ebl"][:, hh:hh + 1], pB[0:64, hh * 128:(hh + 1) * 128],
                             ALU.mult, ALU.add, ["Cm", "m_ebl", "pB"], ["Cm"])
                self.tt("dve", nm[:], nm[:], T["m_ebl"][:], ALU.mult, ["nm", "m_ebl"], ["nm"])
                self.tt("dve", nm[:], nm[:], psm[0:64, 52:56], ALU.add, ["nm", "psm"], ["nm"])
                self.cp("act", Cmb[:], Cm[:], ["Cm"], ["Cmb"])
                self.cp("act", nmb[:], nm[:], ["nm"], ["nmb"])
                if self.cut(6):
                    return
            self.dma("sp", self.oa_d[:, t0:t0 + TT].rearrange("(h p) t -> p h t", p=128), oaT[:], ["oaT"], [])
            self.dma("sp", self.ob_d[:, t0:t0 + TT].rearrange("(h p) t -> p h t", p=128), obT[:], ["obT"], [])

    def phase_1b(self, st, l, xsrc):
        sb, ps = self.sb, self.ps
        Wg = sb(st, "Wg", [128, KD, 2048], BF16)
        Wba = sb(st, "Wba", [128, 4, D], BF16)
        Wbb = sb(st, "Wbb", [128, 4, D], BF16)
        Wo = sb(st, "Wo", [128, KD, D], BF16)
        self.load_w(Wg, self.w_in[l], KD, 2048, "Wg", col0=GTA, split=2)
        self.load_w(Wba, self.w_ba[l], 4, D, "Wba", split=1)
        self.load_w(Wbb, self.w_bb[l], 4, D, "Wbb", split=1)
        self.load_w(Wo, self.w_out[l], KD, D, "Wo", split=1)
        xin = sb(st, "xin", [128, KD, TT], F32)
        sq = sb(st, "sq", [128, KD, TT], BF16)
        hT = sb(st, "hT", [128, KD, TT], BF16)
        rstd = sb(st, "rstd", [128, TT], F32)
        oaT = sb(st, "oaT", [128, 4, TT], BF16)
        obT = sb(st, "obT", [128, 4, TT], BF16)
        ga = [sb(st, f"ga{i}", [128, TT], F32) for i in range(2)]
        gb = [sb(st, f"gb{i}", [128, TT], F32) for i in range(2)]
        t1 = [sb(st, f"t1{i}", [128, TT], F32) for i in range(2)]
        t2 = [sb(st, f"t2{i}", [128, TT], F32) for i in range(2)]
        mixed = sb(st, "mixed", [128, KD, TT], BF16)
        ppb = [ps(st, f"ppb{i}", [128, 512], F32) for i in range(4)]
        pp = [ppb[i // 2][:, (i % 2) * 256:(i % 2 + 1) * 256] for i in range(8)]
        pss = ps(st, "pss", [128, 256], F32)
        for tt in range(self.NT):
            t0 = tt * TT
            self.dma("sp", xin[:], xsrc[:, t0:t0 + TT].rearrange("(k p) t -> p k t", p=128), [], ["xin"])
            self.dma("sp", oaT[:], self.oa_d[:, t0:t0 + TT].rearrange("(h p) t -> p h t", p=128), [], ["oaT"])
            self.dma("sp", obT[:], self.ob_d[:, t0:t0 + TT].rearrange("(h p) t -> p h t", p=128), [], ["obT"])
            self.rmsnorm(xin, sq, hT, rstd, pss[:], l * 24 + 0, ("xin", "sq", "hT", "rstd", "pss"))
            for oc in range(KD):
                i = oc % 2
                ocs = slice(oc * 128, (oc + 1) * 128)
                p0, p1, p2, p3 = pp[4 * i], pp[4 * i + 1], pp[4 * i + 2], pp[4 * i + 3]
                k0 = k1 = f"ppb{2 * i}"
                k2 = k3 = f"ppb{2 * i + 1}"
                for kc in range(KD):
                    self.mm(p0[:], Wg[:, kc, oc * 128:(oc + 1) * 128], hT[:, kc, :], kc == 0, kc == KD - 1, ["Wg", "hT"], [k0])
                for kc in range(KD):
                    self.mm(p1[:], Wg[:, kc, 1024 + oc * 128:1024 + (oc + 1) * 128], hT[:, kc, :], kc == 0, kc == KD - 1, ["Wg", "hT"], [k1])
                for hh in range(4):
                    self.mm(p2[:], Wba[:, hh, ocs], oaT[:, hh, :], hh == 0, hh == 3, ["Wba", "oaT"], [k2])
                for hh in range(4):
                    self.mm(p3[:], Wbb[:, hh, ocs], obT[:, hh, :], hh == 0, hh == 3, ["Wbb", "obT"], [k3])
                self.act(ga[i][:], p0[:], AF.Sigmoid, [k0], [f"ga{i}"])
                self.act(gb[i][:], p1[:], AF.Sigmoid, [k1], [f"gb{i}"])
                self.tt("dve", t1[i][:], p2[:], ga[i][:], ALU.mult, [k2, f"ga{i}"], [f"t1{i}"])
                self.tt("dve", t2[i][:], p3[:], gb[i][:], ALU.mult, [k3, f"gb{i}"], [f"t2{i}"])
                self.tt("pool", mixed[:, oc, :], t1[i][:], t2[i][:], ALU.add, [f"t1{i}", f"t2{i}"], ["mixed"])
            for oc in range(KD):
                i = oc % 2
                p0, k0 = pp[4 * i], f"ppb{2 * i}"
                for kc in range(KD):
                    self.mm(p0[:], Wo[:, kc, oc * 128:(oc + 1) * 128], mixed[:, kc, :], kc == 0, kc == KD - 1, ["Wo", "mixed"], [k0])
                self.tt("dve", xin[:, oc, :], xin[:, oc, :], p0[:], ALU.add, ["xin", k0], ["xin"])
            self.dma("sp", self.xa[:, t0:t0 + TT].rearrange("(k p) t -> p k t", p=128), xin[:], ["xin"], [])

    def phase_2(self, st, l, last):
        sb, ps = self.sb, self.ps
        W1 = sb(st, "W1", [128, KD, D_FF], BF16)
        W3 = sb(st, "W3", [128, KD, D_FF], BF16)
        W2 = sb(st, "W2", [128, NFF, D], BF16)
        Wpg = sb(st, "Wpg", [128, KD, D], BF16)
        Wpl = sb(st, "Wpl", [128, 2, D], BF16)
        self.load_w(W1, self.w1[l], KD, D_FF, "W1", split=2)
        self.load_w(W3, self.w3[l], KD, D_FF, "W3", split=2)
        self.load_w(W2, self.w2[l], NFF, D, "W2", split=1)
        self.load_w(Wpg, self.w_pg[l], KD, D, "Wpg", split=1)
        self.load_w(Wpl, self.w_ple[l], 2, D, "Wpl", split=1)
        xin = sb(st, "xin", [128, KD, TT], F32)
        sq = sb(st, "sq", [128, KD, TT], BF16)
        hT = sb(st, "hT", [128, KD, TT], BF16)
        rstd = sb(st, "rstd", [128, TT], F32)
        G = sb(st, "G", [128, NFF, TT], BF16)
        sa = [sb(st, f"sa{i}", [128, TT], F32) for i in range(2)]
        pTb = sb(st, "pTb", [128, 2, TT], BF16)
        gt = [sb(st, f"gt{i}", [128, TT], F32) for i in range(2)]
        tp = [sb(st, f"tp{i}", [128, TT], F32) for i in range(2)]
        ppb = [ps(st, f"ppb{i}", [128, 512], F32) for i in range(4)]
        pp = [ppb[i // 2][:, (i % 2) * 256:(i % 2 + 1) * 256] for i in range(8)]
        pss = ps(st, "pss", [128, 256], F32)
        gbase = self.L * 24
        for tt in range(self.NT):
            t0 = tt * TT
            self.dma("sp", xin[:], self.xa[:, t0:t0 + TT].rearrange("(k p) t -> p k t", p=128), [], ["xin"])
            self.dma("pool", pTb[:], self.pT[l, :, t0:t0 + TT].rearrange("(k p) t -> p k t", p=128), [], ["pTb"])
            self.rmsnorm(xin, sq, hT, rstd, pss[:], l * 24 + 8, ("xin", "sq", "hT", "rstd", "pss"))
            for j in range(NFF):
                i = j % 2
                pa, pb = pp[2 * i], pp[2 * i + 1]
                ka = kb = f"ppb{i}"
                for kc in range(KD):
                    self.mm(pa[:], W1[:, kc, j * 128:(j + 1) * 128], hT[:, kc, :], kc == 0, kc == KD - 1, ["W1", "hT"], [ka])
                for kc in range(KD):
                    self.mm(pb[:], W3[:, kc, j * 128:(j + 1) * 128], hT[:, kc, :], kc == 0, kc == KD - 1, ["W3", "hT"], [kb])
                self.act(sa[i][:], pa[:], AF.Silu, [ka], [f"sa{i}"])
                self.tt("dve", G[:, j, :], sa[i][:], pb[:], ALU.mult, [f"sa{i}", kb], ["G"])
            for oc in range(KD):
                i = oc % 2
                p0, k0 = pp[4 + 2 * i], f"ppb{2 + i}"
                for j in range(NFF):
                    self.mm(p0[:], W2[:, j, oc * 128:(oc + 1) * 128], G[:, j, :], j == 0, j == NFF - 1, ["W2", "G"], [k0])
                self.tt("dve", xin[:, oc, :], xin[:, oc, :], p0[:], ALU.add, ["xin", k0], ["xin"])
            self.rmsnorm(xin, sq, hT, rstd, pss[:], l * 24 + 16, ("xin", "sq", "hT", "rstd", "pss"))
            for oc in range(KD):
                i = oc % 2
                p0, k0 = pp[4 + 2 * i], f"ppb{2 + i}"
                p1, k1 = pp[5 + 2 * i], f"ppb{2 + i}"
                for kc in range(KD):
                    self.mm(p0[:], Wpg[:, kc, oc * 128:(oc + 1) * 128], hT[:, kc, :], kc == 0, kc == KD - 1, ["Wpg", "hT"], [k0])
                for k2 in range(2):
                    self.mm(p1[:], Wpl[:, k2, oc * 128:(oc + 1) * 128], pTb[:, k2, :], k2 == 0, k2 == 1, ["Wpl", "pTb"], [k1])
                self.act(gt[i][:], p0[:], AF.Sigmoid, [k0], [f"gt{i}"])
                self.tt("dve", tp[i][:], gt[i][:], p1[:], ALU.mult, [f"gt{i}", k1], [f"tp{i}"])
                self.tt("pool", xin[:, oc, :], xin[:, oc, :], tp[i][:], ALU.add, ["xin", f"tp{i}"], ["xin"])
            if not last:
                self.dma("sp", self.xb[:, t0:t0 + TT].rearrange("(k p) t -> p k t", p=128), xin[:], ["xin"], [])
            else:
                self.act(sq[:], xin[:], AF.Square, ["xin"], ["sq"])
                for kc in range(KD):
                    self.mm(pss[:], self.onesb[:], sq[:, kc, :], kc == 0, kc == KD - 1, ["sq", "onesb"], ["pss"])
                self.act(rstd[:], pss[:], AF.Ln, ["pss"], ["rstd"], bias=float(D * EPS))
                self.act(rstd[:], rstd[:], AF.Exp, ["rstd"], ["rstd"], scale=-0.5)
                for kc in range(KD):
                    self.stt("dve", xin[:, kc, :], xin[:, kc, :], self.g32[:, gbase + kc:gbase + kc + 1], rstd[:],
                             ALU.mult, ALU.mult, ["xin", "rstd", "g32"], ["xin"])
                self.dma("sp", self.outT[:, t0:t0 + TT].rearrange("(k p) t -> p k t", p=128), xin[:], ["xin"], [], final=True)


def pack_vec(inp, L):
    v = np.zeros((128, L, NV), np.float32)
    for l in range(L):
        v[:, l, V_GMIX:V_GMIX + 8] = inp["g_mix"][l].reshape(8, 128).T
        v[:, l, V_GFFN:V_GFFN + 8] = inp["g_ffn"][l].reshape(8, 128).T
        v[:, l, V_GPLE:V_GPLE + 8] = inp["g_ple"][l].reshape(8, 128).T
        v[:, l, V_CONV:V_CONV + 48] = inp["conv_w"][l].reshape(4, 12, 128).transpose(2, 1, 0).reshape(128, 48)
        v[:, l, V_GDNN] = inp["gdn_norm"][l]
        v[:, l, V_MLN:V_MLN + 4] = inp["ml_norm"][l].T
        v[:, l, V_ALOG:V_ALOG + 4] = inp["a_log"][l][None, :]
        v[:, l, V_DTB:V_DTB + 4] = inp["dt_bias"][l][None, :]
        v[:, l, V_IB:V_IB + 4] = inp["ml_i_bias"][l][None, :]
        v[:, l, V_FB:V_FB + 4] = inp["ml_f_bias"][l][None, :]
    return np.ascontiguousarray(v.reshape(128, L * NV))


_CACHE = {}


def get_nc(S, L):
    key = (S, L)
    if key not in _CACHE:
        _CACHE[key] = Builder(S, L).build()
    return _CACHE[key]


def make_in_maps(inp, S, L, batches):
    f = lambda a: np.ascontiguousarray(np.asarray(a, dtype=np.float32))
    shared = {
        "vec": pack_vec({k: np.asarray(v) for k, v in inp.items()}, L),
        "gfin": np.ascontiguousarray(np.asarray(inp["g_final"], np.float32).reshape(8, 128).T),
        "cst": make_consts(),
        "w_in": f(inp["w_in"]), "w_branch_a": f(inp["w_branch_a"]), "w_branch_b": f(inp["w_branch_b"]),
        "w_out": f(inp["w_out"]), "w1": f(inp["w1"]), "w3": f(inp["w3"]), "w2": f(inp["w2"]),
        "w_ple_gate": f(inp["w_ple_gate"]), "w_ple": f(inp["w_ple"]),
    }
    maps = []
    for b in batches:
        m = dict(shared)
        m["xT"] = np.ascontiguousarray(np.asarray(inp["x"][b], np.float32).T)
        m["pT"] = np.ascontiguousarray(np.asarray(inp["p"][:, b], np.float32).transpose(0, 2, 1))
        maps.append(m)
    return maps


def kernel(**inputs):
    x = np.asarray(inputs["x"])
    B, S, _ = x.shape
    L = int(np.asarray(inputs["w_in"]).shape[0])
    nc = get_nc(S, L)
    batches = [i % B for i in range(8)]
    maps = make_in_maps(inputs, S, L, batches)
    res = run_bass_kernel_spmd(nc, maps, core_ids=list(range(8)))
    out = np.empty((B, S, D), np.float32)
    for b in range(B):
        out[b] = res.results[b]["outT"].T
    return out
```

```python
import contextlib
import numpy as np
import concourse.bass as bass
import concourse.mybir as mybir
from concourse.bass_utils import run_bass_kernel_spmd

F32 = mybir.dt.float32
BF16 = mybir.dt.bfloat16
AF = mybir.ActivationFunctionType
ALU = mybir.AluOpType

ENGS = ("pe", "act", "dve", "pool", "sp")
EPOCH = 30000
NEPOCH = 8
NDMA_SEM = 8


class Op:
    __slots__ = ("eng", "fn", "deps", "is_dma", "milestone", "dma_slot", "dma_cnt", "needed",
                 "dma_prev", "final", "phase")

    def __init__(self, eng, fn, is_dma, phase):
        self.eng = eng
        self.fn = fn
        self.deps = []
        self.is_dma = is_dma
        self.milestone = None
        self.needed = False
        self.dma_slot = None
        self.dma_cnt = None
        self.dma_prev = None
        self.final = False
        self.phase = phase


class Prog:
    def __init__(self, nc, stack):
        self.nc = nc
        self.phase = 0
        self.pending = {e: [] for e in ENGS}
        self.last_writer = {}
        self.readers = {}
        self.dma_count = {e: 0 for e in ENGS}
        self.dma_chain = {e: [None] * NDMA_SEM for e in ENGS}
        self.ms_count = {e: 0 for e in ENGS}
        self.last_op = {e: None for e in ENGS}
        self.barrier = {e: [] for e in ENGS}
        self.known = {e: {} for e in ENGS}
        self.sems = {e: [stack.enter_context(nc.semaphore(f"s_{e}_{i}")) for i in range(NEPOCH)]
                     for e in ("pe", "act", "dve", "pool")}
        self.dsems = {e: [stack.enter_context(nc.semaphore(f"d_{e}_{i}")) for i in range(NDMA_SEM)]
                      for e in ("sp", "pool", "act")}
        self.n_ops = 0

    def op(self, eng, fn, reads=(), writes=(), dma=False):
        o = Op(eng, fn, dma, self.phase)
        deps = set()
        for b in reads:
            for w in self.last_writer.get(b, ()):
                deps.add(w)
        for b in writes:
            for w in self.last_writer.get(b, ()):
                deps.add(w)
            for r in self.readers.get(b, ()):
                deps.add(r)
        for d in deps:
            if d.phase < self.phase:
                continue
            if d.eng == "pe" and eng == "pe" and not d.is_dma and not dma:
                continue
            o.deps.append(d)
            d.needed = True
        if self.barrier[eng]:
            o.deps.extend(self.barrier[eng])
            self.barrier[eng] = []
        for b in reads:
            self.readers.setdefault(b, []).append(o)
        for b in writes:
            prev = self.last_writer.get(b, [])
            if dma and prev and all(w.is_dma for w in prev) and not self.readers.get(b):
                self.last_writer[b] = prev + [o]
            else:
                self.last_writer[b] = [o]
            self.readers[b] = []
        if dma:
            n = self.dma_count[eng]
            self.dma_count[eng] = n + 1
            slot = n % NDMA_SEM
            o.dma_slot = slot
            o.dma_cnt = n // NDMA_SEM + 1
            o.dma_prev = self.dma_chain[eng][slot]
            self.dma_chain[eng][slot] = o
            o.needed = True
        else:
            self.last_op[eng] = o
        self.pending[eng].append(o)
        self.n_ops += 1
        return o

    def _wait_for(self, e, engobj, d):
        known = self.known[e]
        if d.is_dma:
            key = ("d", d.eng, d.dma_slot)
            val = d.dma_cnt * 16
            sem = self.dsems[d.eng][d.dma_slot]
        else:
            ep = (d.milestone - 1) // EPOCH
            key = ("c", d.eng, ep)
            val = d.milestone - ep * EPOCH
            sem = self.sems[d.eng][ep]
        if known.get(key, 0) >= val:
            return
        engobj.wait_ge(sem, val)
        known[key] = val

    def emit(self, final=False):
        nc = self.nc
        tails = []
        for e in ENGS:
            o = self.last_op[e]
            if o is not None:
                o.needed = True
                tails.append(o)
            for d in self.dma_chain[e]:
                if d is not None:
                    tails.append(d)
        for e in ENGS:
            for o in self.pending[e]:
                if not o.is_dma and o.needed:
                    self.ms_count[e] += 1
                    o.milestone = self.ms_count[e]
            assert self.ms_count[e] < EPOCH * NEPOCH, (e, self.ms_count[e])
        finals = [o for e in ENGS for o in self.pending[e] if o.final]

        def run(e, engobj):
            for o in self.pending[e]:
                for d in o.deps:
                    self._wait_for(e, engobj, d)
                if o.is_dma and o.dma_prev is not None:
                    self._wait_for(e, engobj, o.dma_prev)
                ins = o.fn(engobj)
                if o.is_dma:
                    ins.then_inc(self.dsems[e][o.dma_slot], 16)
                elif o.needed:
                    ep = (o.milestone - 1) // EPOCH
                    ins.then_inc(self.sems[e][ep], 1)
            if e == "sp" and final:
                for o in finals:
                    self._wait_for(e, engobj, o)

        with nc.Block() as block:
            block.tensor(lambda eng: run("pe", eng))
            block.scalar(lambda eng: run("act", eng))
            block.vector(lambda eng: run("dve", eng))
            block.gpsimd(lambda eng: run("pool", eng))
            block.sync(lambda eng: run("sp", eng))
        self.pending = {e: [] for e in ENGS}
        self.phase += 1
        for e in ENGS:
            self.barrier[e] = list(tails)


D = 1024
KD = 8
TT = 256
CH = 64
IN_W = 5648
D_FF = 2816
NFF = 22
EPS = 1e-6
GQ, GK, GV, GZ, GB, GA = 0, 512, 1024, 1536, 2048, 2052
MQ, MK, MV, MO, MI, MF = 2056, 2312, 2568, 3080, 3592, 3596
GTA, GTB = 3600, 4624
W1A = 3600
V_GMIX, V_GFFN, V_GPLE, V_CONV, V_GDNN, V_MLN, V_ALOG, V_DTB, V_IB, V_FB, NV = 0, 8, 16, 24, 72, 73, 77, 81, 85, 89, 93
C_ID, C_U, C_SL, C_MLN, C_MS01, C_MUN, C_ONE, NCONST = 0, 128, 192, 256, 320, 384, 448, 576
NEG = -30000.0


def make_consts():
    c = np.zeros((128, NCONST), np.float32)
    c[:, C_ID:C_ID + 128] = np.eye(128, dtype=np.float32)
    i = np.arange(64)
    c[:64, C_U:C_U + 64] = (i[:, None] <= i[None, :])
    c[:64, C_SL:C_SL + 64] = (i[:, None] > i[None, :])
    c[:64, C_MLN:C_MLN + 64] = np.where(i[None, :] <= i[:, None], 0.0, NEG)
    c[:64, C_MS01:C_MS01 + 64] = (i[None, :] < i[:, None])
    c[:64, C_MUN:C_MUN + 64] = np.where(i[:, None] <= i[None, :], 0.0, NEG)
    c[:, C_ONE:C_ONE + 128] = 1.0
    return c


class Builder:
    def __init__(self, S, L, last_final=True, x_in_name="xT", dbg=False):
        self.S, self.L = S, L
        self.NT = S // TT
        nc = self.nc = bass.Bass("TRN2", target_bir_lowering=False)
        self.stack = contextlib.ExitStack()
        dr = lambda name, shape, dt=F32, kind="ExternalInput": nc.dram_tensor(name, shape, dt, kind=kind).ap()
        self.xT = dr("xT", [D, S])
        self.pT = dr("pT", [L, 256, S])
        self.vec = dr("vec", [128, L * NV])
        self.gfin = dr("gfin", [128, KD])
        self.cst = dr("cst", [128, NCONST])
        self.w_in = dr("w_in", [L, D, IN_W])
        self.w_ba = dr("w_branch_a", [L, 512, D])
        self.w_bb = dr("w_branch_b", [L, 512, D])
        self.w_out = dr("w_out", [L, D, D])
        self.w1 = dr("w1", [L, D, D_FF])
        self.w3 = dr("w3", [L, D, D_FF])
        self.w2 = dr("w2", [L, D_FF, D])
        self.w_pg = dr("w_ple_gate", [L, D, D])
        self.w_ple = dr("w_ple", [L, 256, D])
        self.outT = dr("outT", [D, S], kind="ExternalOutput")
        self.xa = nc.dram_tensor("xa_s", [D, S], F32).ap()
        self.xb = nc.dram_tensor("xb_s", [D, S], F32).ap()
        self.oa_d = nc.dram_tensor("oa_s", [512, S], BF16).ap()
        self.ob_d = nc.dram_tensor("ob_s", [512, S], BF16).ap()
        self.dbg = {}
        self.want_dbg = dbg

    BAD_LO, BAD_HI = 176000, 180288

    def sb(self, st, name, shape, dt):
        nm = f"{name}_{self.P.phase}"
        cm = self.nc.sbuf_tensor(nm, shape, dt)
        t = cm.__enter__()
        addr = int(self.nc.lookup_mloc(t).addr)
        nbytes = int(np.prod(shape[1:])) * (2 if dt == BF16 else 4)
        if addr < self.BAD_HI and addr + nbytes > self.BAD_LO:
            cm.__exit__(None, None, None)
            st.enter_context(self.nc.sbuf_tensor(f"hole_{nm}", [128, (self.BAD_HI - addr + 3) // 4], F32))
            return st.enter_context(self.nc.sbuf_tensor(nm + "_r", shape, dt))
        st.push(cm)
        return t

    def ps(self, st, name, shape, dt):
        return st.enter_context(self.nc.psum_tensor(f"{name}_{self.P.phase}", shape, dt))

    def mm(self, out, lhsT, rhs, start, stop, reads, writes):
        return self.P.op("pe", lambda e: e.matmul(out, lhsT=lhsT, rhs=rhs, start=start, stop=stop), reads, writes)

    def tr(self, out, in_, ident, reads, writes):
        return self.P.op("pe", lambda e: e.transpose(out=out, in_=in_, identity=ident), reads, writes)

    def act(self, out, in_, func, reads, writes, bias=None, scale=None, accum=None):
        kw = {}
        if bias is not None:
            kw["bias"] = bias
        if scale is not None:
            kw["scale"] = scale
        if accum is not None:
            kw["accum_out"] = accum
        return self.P.op("act", lambda e: e.activation(out=out, in_=in_, func=func, **kw), reads, writes)

    def tt(self, eng, out, in0, in1, op, reads, writes):
        return self.P.op(eng, lambda e: e.tensor_tensor(out=out, in0=in0, in1=in1, op=op), reads, writes)

    def ts(self, eng, out, in0, s1, s2, op0, op1, reads, writes):
        if op1 is None:
            return self.P.op(eng, lambda e: e.tensor_scalar(out=out, in0=in0, scalar1=s1, scalar2=None, op0=op0), reads, writes)
        return self.P.op(eng, lambda e: e.tensor_scalar(out=out, in0=in0, scalar1=s1, scalar2=s2, op0=op0, op1=op1), reads, writes)

    def stt(self, eng, out, in0, scalar, in1, op0, op1, reads, writes):
        return self.P.op(eng, lambda e: e.scalar_tensor_tensor(out=out, in0=in0, scalar=scalar, in1=in1, op0=op0, op1=op1), reads, writes)

    def cp(self, eng, out, in_, reads, writes):
        if eng == "act":
            return self.P.op("act", lambda e: e.copy(out=out, in_=in_), reads, writes)
        return self.P.op(eng, lambda e: e.tensor_copy(out=out, in_=in_), reads, writes)

    def dma(self, eng, out, in_, reads, writes, final=False):
        o = self.P.op(eng, lambda e: e.dma_start(out=out, in_=in_), reads, writes, dma=True)
        o.final = final
        return o

    def cut(self, k):
        import os
        return int(os.environ.get("MK_CUT", "99")) <= k

    def tap(self, name, src_ap, shape, reads):
        if not self.want_dbg or name in self.dbg:
            return
        t = self.nc.dram_tensor("dbg_" + name, list(shape), F32, kind="ExternalOutput").ap()
        self.dbg[name] = t
        self.dma("sp", t, src_ap, reads, [], final=True)

    def build(self):
        nc = self.nc
        with self.stack as top:
            self.P = P = Prog(nc, top)
            self.cf = self.sb(top, "cf", [128, NCONST], F32)
            self.identb = self.sb(top, "identb", [128, 128], BF16)
            self.onesb = self.sb(top, "onesb", [128, 128], BF16)
            self.vecs = self.sb(top, "vecs", [128, self.L * NV], F32)
            self.g32 = self.sb(top, "g32", [128, self.L * 24 + 8], F32)
            self.posa = self.sb(top, "posa", [128, self.L * 4], F32)
            self.nega = self.sb(top, "nega", [128, self.L * 4], F32)
            self.gfs = self.sb(top, "gfs", [128, KD], F32)
            self.dma("sp", self.cf[:], self.cst, [], ["cf"])
            self.dma("sp", self.vecs[:], self.vec, [], ["vecs"])
            self.dma("sp", self.gfs[:], self.gfin, [], ["gfs"])
            self.cp("dve", self.identb[:], self.cf[:, C_ID:C_ID + 128], ["cf"], ["identb"])
            self.cp("dve", self.onesb[:], self.cf[:, C_ONE:C_ONE + 128], ["cf"], ["onesb"])
            for l in range(self.L):
                self.ts("dve", self.g32[:, l * 24:(l + 1) * 24], self.vecs[:, l * NV:l * NV + 24], 32.0, None,
                        ALU.mult, None, ["vecs"], ["g32"])
                self.act(self.posa[:, l * 4:(l + 1) * 4], self.vecs[:, l * NV + V_ALOG:l * NV + V_ALOG + 4], AF.Exp,
                         ["vecs"], ["posa"])
            self.ts("dve", self.g32[:, self.L * 24:self.L * 24 + 8], self.gfs[:], 32.0, None, ALU.mult, None, ["gfs"], ["g32"])
            self.ts("dve", self.nega[:], self.posa[:], -1.0, None, ALU.mult, None, ["posa"], ["nega"])
            for l in range(self.L):
                src = self.xT if l == 0 else self.xb
                last = (l == self.L - 1)
                import os
                stop = int(os.environ.get("MK_STOP", "99"))
                with contextlib.ExitStack() as st:
                    if stop >= 1:
                        self.phase_1a(st, l, src)
                    P.emit()
                with contextlib.ExitStack() as st:
                    if stop >= 2:
                        self.phase_1b(st, l, src)
                    P.emit()
                with contextlib.ExitStack() as st:
                    if stop >= 3:
                        self.phase_2(st, l, last)
                    P.emit(final=last)
        return nc

    def load_w(self, dst, src2d, kc_n, ncols, key, col0=0, split=2):
        step = (ncols + split - 1) // split
        for kc in range(kc_n):
            for c0 in range(0, ncols, step):
                c1 = min(ncols, c0 + step)
                self.dma("pool", dst[:, kc, c0:c1], src2d[kc * 128:(kc + 1) * 128, col0 + c0:col0 + c1], [], [key])

    def rmsnorm(self, xin, sq, hT, rstd, pss, gcol, keys):
        kx, ksq, kh, krs, kps = keys
        self.act(sq[:], xin[:], AF.Square, [kx], [ksq])
        for kc in range(KD):
            self.mm(pss, self.onesb[:], sq[:, kc, :], kc == 0, kc == KD - 1, [ksq, "onesb"], [kps])
        self.act(rstd[:], pss, AF.Ln, [kps], [krs], bias=float(D * EPS))
        self.act(rstd[:], rstd[:], AF.Exp, [krs], [krs], scale=-0.5)
        for kc in range(KD):
            self.stt("dve", hT[:, kc, :], xin[:, kc, :], self.g32[:, gcol + kc:gcol + kc + 1], rstd[:],
                     ALU.mult, ALU.mult, [kx, krs, "g32"], [kh])

    def phase_1a(self, st, l, xsrc):
        P, S = self.P, self.S
        sb, ps = self.sb, self.ps
        vb = l * NV
        W = sb(st, "W1a", [128, KD, W1A], BF16)
        self.load_w(W, self.w_in[l], KD, W1A, "W1a", split=3)
        if self.cut(0):
            return
        xin = sb(st, "xin", [128, KD, TT], F32)
        sq = sb(st, "sq", [128, KD, TT], BF16)
        hT = sb(st, "hT", [128, KD, TT], BF16)
        rstd = sb(st, "rstd", [128, TT], F32)
        raw = [sb(st, f"raw{i}", [128, TT + 3], F32) for i in range(2)]
        halo = sb(st, "halo", [128, 12, 3], F32)
        acc = [sb(st, f"acc{i}", [128, TT], F32) for i in range(2)]
        sl = [sb(st, f"sl{i}", [128, TT], F32) for i in range(2)]
        sqk = [sb(st, f"sqk{i}", [128, TT], BF16) for i in range(2)]
        rn = [sb(st, f"rn{i}", [128, TT], F32) for i in range(2)]
        gqT = sb(st, "gqT", [128, 4, TT], BF16)
        gkT = sb(st, "gkT", [128, 4, TT], BF16)
        gvT = sb(st, "gvT", [128, 4, TT], BF16)
        gzT = sb(st, "gzT", [128, 4, TT], BF16)
        mqT = sb(st, "mqT", [64, 4, TT], BF16)
        mkT = sb(st, "mkT", [64, 4, TT], BF16)
        mvT = sb(st, "mvT", [128, 4, TT], BF16)
        moT = sb(st, "moT", [128, 4, TT], BF16)
        ztmp = [sb(st, f"ztmp{i}", [128, TT], F32) for i in range(2)]
        gv_tok = sb(st, "gv_tok", [64, 4, 4, 128], BF16)
        gk_tok = sb(st, "gk_tok", [64, 4, 4, 128], BF16)
        mv_tok = sb(st, "mv_tok", [64, 4, 4, 128], BF16)
        mk_tok = sb(st, "mk_tok", [64, 4, 4, 64], BF16)
        oaT = sb(st, "oaT", [128, 4, TT], BF16)
        obT = sb(st, "obT", [128, 4, TT], BF16)
        Sg = sb(st, "Sg", [128, 4, 128], F32)
        Sgb = sb(st, "Sgb", [128, 4, 128], BF16)
        Cm = sb(st, "Cm", [64, 4, 128], F32)
        Cmb = sb(st, "Cmb", [64, 4, 128], BF16)
        nm = sb(st, "nm", [64, 4], F32)
        nmb = sb(st, "nmb", [64, 4], BF16)
        T = {}
        def tmp(name, shape, dt):
            T[name] = sb(st, name, shape, dt)
            return T[name]
        tmp("g_at", [64, 4], F32); tmp("beta", [64, 4], F32); tmp("g_ea", [64, 4], F32); tmp("g_sp", [64, 4], F32)
        tmp("g_g", [64, 4], F32); tmp("g_ng", [64, 4], F32); tmp("Gdn", [64, 4, 64], F32); tmp("gbc", [64, 4, 64], F32)
        tmp("e8", [64, 8], F32); tmp("egl", [128, 4], F32); tmp("bg", [64, 4], F32)
        tmp("Ef", [64, 4, 64], F32); tmp("dtm", [64, 4, 64], F32); tmp("T1", [64, 4, 64], F32); tmp("T2", [64, 4, 64], F32)
        tmp("attn", [64, 256], BF16); tmp("NAT", [64, 512], BF16)
        for j in range(6):
            tmp(f"NM{j}", [64, 512], BF16)
            tmp(f"IpN{j}", [64, 4, 64], BF16)
        tmp("X0", [64, 4, 256], BF16); tmp("X1", [64, 4, 256], BF16)
        tmp("u", [64, 512], F32); tmp("wT", [128, 256], BF16); tmp("vnew", [64, 512], BF16)
        tmp("qs", [64, 4, 128], F32); tmp("o", [64, 4, 128], F32); tmp("kdec", [64, 4, 128], BF16)
        tmp("junk", [64, 128], F32); tmp("ssq", [64, 4], F32); tmp("rs", [64, 4], F32); tmp("on", [64, 4, 128], BF16)
        tmp("m_ig", [64, 4], F32); tmp("m_fg", [64, 4], F32); tmp("m_ef", [64, 4], F32); tmp("m_sp", [64, 4], F32)
        tmp("m_lf", [64, 4], F32); tmp("Ld", [64, 4, 64], F32); tmp("nlb", [64, 4, 64], F32); tmp("ibc", [64, 4, 64], F32)
        tmp("m_eb", [64, 4], F32); tmp("m_ebl", [64, 4], F32); tmp("m_a", [64, 4], F32); tmp("m_ea", [64, 4], F32)
        tmp("sT", [64, 256], BF16); tmp("nume", [64, 4, 128], F32); tmp("num", [64, 4, 128], F32)
        tmp("den", [64, 4], F32); tmp("dene", [64, 4], F32); tmp("rr", [64, 4], F32); tmp("sc", [64, 4], F32)
        tmp("hn", [64, 4, 128], BF16); tmp("kw", [64, 4, 64], BF16)
        ppb = [ps(st, f"ppb{i}", [128, 512], F32) for i in range(2)]
        pp = [ppb[i][:, 0:256] for i in range(2)]
        ptrb = ps(st, "ptrb", [128, 1024], BF16)
        ptr = [ptrb[:, i * 512:(i + 1) * 512] for i in range(2)]
        psm = ps(st, "psm", [128, 512], F32)
        pA = ps(st, "pA", [128, 512], F32)
        pB = ps(st, "pB", [128, 512], F32)
        pCD = ps(st, "pCD", [128, 1024], F32)
        mA, mB = ppb[0], ppb[1]
        cf = self.cf
        Uf = cf[0:64, C_U:C_U + 64]
        SLf = cf[0:64, C_SL:C_SL + 64]
        ones64 = cf[0:64, C_ONE:C_ONE + 64]
        ones64x128 = cf[0:64, C_ONE:C_ONE + 128]
        idb64 = self.identb[0:64, 0:64]
        SQ128 = float(np.sqrt(128.0))

        def bc_h(ap64):
            return ap64[:, None, :].broadcast_to([64, 4, 64])

        def bc_last(ap, n):
            return ap[:, :, None].broadcast_to([64, 4, n])

        self.P.op("pool", lambda e: e.memset(Sg[:], 0.0), [], ["Sg"])
        self.P.op("pool", lambda e: e.memset(Sgb[:], 0.0), [], ["Sgb"])
        self.P.op("pool", lambda e: e.memset(Cm[:], 0.0), [], ["Cm"])
        self.P.op("pool", lambda e: e.memset(Cmb[:], 0.0), [], ["Cmb"])
        self.P.op("pool", lambda e: e.memset(nm[:], 0.0), [], ["nm"])
        self.P.op("pool", lambda e: e.memset(nmb[:], 0.0), [], ["nmb"])
        self.P.op("pool", lambda e: e.memset(halo[:], 0.0), [], ["halo"])

        ppi = [0]

        def proj(col0, ncols=128):
            i = ppi[0] % 2
            ppi[0] += 1
            for kc in range(KD):
                self.mm(pp[i][0:ncols, :], W[:, kc, col0:col0 + ncols], hT[:, kc, :], kc == 0, kc == KD - 1,
                        ["W1a", "hT"], [f"pp{i}"])
            return pp[i], f"pp{i}"

        for tt in range(self.NT):
            t0 = tt * TT
            self.dma("sp", xin[:], xsrc[:, t0:t0 + TT].rearrange("(k p) t -> p k t", p=128), [], ["xin"])
            self.rmsnorm(xin, sq, hT, rstd, psm[:, 256:512], l * 24 + 0, ("xin", "sq", "hT", "rstd", "psm"))
            if self.cut(1):
                return
            import os
            ncc = int(os.environ.get("MK_NCC", "12")); stg = int(os.environ.get("MK_ST", "9"))
            for cc in range(ncc):
                pt, pk = proj(cc * 128)
                r = raw[cc % 2]; rk = f"raw{cc % 2}"
                a = acc[cc % 2]; ak = f"acc{cc % 2}"
                self.cp("pool", r[:, 0:3], halo[:, cc, :], ["halo"], [rk])
                self.cp("act", r[:, 3:TT + 3], pt[:], [pk], [rk])
                self.cp("pool", halo[:, cc, :], r[:, TT:TT + 3], [rk], ["halo"])
                if stg < 2:
                    continue
                wc = lambda j: self.vecs[:, vb + V_CONV + cc * 4 + j: vb + V_CONV + cc * 4 + j + 1]
                self.ts("dve", a[:], r[:, 3:TT + 3], wc(3), None, ALU.mult, None, [rk, "vecs"], [ak])
                for j in (2, 1, 0):
                    self.stt("dve", a[:], r[:, j:j + TT], wc(j), a[:], ALU.mult, ALU.add, [rk, ak, "vecs"], [ak])
                kind, hh = cc // 4, cc % 4
                if stg < 3:
                    continue
                if kind == 2:
                    self.act(gvT[:, hh, :], a[:], AF.Silu, [ak], ["gvT"])
                else:
                    s_ = sl[cc % 2]; sk = f"sl{cc % 2}"
                    q_ = sqk[cc % 2]; qk_ = f"sqk{cc % 2}"
                    r_ = rn[cc % 2]; rnk = f"rn{cc % 2}"
                    self.act(s_[:], a[:], AF.Silu, [ak], [sk])
                    self.act(q_[:], s_[:], AF.Square, [sk], [qk_])
                    self.mm(psm[:, 256:512], self.onesb[:], q_[:], True, True, [qk_, "onesb"], ["psm"])
                    self.act(r_[:], psm[:, 256:512], AF.Ln, ["psm"], [rnk], bias=float(EPS))
                    self.act(r_[:], r_[:], AF.Exp, [rnk], [rnk], scale=-0.5)
                    dst, dk_ = (gqT, "gqT") if kind == 0 else (gkT, "gkT")
                    if kind == 0:
                        self.stt("dve", dst[:, hh, :], s_[:], float(128.0 ** -0.5), r_[:], ALU.mult, ALU.mult, [sk, rnk], [dk_])
                    else:
                        self.tt("dve", dst[:, hh, :], s_[:], r_[:], ALU.mult, [sk, rnk], [dk_])
            if self.cut(2):
                return
            for hh in range(4):
                pt, pk = proj(GZ + hh * 128)
                z = ztmp[hh % 2]; zk = f"ztmp{hh % 2}"
                self.act(z[:], pt[:], AF.Silu, [pk], [zk])
                self.ts("dve", gzT[:, hh, :], z[:], self.vecs[:, vb + V_GDNN:vb + V_GDNN + 1], None, ALU.mult, None,
                        [zk, "vecs"], ["gzT"])
            for hh in range(4):
                pt, pk = proj(MQ + hh * 64, 64)
                self.act(mqT[:, hh, :], pt[0:64, :], AF.Copy, [pk], ["mqT"], scale=0.125)
                pt, pk = proj(MK + hh * 64, 64)
                self.cp("act", mkT[:, hh, :], pt[0:64, :], [pk], ["mkT"])
            for hh in range(4):
                pt, pk = proj(MV + hh * 128)
                self.cp("act", mvT[:, hh, :], pt[:], [pk], ["mvT"])
                pt, pk = proj(MO + hh * 128)
                z = ztmp[hh % 2]; zk = f"ztmp{hh % 2}"
                self.act(z[:], pt[:], AF.Sigmoid, [pk], [zk])
                self.ts("dve", moT[:, hh, :], z[:], self.vecs[:, vb + V_MLN + hh:vb + V_MLN + hh + 1], None, ALU.mult, None,
                        [zk, "vecs"], ["moT"])
            if self.cut(3):
                return
            for c in range(4):
                cs = slice(c * 64, (c + 1) * 64)
                for (srcT, sk, dst, dk_) in ((gvT, "gvT", gv_tok, "gv_tok"), (gkT, "gkT", gk_tok, "gk_tok"), (mvT, "mvT", mv_tok, "mv_tok")):
                    p_ = ptr[c % 2]; pk = "ptr"
                    for hh in range(4):
                        self.tr(p_[0:64, hh * 128:(hh + 1) * 128], srcT[:, hh, cs], self.identb[:], [sk, "identb"], [pk])
                    self.cp("act", dst[:, c, :, :], p_[0:64, :].rearrange("p (h e) -> p h e", h=4), [pk], [dk_])
                p_ = ptr[c % 2]; pk = "ptr"
                for hh in range(4):
                    self.tr(p_[0:64, hh * 64:(hh + 1) * 64], mkT[:, hh, cs], idb64, ["mkT", "identb"], [pk])
                self.cp("act", mk_tok[:, c, :, :], p_[0:64, 0:256].rearrange("p (h e) -> p h e", h=4), [pk], ["mk_tok"])

            if self.cut(4):
                return
            for c in range(4):
                cs = slice(c * 64, (c + 1) * 64)
                for kc in range(KD):
                    self.mm(psm[0:64, 0:8], hT[:, kc, cs], W[:, kc, GB:GB + 8], kc == 0, kc == KD - 1, ["hT", "W1a"], ["psm"])
                for kc in range(KD):
                    self.mm(psm[0:64, 8:16], hT[:, kc, cs], W[:, kc, MI:MI + 8], kc == 0, kc == KD - 1, ["hT", "W1a"], ["psm"])
                def gdn_gen():
                    self.tt("dve", T["g_at"][:], psm[0:64, 4:8], self.vecs[0:64, vb + V_DTB:vb + V_DTB + 4], ALU.add, ["psm", "vecs"], ["g_at"])
                    self.act(T["beta"][:], psm[0:64, 0:4], AF.Sigmoid, ["psm"], ["beta"])
                    self.act(T["g_ea"][:], T["g_at"][:], AF.Exp, ["g_at"], ["g_ea"])
                    self.act(T["g_sp"][:], T["g_ea"][:], AF.Ln, ["g_ea"], ["g_sp"], bias=1.0)
                    self.tt("dve", T["g_g"][:], T["g_sp"][:], self.nega[0:64, l * 4:l * 4 + 4], ALU.mult, ["g_sp", "nega"], ["g_g"])
                    self.tt("dve", T["g_ng"][:], T["g_sp"][:], self.posa[0:64, l * 4:l * 4 + 4], ALU.mult, ["g_sp", "posa"], ["g_ng"])
                    self.tt("dve", T["Gdn"][:], bc_h(Uf), bc_last(T["g_ng"][:], 64), ALU.mult, ["g_ng", "cf"], ["Gdn"])
                    self.cp("pool", T["gbc"][:], bc_last(T["g_g"][:], 64), ["g_g"], ["gbc"])
                    self.mm(pA[0:64, 0:256], Uf, T["gbc"][:].rearrange("p h s -> p (h s)"), True, False, ["gbc", "cf"], ["pA"])
                    self.mm(pA[0:64, 0:256], ones64, T["Gdn"][:].rearrange("p h s -> p (h s)"), False, True, ["Gdn", "cf"], ["pA"])
                    self.mm(psm[0:64, 16:20], Uf, T["g_g"][:], True, True, ["g_g", "cf"], ["psm"])
                    self.mm(psm[0:64, 20:24], SLf, T["g_g"][:], True, True, ["g_g", "cf"], ["psm"])
                    self.mm(psm[:, 24:28], ones64x128, T["g_g"][:], True, True, ["g_g", "cf"], ["psm"])
                    yield
                    self.act(T["e8"][:], psm[0:64, 16:24], AF.Exp, ["psm"], ["e8"])
                    self.act(T["egl"][:], psm[:, 24:28], AF.Exp, ["psm"], ["egl"])
                    e_gc = T["e8"][:, 0:4]; e_rc = T["e8"][:, 4:8]
                    self.tt("dve", T["bg"][:], T["beta"][:], e_gc, ALU.mult, ["beta", "e8"], ["bg"])
                    self.tt("dve", T["Ef"][:], pA[0:64, 0:256].rearrange("p (h s) -> p h s", h=4), bc_h(cf[0:64, C_MLN:C_MLN + 64]), ALU.add,
                            ["pA", "cf"], ["Ef"])
                    self.act(T["Ef"][:], T["Ef"][:], AF.Exp, ["Ef"], ["Ef"])
                    for hh in range(4):
                        self.mm(pB[0:64, hh * 64:(hh + 1) * 64], gkT[:, hh, cs], gkT[:, hh, cs], True, True, ["gkT"], ["pB"])
                    for hh in range(4):
                        self.mm(pB[0:64, 256 + hh * 64:256 + (hh + 1) * 64], gqT[:, hh, cs], gkT[:, hh, cs], True, True, ["gqT", "gkT"], ["pB"])
                    yield
                    self.tt("dve", T["T1"][:], pB[0:64, 0:256].rearrange("p (h s) -> p h s", h=4), T["Ef"][:], ALU.mult, ["pB", "Ef"], ["T1"])
                    self.tt("dve", T["T2"][:], T["T1"][:], bc_last(T["beta"][:], 64), ALU.mult, ["T1", "beta"], ["T2"])
                    M0 = T["NM0"][:, 256:512]
                    self.tt("dve", M0.rearrange("p (h s) -> p h s", h=4), T["T2"][:], bc_h(cf[0:64, C_MS01:C_MS01 + 64]), ALU.mult,
                            ["T2", "cf"], ["NM0m"])
                    self.tt("dve", T["attn"][:].rearrange("p (h s) -> p h s", h=4), pB[0:64, 256:512].rearrange("p (h s) -> p h s", h=4),
                            T["Ef"][:], ALU.mult, ["pB", "Ef"], ["attn"])
                    p_ = ptr[0]; pk = "ptr"
                    for hh in range(4):
                        self.tr(p_[0:64, hh * 64:(hh + 1) * 64], M0[:, hh * 64:(hh + 1) * 64], idb64, ["NM0m", "identb"], [pk])
                    for hh in range(4):
                        self.tr(p_[0:64, 256 + hh * 64:256 + (hh + 1) * 64], T["attn"][:, hh * 64:(hh + 1) * 64], idb64, ["attn", "identb"], [pk])
                    yield
                    self.cp("act", T["NM0"][:, 0:256], p_[0:64, 0:256], [pk], ["NM0n"])
                    self.cp("act", T["NAT"][:, 0:256], p_[0:64, 256:512], [pk], ["NAT"])
                    attnT = T["NAT"][:, 0:256]
                    self.tt("pool", T["IpN0"][:], bc_h(idb64), T["NM0"][:, 0:256].rearrange("p (h s) -> p h s", h=4), ALU.subtract,
                            ["NM0n", "identb"], ["IpN0"])
                    for j in range(1, 6):
                        prv = T[f"NM{j - 1}"]; cur = T[f"NM{j}"]
                        pn, pm = f"NM{j - 1}n", f"NM{j - 1}m"
                        for hh in range(4):
                            hs = slice(hh * 64, (hh + 1) * 64)
                            hs2 = slice(256 + hh * 64, 256 + (hh + 1) * 64)
                            self.mm(pA[0:64, hs], prv[:, hs2], prv[:, hs], True, True, [pn, pm], ["pA"])
                            if j < 5:
                                self.mm(pA[0:64, hs2], prv[:, hs], prv[:, hs2], True, True, [pn, pm], ["pA"])
                        if j < 5:
                            self.cp("act" if j % 2 else "dve", cur[:, :], pA[0:64, :], ["pA"], [f"NM{j}n", f"NM{j}m"])
                        else:
                            self.cp("act", cur[:, 0:256], pA[0:64, 0:256], ["pA"], [f"NM{j}n"])
                        self.tt("pool", T[f"IpN{j}"][:], bc_h(idb64), cur[:, 0:256].rearrange("p (h s) -> p h s", h=4), ALU.add,
                                [f"NM{j}n", "identb"], [f"IpN{j}"])
                    yield
                    self.tt("dve", T["X0"][:, :, 0:128], gv_tok[:, c, :, :], bc_last(T["beta"][:], 128), ALU.mult, ["gv_tok", "beta"], ["X0"])
                    self.tt("dve", T["X0"][:, :, 128:256], gk_tok[:, c, :, :], bc_last(T["bg"][:], 128), ALU.mult, ["gk_tok", "bg"], ["X0"])
                    for j in range(5):
                        Xs = T[f"X{j % 2}"]; Xd = T[f"X{(j + 1) % 2}"]
                        for hh in range(4):
                            self.mm(pCD[0:64, hh * 256:(hh + 1) * 256], T[f"IpN{j}"][:, hh, :], Xs[:, hh, :], True, True,
                                    [f"IpN{j}", f"X{j % 2}"], ["pCD"])
                        self.cp("act" if j % 2 else "dve", Xd[:].rearrange("p h n -> p (h n)"), pCD[0:64, :], ["pCD"], [f"X{(j + 1) % 2}"])
                    yield
                    X5 = T["X1"]
                    for hh in range(4):
                        self.mm(pCD[0:64, hh * 128:(hh + 1) * 128], T["IpN5"][:, hh, :], X5[:, hh, 0:128], True, True, ["IpN5", "X1"], ["pCD"])
                    for hh in range(4):
                        self.mm(pCD[:, 512 + hh * 64:512 + (hh + 1) * 64], X5[:, hh, 128:256], T["IpN5"][:, hh, :], True, True, ["IpN5", "X1"], ["pCD"])
                    yield
                    self.cp("dve", T["u"][:], pCD[0:64, 0:512], ["pCD"], ["u"])
                    self.cp("act", T["wT"][:], pCD[:, 512:768], ["pCD"], ["wT"])
                    for hh in range(4):
                        self.mm(pA[0:64, hh * 128:(hh + 1) * 128], T["wT"][:, hh * 64:(hh + 1) * 64], Sgb[:, hh, :], True, True, ["wT", "Sgb"], ["pA"])
                    for hh in range(4):
                        self.mm(pB[0:64, hh * 128:(hh + 1) * 128], gqT[:, hh, cs], Sgb[:, hh, :], True, True, ["gqT", "Sgb"], ["pB"])
                    yield
                    self.tt("dve", T["vnew"][:], T["u"][:], pA[0:64, :], ALU.subtract, ["u", "pA"], ["vnew"])
                    for hh in range(4):
                        self.mm(pCD[0:64, hh * 128:(hh + 1) * 128], attnT[:, hh * 64:(hh + 1) * 64], T["vnew"][:, hh * 128:(hh + 1) * 128], True, True,
                                ["NAT", "vnew"], ["pCD"])
                    yield
                    for hh in range(4):
                        self.act(T["qs"][:, hh, :], pB[0:64, hh * 128:(hh + 1) * 128], AF.Copy, ["pB", "e8"], ["qs"], scale=T["e8"][:, hh:hh + 1])
                    self.tt("dve", T["o"][:], T["qs"][:], pCD[0:64, 0:512].rearrange("p (h e) -> p h e", h=4), ALU.add, ["qs", "pCD"], ["o"])
                    self.tt("dve", T["kdec"][:], gk_tok[:, c, :, :], bc_last(e_rc, 128), ALU.mult, ["gk_tok", "e8"], ["kdec"])
                    for hh in range(4):
                        self.mm(pCD[:, 512 + hh * 128:512 + (hh + 1) * 128], T["kdec"][:, hh, :], T["vnew"][:, hh * 128:(hh + 1) * 128], True, True,
                                ["kdec", "vnew"], ["pCD"])
                    yield
                    for hh in range(4):
                        self.stt("dve", Sg[:, hh, :], Sg[:, hh, :], T["egl"][:, hh:hh + 1], pCD[:, 512 + hh * 128:512 + (hh + 1) * 128],
                                 ALU.mult, ALU.add, ["Sg", "egl", "pCD"], ["Sg"])
                    self.cp("act", Sgb[:], Sg[:], ["Sg"], ["Sgb"])
                    self.P.op("pool", lambda e: e.memset(T["ssq"][:], 0.0), [], ["ssq"])
                    for hh in range(4):
                        self.act(T["junk"][:], T["o"][:, hh, :], AF.Square, ["o"], ["junk", "ssq"], accum=T["ssq"][:, hh:hh + 1])
                    self.act(T["rs"][:], T["ssq"][:], AF.Ln, ["ssq"], ["rs"], bias=float(128 * EPS))
                    self.act(T["rs"][:], T["rs"][:], AF.Exp, ["rs"], ["rs"], scale=-0.5)
                    self.stt("dve", T["on"][:], T["o"][:], SQ128, bc_last(T["rs"][:], 128), ALU.mult, ALU.mult, ["o", "rs"], ["on"])
                    p_ = ptr[1]; pk = "ptr"
                    for hh in range(4):
                        self.tr(p_[:, hh * 64:(hh + 1) * 64], T["on"][:, hh, :], idb64, ["on", "identb"], [pk])
                    yield
                    self.tt("dve", oaT[:, :, cs], p_[:, 0:256].rearrange("p (h c) -> p h c", h=4), gzT[:, :, cs], ALU.mult, [pk, "gzT"], ["oaT"])
                    yield
                def mls_gen():
                    self.tt("dve", T["m_ig"][:], psm[0:64, 8:12], self.vecs[0:64, vb + V_IB:vb + V_IB + 4], ALU.add, ["psm", "vecs"], ["m_ig"])
                    self.tt("dve", T["m_fg"][:], psm[0:64, 12:16], self.vecs[0:64, vb + V_FB:vb + V_FB + 4], ALU.add, ["psm", "vecs"], ["m_fg"])
                    self.act(T["m_ef"][:], T["m_fg"][:], AF.Exp, ["m_fg"], ["m_ef"], scale=-1.0)
                    self.act(T["m_sp"][:], T["m_ef"][:], AF.Ln, ["m_ef"], ["m_sp"], bias=1.0)
                    self.ts("dve", T["m_lf"][:], T["m_sp"][:], -1.0, None, ALU.mult, None, ["m_sp"], ["m_lf"])
                    self.tt("dve", T["Ld"][:], bc_h(Uf), bc_last(T["m_lf"][:], 64), ALU.mult, ["m_lf", "cf"], ["Ld"])
                    self.cp("pool", T["nlb"][:], bc_last(T["m_sp"][:], 64), ["m_sp"], ["nlb"])
                    self.cp("pool", T["ibc"][:], bc_last(T["m_ig"][:], 64), ["m_ig"], ["ibc"])
                    fl = lambda t_: t_[:].rearrange("p h s -> p (h s)")
                    self.mm(mA[0:64, 0:256], ones64, fl(T["Ld"]), True, False, ["Ld", "cf"], ["pp0"])
                    self.mm(mA[0:64, 0:256], Uf, fl(T["nlb"]), False, False, ["nlb", "cf"], ["pp0"])
                    self.mm(mA[0:64, 0:256], cf[0:64, C_ID:C_ID + 64], fl(T["ibc"]), False, True, ["ibc", "cf"], ["pp0"])
                    for hh in range(4):
                        self.mm(mA[0:64, 256 + hh * 64:256 + (hh + 1) * 64], mkT[:, hh, cs], mqT[:, hh, cs], True, True, ["mkT", "mqT"], ["pp0"])
                    self.mm(psm[0:64, 32:36], Uf, T["m_lf"][:], True, True, ["m_lf", "cf"], ["psm"])
                    self.mm(psm[0:64, 36:40], SLf, T["m_lf"][:], True, True, ["m_lf", "cf"], ["psm"])
                    self.mm(psm[0:64, 40:44], ones64, T["m_lf"][:], True, True, ["m_lf", "cf"], ["psm"])
                    yield
                    self.act(T["m_eb"][:], psm[0:64, 32:36], AF.Exp, ["psm"], ["m_eb"])
                    self.act(T["m_ebl"][:], psm[0:64, 40:44], AF.Exp, ["psm"], ["m_ebl"])
                    self.tt("dve", T["m_a"][:], psm[0:64, 36:40], T["m_ig"][:], ALU.add, ["psm", "m_ig"], ["m_a"])
                    self.act(T["m_ea"][:], T["m_a"][:], AF.Exp, ["m_a"], ["m_ea"])
                    self.tt("dve", T["dtm"][:], mA[0:64, 0:256].rearrange("p (h s) -> p h s", h=4), bc_h(cf[0:64, C_MUN:C_MUN + 64]), ALU.add,
                            ["pp0", "cf"], ["dtm"])
                    self.act(T["dtm"][:], T["dtm"][:], AF.Exp, ["dtm"], ["dtm"])
                    self.tt("dve", T["sT"][:].rearrange("p (h s) -> p h s", h=4), mA[0:64, 256:512].rearrange("p (h s) -> p h s", h=4), T["dtm"][:],
                            ALU.mult, ["pp0", "dtm"], ["sT"])
                    for hh in range(4):
                        self.mm(mB[0:64, hh * 128:(hh + 1) * 128], mqT[:, hh, cs], Cmb[:, hh, :], True, True, ["mqT", "Cmb"], ["pp1"])
                    for hh in range(4):
                        self.mm(psm[0:64, 48 + hh:49 + hh], mqT[:, hh, cs], nmb[:, hh:hh + 1], True, True, ["mqT", "nmb"], ["psm"])
                    for hh in range(4):
                        self.mm(psm[0:64, 44 + hh:45 + hh], T["sT"][:, hh * 64:(hh + 1) * 64], self.onesb[0:64, 0:1], True, True, ["sT", "onesb"], ["psm"])
                    yield
                    for hh in range(4):
                        self.act(T["nume"][:, hh, :], mB[0:64, hh * 128:(hh + 1) * 128], AF.Copy, ["pp1", "m_eb"], ["nume"],
                                 scale=T["m_eb"][:, hh:hh + 1])
                    for hh in range(4):
                        self.mm(mB[0:64, hh * 128:(hh + 1) * 128], T["sT"][:, hh * 64:(hh + 1) * 64], mv_tok[:, c, hh, :], True, True, ["sT", "mv_tok"], ["pp1"])
                    yield
                    self.tt("dve", T["num"][:], T["nume"][:], mB[0:64, 0:512].rearrange("p (h e) -> p h e", h=4), ALU.add, ["nume", "pp1"], ["num"])
                    self.tt("dve", T["dene"][:], psm[0:64, 48:52], T["m_eb"][:], ALU.mult, ["psm", "m_eb"], ["dene"])
                    self.tt("dve", T["den"][:], T["dene"][:], psm[0:64, 44:48], ALU.add, ["dene", "psm"], ["den"])
                    self.stt("dve", T["dene"][:], T["den"][:], -1.0, T["den"][:], ALU.mult, ALU.max, ["den"], ["dene"])
                    self.ts("dve", T["den"][:], T["dene"][:], 1.0, None, ALU.max, None, ["dene"], ["den"])
                    self.P.op("dve", lambda e: e.reciprocal(out=T["rr"][:], in_=T["den"][:]), ["den"], ["rr"])
                    self.P.op("pool", lambda e: e.memset(T["ssq"][:], 0.0), [], ["ssq"])
                    for hh in range(4):
                        self.act(T["junk"][:], T["num"][:, hh, :], AF.Square, ["num"], ["junk", "ssq"], accum=T["ssq"][:, hh:hh + 1])
                    self.tt("dve", T["sc"][:], T["rr"][:], T["rr"][:], ALU.mult, ["rr"], ["sc"])
                    self.tt("dve", T["rs"][:], T["ssq"][:], T["sc"][:], ALU.mult, ["ssq", "sc"], ["rs"])
                    self.ts("dve", T["rs"][:], T["rs"][:], 1.0 / 128.0, EPS, ALU.mult, ALU.add, ["rs"], ["rs"])
                    self.act(T["rs"][:], T["rs"][:], AF.Ln, ["rs"], ["rs"])
                    self.act(T["rs"][:], T["rs"][:], AF.Exp, ["rs"], ["rs"], scale=-0.5)
                    self.tt("dve", T["sc"][:], T["rr"][:], T["rs"][:], ALU.mult, ["rr", "rs"], ["sc"])
                    self.tt("dve", T["hn"][:], T["num"][:], bc_last(T["sc"][:], 128), ALU.mult, ["num", "sc"], ["hn"])
                    p_ = ptr[1]; pk = "ptr"
                    for hh in range(4):
                        self.tr(p_[:, 256 + hh * 64:256 + (hh + 1) * 64], T["hn"][:, hh, :], idb64, ["hn", "identb"], [pk])
                    yield
                    self.tt("dve", obT[:, :, cs], p_[:, 256:512].rearrange("p (h c) -> p h c", h=4), moT[:, :, cs], ALU.mult, [pk, "moT"], ["obT"])
                    self.tt("dve", T["kw"][:], mk_tok[:, c, :, :], bc_last(T["m_ea"][:], 64), ALU.mult, ["mk_tok", "m_ea"], ["kw"])
                    for hh in range(4):
                        self.mm(mA[0:64, hh * 128:(hh + 1) * 128], T["kw"][:, hh, :], mv_tok[:, c, hh, :], True, True, ["kw", "mv_tok"], ["pp0"])
                        self.mm(psm[0:64, 52 + hh:53 + hh], T["kw"][:, hh, :], self.onesb[0:64, 0:1], True, True, ["kw", "onesb"], ["psm"])
                    yield
                    for hh in range(4):
                        self.stt("dve", Cm[:, hh, :], Cm[:, hh, :], T["m_ebl"][:, hh:hh + 1], mA[0:64, hh * 128:(hh + 1) * 128],
                                 ALU.mult, ALU.add, ["Cm", "m_ebl", "pp0"], ["Cm"])
                    self.tt("dve", nm[:], nm[:], T["m_ebl"][:], ALU.mult, ["nm", "m_ebl"], ["nm"])
                    self.tt("dve", nm[:], nm[:], psm[0:64, 52:56], ALU.add, ["nm", "psm"], ["nm"])
                    self.cp("act", Cmb[:], Cm[:], ["Cm"], ["Cmb"])
                    self.cp("act", nmb[:], nm[:], ["nm"], ["nmb"])
                    yield
                gens = [gdn_gen(), mls_gen()]
                while gens:
                    for g_ in list(gens):
                        try:
                            next(g_)
                        except StopIteration:
                            gens.remove(g_)
            self.dma("sp", self.oa_d[:, t0:t0 + TT].rearrange("(h p) t -> p h t", p=128), oaT[:], ["oaT"], [])
            self.dma("sp", self.ob_d[:, t0:t0 + TT].rearrange("(h p) t -> p h t", p=128), obT[:], ["obT"], [])

    def phase_1b(self, st, l, xsrc):
        sb, ps = self.sb, self.ps
        Wg = sb(st, "Wg", [128, KD, 2048], BF16)
        Wba = sb(st, "Wba", [128, 4, D], BF16)
        Wbb = sb(st, "Wbb", [128, 4, D], BF16)
        Wo = sb(st, "Wo", [128, KD, D], BF16)
        self.load_w(Wg, self.w_in[l], KD, 2048, "Wg", col0=GTA, split=2)
        self.load_w(Wba, self.w_ba[l], 4, D, "Wba", split=1)
        self.load_w(Wbb, self.w_bb[l], 4, D, "Wbb", split=1)
        self.load_w(Wo, self.w_out[l], KD, D, "Wo", split=1)
        xin = sb(st, "xin", [128, KD, TT], F32)
        sq = sb(st, "sq", [128, KD, TT], BF16)
        hT = sb(st, "hT", [128, KD, TT], BF16)
        rstd = sb(st, "rstd", [128, TT], F32)
        oaT = sb(st, "oaT", [128, 4, TT], BF16)
        obT = sb(st, "obT", [128, 4, TT], BF16)
        ga = [sb(st, f"ga{i}", [128, TT], F32) for i in range(2)]
        gb = [sb(st, f"gb{i}", [128, TT], F32) for i in range(2)]
        t1 = [sb(st, f"t1{i}", [128, TT], F32) for i in range(2)]
        t2 = [sb(st, f"t2{i}", [128, TT], F32) for i in range(2)]
        mixed = sb(st, "mixed", [128, KD, TT], BF16)
        ppb = [ps(st, f"ppb{i}", [128, 512], F32) for i in range(4)]
        pp = [ppb[i // 2][:, (i % 2) * 256:(i % 2 + 1) * 256] for i in range(8)]
        pss = ps(st, "pss", [128, 256], F32)
        for tt in range(self.NT):
            t0 = tt * TT
            self.dma("sp", xin[:], xsrc[:, t0:t0 + TT].rearrange("(k p) t -> p k t", p=128), [], ["xin"])
            self.dma("sp", oaT[:], self.oa_d[:, t0:t0 + TT].rearrange("(h p) t -> p h t", p=128), [], ["oaT"])
            self.dma("sp", obT[:], self.ob_d[:, t0:t0 + TT].rearrange("(h p) t -> p h t", p=128), [], ["obT"])
            self.rmsnorm(xin, sq, hT, rstd, pss[:], l * 24 + 0, ("xin", "sq", "hT", "rstd", "pss"))
            for oc in range(KD):
                i = oc % 2
                ocs = slice(oc * 128, (oc + 1) * 128)
                p0, p1, p2, p3 = pp[4 * i], pp[4 * i + 1], pp[4 * i + 2], pp[4 * i + 3]
                k0 = k1 = f"ppb{2 * i}"
                k2 = k3 = f"ppb{2 * i + 1}"
                for kc in range(KD):
                    self.mm(p0[:], Wg[:, kc, oc * 128:(oc + 1) * 128], hT[:, kc, :], kc == 0, kc == KD - 1, ["Wg", "hT"], [k0])
                for kc in range(KD):
                    self.mm(p1[:], Wg[:, kc, 1024 + oc * 128:1024 + (oc + 1) * 128], hT[:, kc, :], kc == 0, kc == KD - 1, ["Wg", "hT"], [k1])
                for hh in range(4):
                    self.mm(p2[:], Wba[:, hh, ocs], oaT[:, hh, :], hh == 0, hh == 3, ["Wba", "oaT"], [k2])
                for hh in range(4):
                    self.mm(p3[:], Wbb[:, hh, ocs], obT[:, hh, :], hh == 0, hh == 3, ["Wbb", "obT"], [k3])
                self.act(ga[i][:], p0[:], AF.Sigmoid, [k0], [f"ga{i}"])
                self.act(gb[i][:], p1[:], AF.Sigmoid, [k1], [f"gb{i}"])
                self.tt("dve", t1[i][:], p2[:], ga[i][:], ALU.mult, [k2, f"ga{i}"], [f"t1{i}"])
                self.tt("dve", t2[i][:], p3[:], gb[i][:], ALU.mult, [k3, f"gb{i}"], [f"t2{i}"])
                self.tt("pool", mixed[:, oc, :], t1[i][:], t2[i][:], ALU.add, [f"t1{i}", f"t2{i}"], ["mixed"])
            for oc in range(KD):
                i = oc % 2
                p0, k0 = pp[4 * i], f"ppb{2 * i}"
                for kc in range(KD):
                    self.mm(p0[:], Wo[:, kc, oc * 128:(oc + 1) * 128], mixed[:, kc, :], kc == 0, kc == KD - 1, ["Wo", "mixed"], [k0])
                self.tt("dve", xin[:, oc, :], xin[:, oc, :], p0[:], ALU.add, ["xin", k0], ["xin"])
            self.dma("sp", self.xa[:, t0:t0 + TT].rearrange("(k p) t -> p k t", p=128), xin[:], ["xin"], [])

    def phase_2(self, st, l, last):
        sb, ps = self.sb, self.ps
        W1 = sb(st, "W1", [128, KD, D_FF], BF16)
        W3 = sb(st, "W3", [128, KD, D_FF], BF16)
        W2 = sb(st, "W2", [128, NFF, D], BF16)
        Wpg = sb(st, "Wpg", [128, KD, D], BF16)
        Wpl = sb(st, "Wpl", [128, 2, D], BF16)
        self.load_w(W1, self.w1[l], KD, D_FF, "W1", split=2)
        self.load_w(W3, self.w3[l], KD, D_FF, "W3", split=2)
        self.load_w(W2, self.w2[l], NFF, D, "W2", split=1)
        self.load_w(Wpg, self.w_pg[l], KD, D, "Wpg", split=1)
        self.load_w(Wpl, self.w_ple[l], 2, D, "Wpl", split=1)
        xin = sb(st, "xin", [128, KD, TT], F32)
        sq = sb(st, "sq", [128, KD, TT], BF16)
        hT = sb(st, "hT", [128, KD, TT], BF16)
        rstd = sb(st, "rstd", [128, TT], F32)
        G = sb(st, "G", [128, NFF, TT], BF16)
        sa = [sb(st, f"sa{i}", [128, TT], F32) for i in range(2)]
        pTb = sb(st, "pTb", [128, 2, TT], BF16)
        gt = [sb(st, f"gt{i}", [128, TT], F32) for i in range(2)]
        tp = [sb(st, f"tp{i}", [128, TT], F32) for i in range(2)]
        ppb = [ps(st, f"ppb{i}", [128, 512], F32) for i in range(4)]
        pp = [ppb[i // 2][:, (i % 2) * 256:(i % 2 + 1) * 256] for i in range(8)]
        pss = ps(st, "pss", [128, 256], F32)
        gbase = self.L * 24
        for tt in range(self.NT):
            t0 = tt * TT
            self.dma("sp", xin[:], self.xa[:, t0:t0 + TT].rearrange("(k p) t -> p k t", p=128), [], ["xin"])
            self.dma("pool", pTb[:], self.pT[l, :, t0:t0 + TT].rearrange("(k p) t -> p k t", p=128), [], ["pTb"])
            self.rmsnorm(xin, sq, hT, rstd, pss[:], l * 24 + 8, ("xin", "sq", "hT", "rstd", "pss"))
            for j in range(NFF):
                i = j % 2
                pa, pb = pp[2 * i], pp[2 * i + 1]
                ka = kb = f"ppb{i}"
                for kc in range(KD):
                    self.mm(pa[:], W1[:, kc, j * 128:(j + 1) * 128], hT[:, kc, :], kc == 0, kc == KD - 1, ["W1", "hT"], [ka])
                for kc in range(KD):
                    self.mm(pb[:], W3[:, kc, j * 128:(j + 1) * 128], hT[:, kc, :], kc == 0, kc == KD - 1, ["W3", "hT"], [kb])
                self.act(sa[i][:], pa[:], AF.Silu, [ka], [f"sa{i}"])
                self.tt("dve", G[:, j, :], sa[i][:], pb[:], ALU.mult, [f"sa{i}", kb], ["G"])
            for oc in range(KD):
                i = oc % 2
                p0, k0 = pp[4 + 2 * i], f"ppb{2 + i}"
                for j in range(NFF):
                    self.mm(p0[:], W2[:, j, oc * 128:(oc + 1) * 128], G[:, j, :], j == 0, j == NFF - 1, ["W2", "G"], [k0])
                self.tt("dve", xin[:, oc, :], xin[:, oc, :], p0[:], ALU.add, ["xin", k0], ["xin"])
            self.rmsnorm(xin, sq, hT, rstd, pss[:], l * 24 + 16, ("xin", "sq", "hT", "rstd", "pss"))
            for oc in range(KD):
                i = oc % 2
                p0, k0 = pp[4 + 2 * i], f"ppb{2 + i}"
                p1, k1 = pp[5 + 2 * i], f"ppb{2 + i}"
                for kc in range(KD):
                    self.mm(p0[:], Wpg[:, kc, oc * 128:(oc + 1) * 128], hT[:, kc, :], kc == 0, kc == KD - 1, ["Wpg", "hT"], [k0])
                for k2 in range(2):
                    self.mm(p1[:], Wpl[:, k2, oc * 128:(oc + 1) * 128], pTb[:, k2, :], k2 == 0, k2 == 1, ["Wpl", "pTb"], [k1])
                self.act(gt[i][:], p0[:], AF.Sigmoid, [k0], [f"gt{i}"])
                self.tt("dve", tp[i][:], gt[i][:], p1[:], ALU.mult, [f"gt{i}", k1], [f"tp{i}"])
                self.tt("pool", xin[:, oc, :], xin[:, oc, :], tp[i][:], ALU.add, ["xin", f"tp{i}"], ["xin"])
            if not last:
                self.dma("sp", self.xb[:, t0:t0 + TT].rearrange("(k p) t -> p k t", p=128), xin[:], ["xin"], [])
            else:
                self.act(sq[:], xin[:], AF.Square, ["xin"], ["sq"])
                for kc in range(KD):
                    self.mm(pss[:], self.onesb[:], sq[:, kc, :], kc == 0, kc == KD - 1, ["sq", "onesb"], ["pss"])
                self.act(rstd[:], pss[:], AF.Ln, ["pss"], ["rstd"], bias=float(D * EPS))
                self.act(rstd[:], rstd[:], AF.Exp, ["rstd"], ["rstd"], scale=-0.5)
                for kc in range(KD):
                    self.stt("dve", xin[:, kc, :], xin[:, kc, :], self.g32[:, gbase + kc:gbase + kc + 1], rstd[:],
                             ALU.mult, ALU.mult, ["xin", "rstd", "g32"], ["xin"])
                self.dma("sp", self.outT[:, t0:t0 + TT].rearrange("(k p) t -> p k t", p=128), xin[:], ["xin"], [], final=True)


def pack_vec(inp, L):
    v = np.zeros((128, L, NV), np.float32)
    for l in range(L):
        v[:, l, V_GMIX:V_GMIX + 8] = inp["g_mix"][l].reshape(8, 128).T
        v[:, l, V_GFFN:V_GFFN + 8] = inp["g_ffn"][l].reshape(8, 128).T
        v[:, l, V_GPLE:V_GPLE + 8] = inp["g_ple"][l].reshape(8, 128).T
        v[:, l, V_CONV:V_CONV + 48] = inp["conv_w"][l].reshape(4, 12, 128).transpose(2, 1, 0).reshape(128, 48)
        v[:, l, V_GDNN] = inp["gdn_norm"][l]
        v[:, l, V_MLN:V_MLN + 4] = inp["ml_norm"][l].T
        v[:, l, V_ALOG:V_ALOG + 4] = inp["a_log"][l][None, :]
        v[:, l, V_DTB:V_DTB + 4] = inp["dt_bias"][l][None, :]
        v[:, l, V_IB:V_IB + 4] = inp["ml_i_bias"][l][None, :]
        v[:, l, V_FB:V_FB + 4] = inp["ml_f_bias"][l][None, :]
    return np.ascontiguousarray(v.reshape(128, L * NV))


_CACHE = {}


def get_nc(S, L):
    key = (S, L)
    if key not in _CACHE:
        _CACHE[key] = Builder(S, L).build()
    return _CACHE[key]


def make_in_maps(inp, S, L, batches):
    f = lambda a: np.ascontiguousarray(np.asarray(a, dtype=np.float32))
    shared = {
        "vec": pack_vec({k: np.asarray(v) for k, v in inp.items()}, L),
        "gfin": np.ascontiguousarray(np.asarray(inp["g_final"], np.float32).reshape(8, 128).T),
        "cst": make_consts(),
        "w_in": f(inp["w_in"]), "w_branch_a": f(inp["w_branch_a"]), "w_branch_b": f(inp["w_branch_b"]),
        "w_out": f(inp["w_out"]), "w1": f(inp["w1"]), "w3": f(inp["w3"]), "w2": f(inp["w2"]),
        "w_ple_gate": f(inp["w_ple_gate"]), "w_ple": f(inp["w_ple"]),
    }
    maps = []
    for b in batches:
        m = dict(shared)
        m["xT"] = np.ascontiguousarray(np.asarray(inp["x"][b], np.float32).T)
        m["pT"] = np.ascontiguousarray(np.asarray(inp["p"][:, b], np.float32).transpose(0, 2, 1))
        maps.append(m)
    return maps


def kernel(**inputs):
    x = np.asarray(inputs["x"])
    B, S, _ = x.shape
    L = int(np.asarray(inputs["w_in"]).shape[0])
    nc = get_nc(S, L)
    batches = [i % B for i in range(8)]
    maps = make_in_maps(inputs, S, L, batches)
    res = run_bass_kernel_spmd(nc, maps, core_ids=list(range(8)))
    out = np.empty((B, S, D), np.float32)
    for b in range(B):
        out[b] = res.results[b]["outT"].T
    return out
```

```python
import contextlib
import numpy as np
import concourse.bass as bass
import concourse.mybir as mybir
from concourse.bass_utils import run_bass_kernel_spmd

F32 = mybir.dt.float32
BF16 = mybir.dt.bfloat16
AF = mybir.ActivationFunctionType
ALU = mybir.AluOpType

ENGS = ("pe", "act", "dve", "pool", "sp")
EPOCH = 30000
NEPOCH = 8
NDMA_SEM = 8


class Op:
    __slots__ = ("eng", "fn", "deps", "is_dma", "milestone", "dma_slot", "dma_cnt", "needed",
                 "dma_prev", "final", "phase")

    def __init__(self, eng, fn, is_dma, phase):
        self.eng = eng
        self.fn = fn
        self.deps = []
        self.is_dma = is_dma
        self.milestone = None
        self.needed = False
        self.dma_slot = None
        self.dma_cnt = None
        self.dma_prev = None
        self.final = False
        self.phase = phase


class Prog:
    def __init__(self, nc, stack):
        self.nc = nc
        self.phase = 0
        self.pending = {e: [] for e in ENGS}
        self.last_writer = {}
        self.readers = {}
        self.dma_count = {e: 0 for e in ENGS}
        self.dma_chain = {e: [None] * NDMA_SEM for e in ENGS}
        self.ms_count = {e: 0 for e in ENGS}
        self.last_op = {e: None for e in ENGS}
        self.barrier = {e: [] for e in ENGS}
        self.known = {e: {} for e in ENGS}
        self.sems = {e: [stack.enter_context(nc.semaphore(f"s_{e}_{i}")) for i in range(NEPOCH)]
                     for e in ("pe", "act", "dve", "pool")}
        self.dsems = {e: [stack.enter_context(nc.semaphore(f"d_{e}_{i}")) for i in range(NDMA_SEM)]
                      for e in ("sp", "pool", "act")}
        self.n_ops = 0

    def op(self, eng, fn, reads=(), writes=(), dma=False):
        o = Op(eng, fn, dma, self.phase)
        deps = set()
        for b in reads:
            for w in self.last_writer.get(b, ()):
                deps.add(w)
        for b in writes:
            for w in self.last_writer.get(b, ()):
                deps.add(w)
            for r in self.readers.get(b, ()):
                deps.add(r)
        for d in deps:
            if d.phase < self.phase:
                continue
            if d.eng == "pe" and eng == "pe" and not d.is_dma and not dma:
                continue
            o.deps.append(d)
            d.needed = True
        if self.barrier[eng]:
            o.deps.extend(self.barrier[eng])
            self.barrier[eng] = []
        for b in reads:
            self.readers.setdefault(b, []).append(o)
        for b in writes:
            prev = self.last_writer.get(b, [])
            if dma and prev and all(w.is_dma for w in prev) and not self.readers.get(b):
                self.last_writer[b] = prev + [o]
            else:
                self.last_writer[b] = [o]
            self.readers[b] = []
        if dma:
            n = self.dma_count[eng]
            self.dma_count[eng] = n + 1
            slot = n % NDMA_SEM
            o.dma_slot = slot
            o.dma_cnt = n // NDMA_SEM + 1
            o.dma_prev = self.dma_chain[eng][slot]
            self.dma_chain[eng][slot] = o
            o.needed = True
        else:
            self.last_op[eng] = o
        self.pending[eng].append(o)
        self.n_ops += 1
        return o

    def _wait_for(self, e, engobj, d):
        known = self.known[e]
        if d.is_dma:
            key = ("d", d.eng, d.dma_slot)
            val = d.dma_cnt * 16
            sem = self.dsems[d.eng][d.dma_slot]
        else:
            ep = (d.milestone - 1) // EPOCH
            key = ("c", d.eng, ep)
            val = d.milestone - ep * EPOCH
            sem = self.sems[d.eng][ep]
        if known.get(key, 0) >= val:
            return
        engobj.wait_ge(sem, val)
        known[key] = val

    def emit(self, final=False):
        nc = self.nc
        tails = []
        for e in ENGS:
            o = self.last_op[e]
            if o is not None:
                o.needed = True
                tails.append(o)
            for d in self.dma_chain[e]:
                if d is not None:
                    tails.append(d)
        for e in ENGS:
            for o in self.pending[e]:
                if not o.is_dma and o.needed:
                    self.ms_count[e] += 1
                    o.milestone = self.ms_count[e]
            assert self.ms_count[e] < EPOCH * NEPOCH, (e, self.ms_count[e])
        finals = [o for e in ENGS for o in self.pending[e] if o.final]

        def run(e, engobj):
            for o in self.pending[e]:
                for d in o.deps:
                    self._wait_for(e, engobj, d)
                if o.is_dma and o.dma_prev is not None:
                    self._wait_for(e, engobj, o.dma_prev)
                ins = o.fn(engobj)
                if o.is_dma:
                    ins.then_inc(self.dsems[e][o.dma_slot], 16)
                elif o.needed:
                    ep = (o.milestone - 1) // EPOCH
                    ins.then_inc(self.sems[e][ep], 1)
            if e == "sp" and final:
                for o in finals:
                    self._wait_for(e, engobj, o)

        with nc.Block() as block:
            block.tensor(lambda eng: run("pe", eng))
            block.scalar(lambda eng: run("act", eng))
            block.vector(lambda eng: run("dve", eng))
            block.gpsimd(lambda eng: run("pool", eng))
            block.sync(lambda eng: run("sp", eng))
        self.pending = {e: [] for e in ENGS}
        self.phase += 1
        for e in ENGS:
            self.barrier[e] = list(tails)


D = 1024
KD = 8
TT = 256
CH = 64
IN_W = 5648
D_FF = 2816
NFF = 22
EPS = 1e-6
GQ, GK, GV, GZ, GB, GA = 0, 512, 1024, 1536, 2048, 2052
MQ, MK, MV, MO, MI, MF = 2056, 2312, 2568, 3080, 3592, 3596
GTA, GTB = 3600, 4624
W1A = 3600
V_GMIX, V_GFFN, V_GPLE, V_CONV, V_GDNN, V_MLN, V_ALOG, V_DTB, V_IB, V_FB, NV = 0, 8, 16, 24, 72, 73, 77, 81, 85, 89, 93
C_ID, C_U, C_SL, C_MLN, C_MS01, C_MUN, C_ONE, NCONST = 0, 128, 192, 256, 320, 384, 448, 576
NEG = -30000.0


def make_consts():
    c = np.zeros((128, NCONST), np.float32)
    c[:, C_ID:C_ID + 128] = np.eye(128, dtype=np.float32)
    i = np.arange(64)
    c[:64, C_U:C_U + 64] = (i[:, None] <= i[None, :])
    c[:64, C_SL:C_SL + 64] = (i[:, None] > i[None, :])
    c[:64, C_MLN:C_MLN + 64] = np.where(i[None, :] <= i[:, None], 0.0, NEG)
    c[:64, C_MS01:C_MS01 + 64] = (i[None, :] < i[:, None])
    c[:64, C_MUN:C_MUN + 64] = np.where(i[:, None] <= i[None, :], 0.0, NEG)
    c[:, C_ONE:C_ONE + 128] = 1.0
    return c


class Builder:
    def __init__(self, S, L, last_final=True, x_in_name="xT", dbg=False):
        self.S, self.L = S, L
        self.NT = S // TT
        nc = self.nc = bass.Bass("TRN2", target_bir_lowering=False)
        self.stack = contextlib.ExitStack()
        dr = lambda name, shape, dt=F32, kind="ExternalInput": nc.dram_tensor(name, shape, dt, kind=kind).ap()
        self.xT = dr("xT", [D, S])
        self.pT = dr("pT", [L, 256, S])
        self.vec = dr("vec", [128, L * NV])
        self.gfin = dr("gfin", [128, KD])
        self.cst = dr("cst", [128, NCONST])
        self.w_in = dr("w_in", [L, D, IN_W])
        self.w_ba = dr("w_branch_a", [L, 512, D])
        self.w_bb = dr("w_branch_b", [L, 512, D])
        self.w_out = dr("w_out", [L, D, D])
        self.w1 = dr("w1", [L, D, D_FF])
        self.w3 = dr("w3", [L, D, D_FF])
        self.w2 = dr("w2", [L, D_FF, D])
        self.w_pg = dr("w_ple_gate", [L, D, D])
        self.w_ple = dr("w_ple", [L, 256, D])
        self.outT = dr("outT", [D, S], kind="ExternalOutput")
        self.xa = nc.dram_tensor("xa_s", [D, S], F32).ap()
        self.xb = nc.dram_tensor("xb_s", [D, S], F32).ap()
        self.oa_d = nc.dram_tensor("oa_s", [512, S], BF16).ap()
        self.ob_d = nc.dram_tensor("ob_s", [512, S], BF16).ap()
        self.dbg = {}
        self.want_dbg = dbg

    BAD_LO, BAD_HI = 176000, 180288

    def sb(self, st, name, shape, dt):
        nm = f"{name}_{self.P.phase}"
        cm = self.nc.sbuf_tensor(nm, shape, dt)
        t = cm.__enter__()
        addr = int(self.nc.lookup_mloc(t).addr)
        nbytes = int(np.prod(shape[1:])) * (2 if dt == BF16 else 4)
        if addr < self.BAD_HI and addr + nbytes > self.BAD_LO:
            cm.__exit__(None, None, None)
            st.enter_context(self.nc.sbuf_tensor(f"hole_{nm}", [128, (self.BAD_HI - addr + 3) // 4], F32))
            return st.enter_context(self.nc.sbuf_tensor(nm + "_r", shape, dt))
        st.push(cm)
        return t

    def ps(self, st, name, shape, dt):
        return st.enter_context(self.nc.psum_tensor(f"{name}_{self.P.phase}", shape, dt))

    def mm(self, out, lhsT, rhs, start, stop, reads, writes):
        return self.P.op("pe", lambda e: e.matmul(out, lhsT=lhsT, rhs=rhs, start=start, stop=stop), reads, writes)

    def tr(self, out, in_, ident, reads, writes):
        return self.P.op("pe", lambda e: e.transpose(out=out, in_=in_, identity=ident), reads, writes)

    def act(self, out, in_, func, reads, writes, bias=None, scale=None, accum=None):
        kw = {}
        if bias is not None:
            kw["bias"] = bias
        if scale is not None:
            kw["scale"] = scale
        if accum is not None:
            kw["accum_out"] = accum
        return self.P.op("act", lambda e: e.activation(out=out, in_=in_, func=func, **kw), reads, writes)

    def tt(self, eng, out, in0, in1, op, reads, writes):
        return self.P.op(eng, lambda e: e.tensor_tensor(out=out, in0=in0, in1=in1, op=op), reads, writes)

    def ts(self, eng, out, in0, s1, s2, op0, op1, reads, writes):
        if op1 is None:
            return self.P.op(eng, lambda e: e.tensor_scalar(out=out, in0=in0, scalar1=s1, scalar2=None, op0=op0), reads, writes)
        return self.P.op(eng, lambda e: e.tensor_scalar(out=out, in0=in0, scalar1=s1, scalar2=s2, op0=op0, op1=op1), reads, writes)

    def stt(self, eng, out, in0, scalar, in1, op0, op1, reads, writes):
        return self.P.op(eng, lambda e: e.scalar_tensor_tensor(out=out, in0=in0, scalar=scalar, in1=in1, op0=op0, op1=op1), reads, writes)

    def cp(self, eng, out, in_, reads, writes):
        if eng == "act":
            return self.P.op("act", lambda e: e.copy(out=out, in_=in_), reads, writes)
        return self.P.op(eng, lambda e: e.tensor_copy(out=out, in_=in_), reads, writes)

    def dma(self, eng, out, in_, reads, writes, final=False):
        o = self.P.op(eng, lambda e: e.dma_start(out=out, in_=in_), reads, writes, dma=True)
        o.final = final
        return o

    def cut(self, k):
        import os
        return int(os.environ.get("MK_CUT", "99")) <= k

    def tap(self, name, src_ap, shape, reads):
        if not self.want_dbg or name in self.dbg:
            return
        t = self.nc.dram_tensor("dbg_" + name, list(shape), F32, kind="ExternalOutput").ap()
        self.dbg[name] = t
        self.dma("sp", t, src_ap, reads, [], final=True)

    def build(self):
        nc = self.nc
        with self.stack as top:
            self.P = P = Prog(nc, top)
            self.cf = self.sb(top, "cf", [128, NCONST], F32)
            self.identb = self.sb(top, "identb", [128, 128], BF16)
            self.onesb = self.sb(top, "onesb", [128, 128], BF16)
            self.vecs = self.sb(top, "vecs", [128, self.L * NV], F32)
            self.g32 = self.sb(top, "g32", [128, self.L * 24 + 8], F32)
            self.posa = self.sb(top, "posa", [128, self.L * 4], F32)
            self.nega = self.sb(top, "nega", [128, self.L * 4], F32)
            self.gfs = self.sb(top, "gfs", [128, KD], F32)
            self.dma("sp", self.cf[:], self.cst, [], ["cf"])
            self.dma("sp", self.vecs[:], self.vec, [], ["vecs"])
            self.dma("sp", self.gfs[:], self.gfin, [], ["gfs"])
            self.cp("dve", self.identb[:], self.cf[:, C_ID:C_ID + 128], ["cf"], ["identb"])
            self.cp("dve", self.onesb[:], self.cf[:, C_ONE:C_ONE + 128], ["cf"], ["onesb"])
            for l in range(self.L):
                self.ts("dve", self.g32[:, l * 24:(l + 1) * 24], self.vecs[:, l * NV:l * NV + 24], 32.0, None,
                        ALU.mult, None, ["vecs"], ["g32"])
                self.act(self.posa[:, l * 4:(l + 1) * 4], self.vecs[:, l * NV + V_ALOG:l * NV + V_ALOG + 4], AF.Exp,
                         ["vecs"], ["posa"])
            self.ts("dve", self.g32[:, self.L * 24:self.L * 24 + 8], self.gfs[:], 32.0, None, ALU.mult, None, ["gfs"], ["g32"])
            self.ts("dve", self.nega[:], self.posa[:], -1.0, None, ALU.mult, None, ["posa"], ["nega"])
            for l in range(self.L):
                src = self.xT if l == 0 else self.xb
                last = (l == self.L - 1)
                import os
                stop = int(os.environ.get("MK_STOP", "99"))
                with contextlib.ExitStack() as st:
                    if stop >= 1:
                        self.phase_1a(st, l, src)
                    P.emit()
                with contextlib.ExitStack() as st:
                    if stop >= 2:
                        self.phase_1b(st, l, src)
                    P.emit()
                with contextlib.ExitStack() as st:
                    if stop >= 3:
                        self.phase_2(st, l, last)
                    P.emit(final=last)
        return nc

    def load_w(self, dst, src2d, kc_n, ncols, key, col0=0, split=2):
        step = (ncols + split - 1) // split
        for kc in range(kc_n):
            for c0 in range(0, ncols, step):
                c1 = min(ncols, c0 + step)
                self.dma("pool", dst[:, kc, c0:c1], src2d[kc * 128:(kc + 1) * 128, col0 + c0:col0 + c1], [], [key])

    def rmsnorm(self, xin, sq, hT, rstd, pss, gcol, keys):
        kx, ksq, kh, krs, kps = keys
        self.act(sq[:], xin[:], AF.Square, [kx], [ksq])
        for kc in range(KD):
            self.mm(pss, self.onesb[:], sq[:, kc, :], kc == 0, kc == KD - 1, [ksq, "onesb"], [kps])
        self.act(rstd[:], pss, AF.Ln, [kps], [krs], bias=float(D * EPS))
        self.act(rstd[:], rstd[:], AF.Exp, [krs], [krs], scale=-0.5)
        for kc in range(KD):
            self.stt("dve", hT[:, kc, :], xin[:, kc, :], self.g32[:, gcol + kc:gcol + kc + 1], rstd[:],
                     ALU.mult, ALU.mult, [kx, krs, "g32"], [kh])

    def phase_1a(self, st, l, xsrc):
        P, S = self.P, self.S
        sb, ps = self.sb, self.ps
        vb = l * NV
        W = sb(st, "W1a", [128, KD, W1A], BF16)
        self.load_w(W, self.w_in[l], KD, W1A, "W1a", split=3)
        if self.cut(0):
            return
        xin = sb(st, "xin", [128, KD, TT], F32)
        sq = sb(st, "sq", [128, KD, TT], BF16)
        hT = sb(st, "hT", [128, KD, TT], BF16)
        rstd = sb(st, "rstd", [128, TT], F32)
        raw = [sb(st, f"raw{i}", [128, TT + 3], F32) for i in range(2)]
        halo = sb(st, "halo", [128, 12, 3], F32)
        acc = [sb(st, f"acc{i}", [128, TT], F32) for i in range(2)]
        sl = [sb(st, f"sl{i}", [128, TT], F32) for i in range(2)]
        sqk = [sb(st, f"sqk{i}", [128, TT], BF16) for i in range(2)]
        rn = [sb(st, f"rn{i}", [128, TT], F32) for i in range(2)]
        gqT = sb(st, "gqT", [128, 4, TT], BF16)
        gkT = sb(st, "gkT", [128, 4, TT], BF16)
        gvT = sb(st, "gvT", [128, 4, TT], BF16)
        gzT = sb(st, "gzT", [128, 4, TT], BF16)
        mqT = sb(st, "mqT", [64, 4, TT], BF16)
        mkT = sb(st, "mkT", [64, 4, TT], BF16)
        mvT = sb(st, "mvT", [128, 4, TT], BF16)
        moT = sb(st, "moT", [128, 4, TT], BF16)
        ztmp = [sb(st, f"ztmp{i}", [128, TT], F32) for i in range(2)]
        gv_tok = sb(st, "gv_tok", [64, 4, 4, 128], BF16)
        gk_tok = sb(st, "gk_tok", [64, 4, 4, 128], BF16)
        mv_tok = sb(st, "mv_tok", [64, 4, 4, 128], BF16)
        mk_tok = sb(st, "mk_tok", [64, 4, 4, 64], BF16)
        oaT = sb(st, "oaT", [128, 4, TT], BF16)
        obT = sb(st, "obT", [128, 4, TT], BF16)
        Sg = sb(st, "Sg", [128, 4, 128], F32)
        Sgb = sb(st, "Sgb", [128, 4, 128], BF16)
        Cm = sb(st, "Cm", [64, 4, 128], F32)
        Cmb = sb(st, "Cmb", [64, 4, 128], BF16)
        nm = sb(st, "nm", [64, 4], F32)
        nmb = sb(st, "nmb", [64, 4], BF16)
        T = {}
        def tmp(name, shape, dt):
            T[name] = sb(st, name, shape, dt)
            return T[name]
        tmp("g_at", [64, 4], F32); tmp("beta", [64, 4], F32); tmp("g_ea", [64, 4], F32); tmp("g_sp", [64, 4], F32)
        tmp("g_g", [64, 4], F32); tmp("g_ng", [64, 4], F32); tmp("Gdn", [64, 4, 64], F32); tmp("gbc", [64, 4, 64], F32)
        tmp("e8", [64, 8], F32); tmp("egl", [128, 4], F32); tmp("bg", [64, 4], F32)
        tmp("Ef", [64, 4, 64], F32); tmp("dtm", [64, 4, 64], F32); tmp("T1", [64, 4, 64], F32); tmp("T2", [64, 4, 64], F32)
        tmp("attn", [64, 256], BF16); tmp("NAT", [64, 512], BF16)
        for j in range(6):
            tmp(f"NM{j}", [64, 512], BF16)
            tmp(f"IpN{j}", [64, 4, 64], BF16)
        tmp("X0", [64, 4, 256], BF16); tmp("X1", [64, 4, 256], BF16)
        tmp("u", [64, 512], F32); tmp("wT", [128, 256], BF16); tmp("vnew", [64, 512], BF16)
        tmp("qs", [64, 4, 128], F32); tmp("o", [64, 4, 128], F32); tmp("kdec", [64, 4, 128], BF16)
        tmp("junk", [64, 128], F32); tmp("ssq", [64, 4], F32); tmp("rs", [64, 4], F32); tmp("on", [64, 4, 128], BF16)
        tmp("m_ig", [64, 4], F32); tmp("m_fg", [64, 4], F32); tmp("m_ef", [64, 4], F32); tmp("m_sp", [64, 4], F32)
        tmp("m_lf", [64, 4], F32); tmp("Ld", [64, 4, 64], F32); tmp("nlb", [64, 4, 64], F32); tmp("ibc", [64, 4, 64], F32)
        tmp("m_eb", [64, 4], F32); tmp("m_ebl", [64, 4], F32); tmp("m_a", [64, 4], F32); tmp("m_ea", [64, 4], F32)
        tmp("sT", [64, 256], BF16); tmp("nume", [64, 4, 128], F32); tmp("num", [64, 4, 128], F32)
        tmp("den", [64, 4], F32); tmp("dene", [64, 4], F32); tmp("rr", [64, 4], F32); tmp("sc", [64, 4], F32)
        tmp("hn", [64, 4, 128], BF16); tmp("kw", [64, 4, 64], BF16)
        ppb = [ps(st, f"ppb{i}", [128, 512], F32) for i in range(2)]
        pp = [ppb[i][:, 0:256] for i in range(2)]
        ptrb = ps(st, "ptrb", [128, 1024], BF16)
        ptr = [ptrb[:, i * 512:(i + 1) * 512] for i in range(2)]
        psm = ps(st, "psm", [128, 512], F32)
        pA = ps(st, "pA", [128, 512], F32)
        pB = ps(st, "pB", [128, 512], F32)
        pCD = ps(st, "pCD", [128, 1024], F32)
        mA, mB = ppb[0], ppb[1]
        cf = self.cf
        Uf = cf[0:64, C_U:C_U + 64]
        SLf = cf[0:64, C_SL:C_SL + 64]
        ones64 = cf[0:64, C_ONE:C_ONE + 64]
        ones64x128 = cf[0:64, C_ONE:C_ONE + 128]
        idb64 = self.identb[0:64, 0:64]
        SQ128 = float(np.sqrt(128.0))

        def bc_h(ap64):
            return ap64[:, None, :].broadcast_to([64, 4, 64])

        def bc_last(ap, n):
            return ap[:, :, None].broadcast_to([64, 4, n])

        self.P.op("pool", lambda e: e.memset(Sg[:], 0.0), [], ["Sg"])
        self.P.op("pool", lambda e: e.memset(Sgb[:], 0.0), [], ["Sgb"])
        self.P.op("pool", lambda e: e.memset(Cm[:], 0.0), [], ["Cm"])
        self.P.op("pool", lambda e: e.memset(Cmb[:], 0.0), [], ["Cmb"])
        self.P.op("pool", lambda e: e.memset(nm[:], 0.0), [], ["nm"])
        self.P.op("pool", lambda e: e.memset(nmb[:], 0.0), [], ["nmb"])
        self.P.op("pool", lambda e: e.memset(halo[:], 0.0), [], ["halo"])

        ppi = [0]

        def proj(col0, ncols=128):
            i = ppi[0] % 2
            ppi[0] += 1
            for kc in range(KD):
                self.mm(pp[i][0:ncols, :], W[:, kc, col0:col0 + ncols], hT[:, kc, :], kc == 0, kc == KD - 1,
                        ["W1a", "hT"], [f"pp{i}"])
            return pp[i], f"pp{i}"

        for tt in range(self.NT):
            t0 = tt * TT
            self.dma("sp", xin[:], xsrc[:, t0:t0 + TT].rearrange("(k p) t -> p k t", p=128), [], ["xin"])
            self.rmsnorm(xin, sq, hT, rstd, psm[:, 256:512], l * 24 + 0, ("xin", "sq", "hT", "rstd", "psm"))
            if self.cut(1):
                return
            import os
            ncc = int(os.environ.get("MK_NCC", "12")); stg = int(os.environ.get("MK_ST", "9"))
            for cc in range(ncc):
                pt, pk = proj(cc * 128)
                r = raw[cc % 2]; rk = f"raw{cc % 2}"
                a = acc[cc % 2]; ak = f"acc{cc % 2}"
                self.cp("pool", r[:, 0:3], halo[:, cc, :], ["halo"], [rk])
                self.cp("act", r[:, 3:TT + 3], pt[:], [pk], [rk])
                self.cp("pool", halo[:, cc, :], r[:, TT:TT + 3], [rk], ["halo"])
                if stg < 2:
                    continue
                wc = lambda j: self.vecs[:, vb + V_CONV + cc * 4 + j: vb + V_CONV + cc * 4 + j + 1]
                self.ts("dve", a[:], r[:, 3:TT + 3], wc(3), None, ALU.mult, None, [rk, "vecs"], [ak])
                for j in (2, 1, 0):
                    self.stt("dve", a[:], r[:, j:j + TT], wc(j), a[:], ALU.mult, ALU.add, [rk, ak, "vecs"], [ak])
                kind, hh = cc // 4, cc % 4
                if stg < 3:
                    continue
                if kind == 2:
                    self.act(gvT[:, hh, :], a[:], AF.Silu, [ak], ["gvT"])
                else:
                    s_ = sl[cc % 2]; sk = f"sl{cc % 2}"
                    q_ = sqk[cc % 2]; qk_ = f"sqk{cc % 2}"
                    r_ = rn[cc % 2]; rnk = f"rn{cc % 2}"
                    self.act(s_[:], a[:], AF.Silu, [ak], [sk])
                    self.act(q_[:], s_[:], AF.Square, [sk], [qk_])
                    self.mm(psm[:, 256:512], self.onesb[:], q_[:], True, True, [qk_, "onesb"], ["psm"])
                    self.act(r_[:], psm[:, 256:512], AF.Ln, ["psm"], [rnk], bias=float(EPS))
                    self.act(r_[:], r_[:], AF.Exp, [rnk], [rnk], scale=-0.5)
                    dst, dk_ = (gqT, "gqT") if kind == 0 else (gkT, "gkT")
                    if kind == 0:
                        self.stt("dve", dst[:, hh, :], s_[:], float(128.0 ** -0.5), r_[:], ALU.mult, ALU.mult, [sk, rnk], [dk_])
                    else:
                        self.tt("dve", dst[:, hh, :], s_[:], r_[:], ALU.mult, [sk, rnk], [dk_])
            if self.cut(2):
                return
            for hh in range(4):
                pt, pk = proj(GZ + hh * 128)
                z = ztmp[hh % 2]; zk = f"ztmp{hh % 2}"
                self.act(z[:], pt[:], AF.Silu, [pk], [zk])
                self.ts("dve", gzT[:, hh, :], z[:], self.vecs[:, vb + V_GDNN:vb + V_GDNN + 1], None, ALU.mult, None,
                        [zk, "vecs"], ["gzT"])
            for hh in range(4):
                pt, pk = proj(MQ + hh * 64, 64)
                self.act(mqT[:, hh, :], pt[0:64, :], AF.Copy, [pk], ["mqT"], scale=0.125)
                pt, pk = proj(MK + hh * 64, 64)
                self.cp("act", mkT[:, hh, :], pt[0:64, :], [pk], ["mkT"])
            for hh in range(4):
                pt, pk = proj(MV + hh * 128)
                self.cp("act", mvT[:, hh, :], pt[:], [pk], ["mvT"])
                pt, pk = proj(MO + hh * 128)
                z = ztmp[hh % 2]; zk = f"ztmp{hh % 2}"
                self.act(z[:], pt[:], AF.Sigmoid, [pk], [zk])
                self.ts("dve", moT[:, hh, :], z[:], self.vecs[:, vb + V_MLN + hh:vb + V_MLN + hh + 1], None, ALU.mult, None,
                        [zk, "vecs"], ["moT"])
            if self.cut(3):
                return
            for c in range(4):
                cs = slice(c * 64, (c + 1) * 64)
                for (srcT, sk, dst, dk_) in ((gvT, "gvT", gv_tok, "gv_tok"), (gkT, "gkT", gk_tok, "gk_tok"), (mvT, "mvT", mv_tok, "mv_tok")):
                    p_ = ptr[c % 2]; pk = "ptr"
                    for hh in range(4):
                        self.tr(p_[0:64, hh * 128:(hh + 1) * 128], srcT[:, hh, cs], self.identb[:], [sk, "identb"], [pk])
                    self.cp("act", dst[:, c, :, :], p_[0:64, :].rearrange("p (h e) -> p h e", h=4), [pk], [dk_])
                p_ = ptr[c % 2]; pk = "ptr"
                for hh in range(4):
                    self.tr(p_[0:64, hh * 64:(hh + 1) * 64], mkT[:, hh, cs], idb64, ["mkT", "identb"], [pk])
                self.cp("act", mk_tok[:, c, :, :], p_[0:64, 0:256].rearrange("p (h e) -> p h e", h=4), [pk], ["mk_tok"])

            if self.cut(4):
                return
            for c in range(4):
                cs = slice(c * 64, (c + 1) * 64)
                for kc in range(KD):
                    self.mm(psm[0:64, 0:8], hT[:, kc, cs], W[:, kc, GB:GB + 8], kc == 0, kc == KD - 1, ["hT", "W1a"], ["psm"])
                for kc in range(KD):
                    self.mm(psm[0:64, 8:16], hT[:, kc, cs], W[:, kc, MI:MI + 8], kc == 0, kc == KD - 1, ["hT", "W1a"], ["psm"])
                def gdn_gen():
                    self.tt("dve", T["g_at"][:], psm[0:64, 4:8], self.vecs[0:64, vb + V_DTB:vb + V_DTB + 4], ALU.add, ["psm", "vecs"], ["g_at"])
                    self.act(T["beta"][:], psm[0:64, 0:4], AF.Sigmoid, ["psm"], ["beta"])
                    self.act(T["g_ea"][:], T["g_at"][:], AF.Exp, ["g_at"], ["g_ea"])
                    self.act(T["g_sp"][:], T["g_ea"][:], AF.Ln, ["g_ea"], ["g_sp"], bias=1.0)
                    self.tt("dve", T["g_g"][:], T["g_sp"][:], self.nega[0:64, l * 4:l * 4 + 4], ALU.mult, ["g_sp", "nega"], ["g_g"])
                    self.tt("dve", T["g_ng"][:], T["g_sp"][:], self.posa[0:64, l * 4:l * 4 + 4], ALU.mult, ["g_sp", "posa"], ["g_ng"])
                    self.tt("dve", T["Gdn"][:], bc_h(Uf), bc_last(T["g_ng"][:], 64), ALU.mult, ["g_ng", "cf"], ["Gdn"])
                    self.cp("pool", T["gbc"][:], bc_last(T["g_g"][:], 64), ["g_g"], ["gbc"])
                    self.mm(pA[0:64, 0:256], Uf, T["gbc"][:].rearrange("p h s -> p (h s)"), True, False, ["gbc", "cf"], ["pA"])
                    self.mm(pA[0:64, 0:256], ones64, T["Gdn"][:].rearrange("p h s -> p (h s)"), False, True, ["Gdn", "cf"], ["pA"])
                    self.mm(psm[0:64, 16:20], Uf, T["g_g"][:], True, True, ["g_g", "cf"], ["psm"])
                    self.mm(psm[0:64, 20:24], SLf, T["g_g"][:], True, True, ["g_g", "cf"], ["psm"])
                    self.mm(psm[:, 24:28], ones64x128, T["g_g"][:], True, True, ["g_g", "cf"], ["psm"])
                    yield
                    self.act(T["e8"][:], psm[0:64, 16:24], AF.Exp, ["psm"], ["e8"])
                    self.act(T["egl"][:], psm[:, 24:28], AF.Exp, ["psm"], ["egl"])
                    e_gc = T["e8"][:, 0:4]; e_rc = T["e8"][:, 4:8]
                    self.tt("dve", T["bg"][:], T["beta"][:], e_gc, ALU.mult, ["beta", "e8"], ["bg"])
                    self.tt("dve", T["Ef"][:], pA[0:64, 0:256].rearrange("p (h s) -> p h s", h=4), bc_h(cf[0:64, C_MLN:C_MLN + 64]), ALU.add,
                            ["pA", "cf"], ["Ef"])
                    self.act(T["Ef"][:], T["Ef"][:], AF.Exp, ["Ef"], ["Ef"])
                    for hh in range(4):
                        self.mm(pB[0:64, hh * 64:(hh + 1) * 64], gkT[:, hh, cs], gkT[:, hh, cs], True, True, ["gkT"], ["pB"])
                    for hh in range(4):
                        self.mm(pB[0:64, 256 + hh * 64:256 + (hh + 1) * 64], gqT[:, hh, cs], gkT[:, hh, cs], True, True, ["gqT", "gkT"], ["pB"])
                    yield
                    self.tt("dve", T["T1"][:], pB[0:64, 0:256].rearrange("p (h s) -> p h s", h=4), T["Ef"][:], ALU.mult, ["pB", "Ef"], ["T1"])
                    self.tt("dve", T["T2"][:], T["T1"][:], bc_last(T["beta"][:], 64), ALU.mult, ["T1", "beta"], ["T2"])
                    M0 = T["NM0"][:, 256:512]
                    self.tt("dve", M0.rearrange("p (h s) -> p h s", h=4), T["T2"][:], bc_h(cf[0:64, C_MS01:C_MS01 + 64]), ALU.mult,
                            ["T2", "cf"], ["NM0m"])
                    self.tt("dve", T["attn"][:].rearrange("p (h s) -> p h s", h=4), pB[0:64, 256:512].rearrange("p (h s) -> p h s", h=4),
                            T["Ef"][:], ALU.mult, ["pB", "Ef"], ["attn"])
                    p_ = ptr[0]; pk = "ptr"
                    for hh in range(4):
                        self.tr(p_[0:64, hh * 64:(hh + 1) * 64], M0[:, hh * 64:(hh + 1) * 64], idb64, ["NM0m", "identb"], [pk])
                    for hh in range(4):
                        self.tr(p_[0:64, 256 + hh * 64:256 + (hh + 1) * 64], T["attn"][:, hh * 64:(hh + 1) * 64], idb64, ["attn", "identb"], [pk])
                    yield
                    self.cp("act", T["NM0"][:, 0:256], p_[0:64, 0:256], [pk], ["NM0n"])
                    self.cp("act", T["NAT"][:, 0:256], p_[0:64, 256:512], [pk], ["NAT"])
                    attnT = T["NAT"][:, 0:256]
                    self.tt("pool", T["IpN0"][:], bc_h(idb64), T["NM0"][:, 0:256].rearrange("p (h s) -> p h s", h=4), ALU.subtract,
                            ["NM0n", "identb"], ["IpN0"])
                    self.tt("dve", T["X0"][:, :, 0:128], gv_tok[:, c, :, :], bc_last(T["beta"][:], 128), ALU.mult, ["gv_tok", "beta"], ["X0"])
                    self.tt("dve", T["X0"][:, :, 128:256], gk_tok[:, c, :, :], bc_last(T["bg"][:], 128), ALU.mult, ["gk_tok", "bg"], ["X0"])
                    for j in range(1, 6):
                        prv = T[f"NM{j - 1}"]; cur = T[f"NM{j}"]
                        pn, pm = f"NM{j - 1}n", f"NM{j - 1}m"
                        for hh in range(4):
                            hs = slice(hh * 64, (hh + 1) * 64)
                            hs2 = slice(256 + hh * 64, 256 + (hh + 1) * 64)
                            self.mm(pA[0:64, hs], prv[:, hs2], prv[:, hs], True, True, [pn, pm], ["pA"])
                            if j < 5:
                                self.mm(pA[0:64, hs2], prv[:, hs], prv[:, hs2], True, True, [pn, pm], ["pA"])
                        ja = j - 1
                        Xs = T[f"X{ja % 2}"]; Xd = T[f"X{(ja + 1) % 2}"]
                        for hh in range(4):
                            self.mm(pCD[0:64, hh * 256:(hh + 1) * 256], T[f"IpN{ja}"][:, hh, :], Xs[:, hh, :], True, True,
                                    [f"IpN{ja}", f"X{ja % 2}"], ["pCD"])
                        if j < 5:
                            self.cp("act" if j % 2 else "dve", cur[:, :], pA[0:64, :], ["pA"], [f"NM{j}n", f"NM{j}m"])
                        else:
                            self.cp("act", cur[:, 0:256], pA[0:64, 0:256], ["pA"], [f"NM{j}n"])
                        self.cp("dve" if j % 2 else "act", Xd[:].rearrange("p h n -> p (h n)"), pCD[0:64, :], ["pCD"], [f"X{(ja + 1) % 2}"])
                        self.tt("pool", T[f"IpN{j}"][:], bc_h(idb64), cur[:, 0:256].rearrange("p (h s) -> p h s", h=4), ALU.add,
                                [f"NM{j}n", "identb"], [f"IpN{j}"])
                    X5 = T["X1"]
                    for hh in range(4):
                        self.mm(pCD[0:64, hh * 128:(hh + 1) * 128], T["IpN5"][:, hh, :], X5[:, hh, 0:128], True, True, ["IpN5", "X1"], ["pCD"])
                    for hh in range(4):
                        self.mm(pCD[:, 512 + hh * 64:512 + (hh + 1) * 64], X5[:, hh, 128:256], T["IpN5"][:, hh, :], True, True, ["IpN5", "X1"], ["pCD"])
                    yield
                    self.cp("dve", T["u"][:], pCD[0:64, 0:512], ["pCD"], ["u"])
                    self.cp("act", T["wT"][:], pCD[:, 512:768], ["pCD"], ["wT"])
                    for hh in range(4):
                        self.mm(pA[0:64, hh * 128:(hh + 1) * 128], T["wT"][:, hh * 64:(hh + 1) * 64], Sgb[:, hh, :], True, True, ["wT", "Sgb"], ["pA"])
                    for hh in range(4):
                        self.mm(pB[0:64, hh * 128:(hh + 1) * 128], gqT[:, hh, cs], Sgb[:, hh, :], True, True, ["gqT", "Sgb"], ["pB"])
                    yield
                    self.tt("dve", T["vnew"][:], T["u"][:], pA[0:64, :], ALU.subtract, ["u", "pA"], ["vnew"])
                    for hh in range(4):
                        self.mm(pCD[0:64, hh * 128:(hh + 1) * 128], attnT[:, hh * 64:(hh + 1) * 64], T["vnew"][:, hh * 128:(hh + 1) * 128], True, True,
                                ["NAT", "vnew"], ["pCD"])
                    yield
                    for hh in range(4):
                        self.act(T["qs"][:, hh, :], pB[0:64, hh * 128:(hh + 1) * 128], AF.Copy, ["pB", "e8"], ["qs"], scale=T["e8"][:, hh:hh + 1])
                    self.tt("dve", T["o"][:], T["qs"][:], pCD[0:64, 0:512].rearrange("p (h e) -> p h e", h=4), ALU.add, ["qs", "pCD"], ["o"])
                    self.tt("dve", T["kdec"][:], gk_tok[:, c, :, :], bc_last(e_rc, 128), ALU.mult, ["gk_tok", "e8"], ["kdec"])
                    for hh in range(4):
                        self.mm(pCD[:, 512 + hh * 128:512 + (hh + 1) * 128], T["kdec"][:, hh, :], T["vnew"][:, hh * 128:(hh + 1) * 128], True, True,
                                ["kdec", "vnew"], ["pCD"])
                    yield
                    for hh in range(4):
                        self.stt("dve", Sg[:, hh, :], Sg[:, hh, :], T["egl"][:, hh:hh + 1], pCD[:, 512 + hh * 128:512 + (hh + 1) * 128],
                                 ALU.mult, ALU.add, ["Sg", "egl", "pCD"], ["Sg"])
                    self.cp("act", Sgb[:], Sg[:], ["Sg"], ["Sgb"])
                    self.P.op("pool", lambda e: e.memset(T["ssq"][:], 0.0), [], ["ssq"])
                    for hh in range(4):
                        self.act(T["junk"][:], T["o"][:, hh, :], AF.Square, ["o"], ["junk", "ssq"], accum=T["ssq"][:, hh:hh + 1])
                    self.act(T["rs"][:], T["ssq"][:], AF.Ln, ["ssq"], ["rs"], bias=float(128 * EPS))
                    self.act(T["rs"][:], T["rs"][:], AF.Exp, ["rs"], ["rs"], scale=-0.5)
                    self.stt("dve", T["on"][:], T["o"][:], SQ128, bc_last(T["rs"][:], 128), ALU.mult, ALU.mult, ["o", "rs"], ["on"])
                    p_ = ptr[1]; pk = "ptr"
                    for hh in range(4):
                        self.tr(p_[:, hh * 64:(hh + 1) * 64], T["on"][:, hh, :], idb64, ["on", "identb"], [pk])
                    yield
                    self.tt("dve", oaT[:, :, cs], p_[:, 0:256].rearrange("p (h c) -> p h c", h=4), gzT[:, :, cs], ALU.mult, [pk, "gzT"], ["oaT"])
                    yield
                def mls_gen():
                    self.tt("dve", T["m_ig"][:], psm[0:64, 8:12], self.vecs[0:64, vb + V_IB:vb + V_IB + 4], ALU.add, ["psm", "vecs"], ["m_ig"])
                    self.tt("dve", T["m_fg"][:], psm[0:64, 12:16], self.vecs[0:64, vb + V_FB:vb + V_FB + 4], ALU.add, ["psm", "vecs"], ["m_fg"])
                    self.act(T["m_ef"][:], T["m_fg"][:], AF.Exp, ["m_fg"], ["m_ef"], scale=-1.0)
                    self.act(T["m_sp"][:], T["m_ef"][:], AF.Ln, ["m_ef"], ["m_sp"], bias=1.0)
                    self.ts("dve", T["m_lf"][:], T["m_sp"][:], -1.0, None, ALU.mult, None, ["m_sp"], ["m_lf"])
                    self.tt("dve", T["Ld"][:], bc_h(Uf), bc_last(T["m_lf"][:], 64), ALU.mult, ["m_lf", "cf"], ["Ld"])
                    self.cp("pool", T["nlb"][:], bc_last(T["m_sp"][:], 64), ["m_sp"], ["nlb"])
                    self.cp("pool", T["ibc"][:], bc_last(T["m_ig"][:], 64), ["m_ig"], ["ibc"])
                    fl = lambda t_: t_[:].rearrange("p h s -> p (h s)")
                    self.mm(mA[0:64, 0:256], ones64, fl(T["Ld"]), True, False, ["Ld", "cf"], ["pp0"])
                    self.mm(mA[0:64, 0:256], Uf, fl(T["nlb"]), False, False, ["nlb", "cf"], ["pp0"])
                    self.mm(mA[0:64, 0:256], cf[0:64, C_ID:C_ID + 64], fl(T["ibc"]), False, True, ["ibc", "cf"], ["pp0"])
                    for hh in range(4):
                        self.mm(mA[0:64, 256 + hh * 64:256 + (hh + 1) * 64], mkT[:, hh, cs], mqT[:, hh, cs], True, True, ["mkT", "mqT"], ["pp0"])
                    self.mm(psm[0:64, 32:36], Uf, T["m_lf"][:], True, True, ["m_lf", "cf"], ["psm"])
                    self.mm(psm[0:64, 36:40], SLf, T["m_lf"][:], True, True, ["m_lf", "cf"], ["psm"])
                    self.mm(psm[0:64, 40:44], ones64, T["m_lf"][:], True, True, ["m_lf", "cf"], ["psm"])
                    yield
                    self.act(T["m_eb"][:], psm[0:64, 32:36], AF.Exp, ["psm"], ["m_eb"])
                    self.act(T["m_ebl"][:], psm[0:64, 40:44], AF.Exp, ["psm"], ["m_ebl"])
                    self.tt("dve", T["m_a"][:], psm[0:64, 36:40], T["m_ig"][:], ALU.add, ["psm", "m_ig"], ["m_a"])
                    self.act(T["m_ea"][:], T["m_a"][:], AF.Exp, ["m_a"], ["m_ea"])
                    self.tt("dve", T["dtm"][:], mA[0:64, 0:256].rearrange("p (h s) -> p h s", h=4), bc_h(cf[0:64, C_MUN:C_MUN + 64]), ALU.add,
                            ["pp0", "cf"], ["dtm"])
                    self.act(T["dtm"][:], T["dtm"][:], AF.Exp, ["dtm"], ["dtm"])
                    self.tt("dve", T["sT"][:].rearrange("p (h s) -> p h s", h=4), mA[0:64, 256:512].rearrange("p (h s) -> p h s", h=4), T["dtm"][:],
                            ALU.mult, ["pp0", "dtm"], ["sT"])
                    for hh in range(4):
                        self.mm(mB[0:64, hh * 128:(hh + 1) * 128], mqT[:, hh, cs], Cmb[:, hh, :], True, True, ["mqT", "Cmb"], ["pp1"])
                    for hh in range(4):
                        self.mm(psm[0:64, 48 + hh:49 + hh], mqT[:, hh, cs], nmb[:, hh:hh + 1], True, True, ["mqT", "nmb"], ["psm"])
                    for hh in range(4):
                        self.mm(psm[0:64, 44 + hh:45 + hh], T["sT"][:, hh * 64:(hh + 1) * 64], self.onesb[0:64, 0:1], True, True, ["sT", "onesb"], ["psm"])
                    yield
                    for hh in range(4):
                        self.act(T["nume"][:, hh, :], mB[0:64, hh * 128:(hh + 1) * 128], AF.Copy, ["pp1", "m_eb"], ["nume"],
                                 scale=T["m_eb"][:, hh:hh + 1])
                    for hh in range(4):
                        self.mm(mB[0:64, hh * 128:(hh + 1) * 128], T["sT"][:, hh * 64:(hh + 1) * 64], mv_tok[:, c, hh, :], True, True, ["sT", "mv_tok"], ["pp1"])
                    yield
                    self.tt("dve", T["num"][:], T["nume"][:], mB[0:64, 0:512].rearrange("p (h e) -> p h e", h=4), ALU.add, ["nume", "pp1"], ["num"])
                    self.tt("dve", T["dene"][:], psm[0:64, 48:52], T["m_eb"][:], ALU.mult, ["psm", "m_eb"], ["dene"])
                    self.tt("dve", T["den"][:], T["dene"][:], psm[0:64, 44:48], ALU.add, ["dene", "psm"], ["den"])
                    self.stt("dve", T["dene"][:], T["den"][:], -1.0, T["den"][:], ALU.mult, ALU.max, ["den"], ["dene"])
                    self.ts("dve", T["den"][:], T["dene"][:], 1.0, None, ALU.max, None, ["dene"], ["den"])
                    self.P.op("dve", lambda e: e.reciprocal(out=T["rr"][:], in_=T["den"][:]), ["den"], ["rr"])
                    self.P.op("pool", lambda e: e.memset(T["ssq"][:], 0.0), [], ["ssq"])
                    for hh in range(4):
                        self.act(T["junk"][:], T["num"][:, hh, :], AF.Square, ["num"], ["junk", "ssq"], accum=T["ssq"][:, hh:hh + 1])
                    self.tt("dve", T["sc"][:], T["rr"][:], T["rr"][:], ALU.mult, ["rr"], ["sc"])
                    self.tt("dve", T["rs"][:], T["ssq"][:], T["sc"][:], ALU.mult, ["ssq", "sc"], ["rs"])
                    self.ts("dve", T["rs"][:], T["rs"][:], 1.0 / 128.0, EPS, ALU.mult, ALU.add, ["rs"], ["rs"])
                    self.act(T["rs"][:], T["rs"][:], AF.Ln, ["rs"], ["rs"])
                    self.act(T["rs"][:], T["rs"][:], AF.Exp, ["rs"], ["rs"], scale=-0.5)
                    self.tt("dve", T["sc"][:], T["rr"][:], T["rs"][:], ALU.mult, ["rr", "rs"], ["sc"])
                    self.tt("dve", T["hn"][:], T["num"][:], bc_last(T["sc"][:], 128), ALU.mult, ["num", "sc"], ["hn"])
                    p_ = ptr[1]; pk = "ptr"
                    for hh in range(4):
                        self.tr(p_[:, 256 + hh * 64:256 + (hh + 1) * 64], T["hn"][:, hh, :], idb64, ["hn", "identb"], [pk])
                    yield
                    self.tt("dve", obT[:, :, cs], p_[:, 256:512].rearrange("p (h c) -> p h c", h=4), moT[:, :, cs], ALU.mult, [pk, "moT"], ["obT"])
                    self.tt("dve", T["kw"][:], mk_tok[:, c, :, :], bc_last(T["m_ea"][:], 64), ALU.mult, ["mk_tok", "m_ea"], ["kw"])
                    for hh in range(4):
                        self.mm(mA[0:64, hh * 128:(hh + 1) * 128], T["kw"][:, hh, :], mv_tok[:, c, hh, :], True, True, ["kw", "mv_tok"], ["pp0"])
                        self.mm(psm[0:64, 52 + hh:53 + hh], T["kw"][:, hh, :], self.onesb[0:64, 0:1], True, True, ["kw", "onesb"], ["psm"])
                    yield
                    for hh in range(4):
                        self.stt("dve", Cm[:, hh, :], Cm[:, hh, :], T["m_ebl"][:, hh:hh + 1], mA[0:64, hh * 128:(hh + 1) * 128],
                                 ALU.mult, ALU.add, ["Cm", "m_ebl", "pp0"], ["Cm"])
                    self.tt("dve", nm[:], nm[:], T["m_ebl"][:], ALU.mult, ["nm", "m_ebl"], ["nm"])
                    self.tt("dve", nm[:], nm[:], psm[0:64, 52:56], ALU.add, ["nm", "psm"], ["nm"])
                    self.cp("act", Cmb[:], Cm[:], ["Cm"], ["Cmb"])
                    self.cp("act", nmb[:], nm[:], ["nm"], ["nmb"])
                    yield
                gens = [gdn_gen(), mls_gen()]
                while gens:
                    for g_ in list(gens):
                        try:
                            next(g_)
                        except StopIteration:
                            gens.remove(g_)
            self.dma("sp", self.oa_d[:, t0:t0 + TT].rearrange("(h p) t -> p h t", p=128), oaT[:], ["oaT"], [])
            self.dma("sp", self.ob_d[:, t0:t0 + TT].rearrange("(h p) t -> p h t", p=128), obT[:], ["obT"], [])

    def phase_1b(self, st, l, xsrc):
        sb, ps = self.sb, self.ps
        Wg = sb(st, "Wg", [128, KD, 2048], BF16)
        Wba = sb(st, "Wba", [128, 4, D], BF16)
        Wbb = sb(st, "Wbb", [128, 4, D], BF16)
        Wo = sb(st, "Wo", [128, KD, D], BF16)
        self.load_w(Wg, self.w_in[l], KD, 2048, "Wg", col0=GTA, split=2)
        self.load_w(Wba, self.w_ba[l], 4, D, "Wba", split=1)
        self.load_w(Wbb, self.w_bb[l], 4, D, "Wbb", split=1)
        self.load_w(Wo, self.w_out[l], KD, D, "Wo", split=1)
        xin = sb(st, "xin", [128, KD, TT], F32)
        sq = sb(st, "sq", [128, KD, TT], BF16)
        hT = sb(st, "hT", [128, KD, TT], BF16)
        rstd = sb(st, "rstd", [128, TT], F32)
        oaT = sb(st, "oaT", [128, 4, TT], BF16)
        obT = sb(st, "obT", [128, 4, TT], BF16)
        ga = [sb(st, f"ga{i}", [128, TT], F32) for i in range(2)]
        gb = [sb(st, f"gb{i}", [128, TT], F32) for i in range(2)]
        t1 = [sb(st, f"t1{i}", [128, TT], F32) for i in range(2)]
        t2 = [sb(st, f"t2{i}", [128, TT], F32) for i in range(2)]
        mixed = sb(st, "mixed", [128, KD, TT], BF16)
        ppb = [ps(st, f"ppb{i}", [128, 512], F32) for i in range(4)]
        pp = [ppb[i // 2][:, (i % 2) * 256:(i % 2 + 1) * 256] for i in range(8)]
        pss = ps(st, "pss", [128, 256], F32)
        for tt in range(self.NT):
            t0 = tt * TT
            self.dma("sp", xin[:], xsrc[:, t0:t0 + TT].rearrange("(k p) t -> p k t", p=128), [], ["xin"])
            self.dma("sp", oaT[:], self.oa_d[:, t0:t0 + TT].rearrange("(h p) t -> p h t", p=128), [], ["oaT"])
            self.dma("sp", obT[:], self.ob_d[:, t0:t0 + TT].rearrange("(h p) t -> p h t", p=128), [], ["obT"])
            self.rmsnorm(xin, sq, hT, rstd, pss[:], l * 24 + 0, ("xin", "sq", "hT", "rstd", "pss"))
            for oc in range(KD):
                i = oc % 2
                ocs = slice(oc * 128, (oc + 1) * 128)
                p0, p1, p2, p3 = pp[4 * i], pp[4 * i + 1], pp[4 * i + 2], pp[4 * i + 3]
                k0 = k1 = f"ppb{2 * i}"
                k2 = k3 = f"ppb{2 * i + 1}"
                for kc in range(KD):
                    self.mm(p0[:], Wg[:, kc, oc * 128:(oc + 1) * 128], hT[:, kc, :], kc == 0, kc == KD - 1, ["Wg", "hT"], [k0])
                for kc in range(KD):
                    self.mm(p1[:], Wg[:, kc, 1024 + oc * 128:1024 + (oc + 1) * 128], hT[:, kc, :], kc == 0, kc == KD - 1, ["Wg", "hT"], [k1])
                for hh in range(4):
                    self.mm(p2[:], Wba[:, hh, ocs], oaT[:, hh, :], hh == 0, hh == 3, ["Wba", "oaT"], [k2])
                for hh in range(4):
                    self.mm(p3[:], Wbb[:, hh, ocs], obT[:, hh, :], hh == 0, hh == 3, ["Wbb", "obT"], [k3])
                self.act(ga[i][:], p0[:], AF.Sigmoid, [k0], [f"ga{i}"])
                self.act(gb[i][:], p1[:], AF.Sigmoid, [k1], [f"gb{i}"])
                self.tt("dve", t1[i][:], p2[:], ga[i][:], ALU.mult, [k2, f"ga{i}"], [f"t1{i}"])
                self.tt("dve", t2[i][:], p3[:], gb[i][:], ALU.mult, [k3, f"gb{i}"], [f"t2{i}"])
                self.tt("pool", mixed[:, oc, :], t1[i][:], t2[i][:], ALU.add, [f"t1{i}", f"t2{i}"], ["mixed"])
            for oc in range(KD):
                i = oc % 2
                p0, k0 = pp[4 * i], f"ppb{2 * i}"
                for kc in range(KD):
                    self.mm(p0[:], Wo[:, kc, oc * 128:(oc + 1) * 128], mixed[:, kc, :], kc == 0, kc == KD - 1, ["Wo", "mixed"], [k0])
                self.tt("dve", xin[:, oc, :], xin[:, oc, :], p0[:], ALU.add, ["xin", k0], ["xin"])
            self.dma("sp", self.xa[:, t0:t0 + TT].rearrange("(k p) t -> p k t", p=128), xin[:], ["xin"], [])

    def phase_2(self, st, l, last):
        sb, ps = self.sb, self.ps
        W1 = sb(st, "W1", [128, KD, D_FF], BF16)
        W3 = sb(st, "W3", [128, KD, D_FF], BF16)
        W2 = sb(st, "W2", [128, NFF, D], BF16)
        Wpg = sb(st, "Wpg", [128, KD, D], BF16)
        Wpl = sb(st, "Wpl", [128, 2, D], BF16)
        self.load_w(W1, self.w1[l], KD, D_FF, "W1", split=2)
        self.load_w(W3, self.w3[l], KD, D_FF, "W3", split=2)
        self.load_w(W2, self.w2[l], NFF, D, "W2", split=1)
        self.load_w(Wpg, self.w_pg[l], KD, D, "Wpg", split=1)
        self.load_w(Wpl, self.w_ple[l], 2, D, "Wpl", split=1)
        xin = sb(st, "xin", [128, KD, TT], F32)
        sq = sb(st, "sq", [128, KD, TT], BF16)
        hT = sb(st, "hT", [128, KD, TT], BF16)
        rstd = sb(st, "rstd", [128, TT], F32)
        G = sb(st, "G", [128, NFF, TT], BF16)
        sa = [sb(st, f"sa{i}", [128, TT], F32) for i in range(2)]
        pTb = sb(st, "pTb", [128, 2, TT], BF16)
        gt = [sb(st, f"gt{i}", [128, TT], F32) for i in range(2)]
        tp = [sb(st, f"tp{i}", [128, TT], F32) for i in range(2)]
        ppb = [ps(st, f"ppb{i}", [128, 512], F32) for i in range(4)]
        pp = [ppb[i // 2][:, (i % 2) * 256:(i % 2 + 1) * 256] for i in range(8)]
        pss = ps(st, "pss", [128, 256], F32)
        gbase = self.L * 24
        for tt in range(self.NT):
            t0 = tt * TT
            self.dma("sp", xin[:], self.xa[:, t0:t0 + TT].rearrange("(k p) t -> p k t", p=128), [], ["xin"])
            self.dma("pool", pTb[:], self.pT[l, :, t0:t0 + TT].rearrange("(k p) t -> p k t", p=128), [], ["pTb"])
            self.rmsnorm(xin, sq, hT, rstd, pss[:], l * 24 + 8, ("xin", "sq", "hT", "rstd", "pss"))
            for j in range(NFF):
                i = j % 2
                pa, pb = pp[2 * i], pp[2 * i + 1]
                ka = kb = f"ppb{i}"
                for kc in range(KD):
                    self.mm(pa[:], W1[:, kc, j * 128:(j + 1) * 128], hT[:, kc, :], kc == 0, kc == KD - 1, ["W1", "hT"], [ka])
                for kc in range(KD):
                    self.mm(pb[:], W3[:, kc, j * 128:(j + 1) * 128], hT[:, kc, :], kc == 0, kc == KD - 1, ["W3", "hT"], [kb])
                self.act(sa[i][:], pa[:], AF.Silu, [ka], [f"sa{i}"])
                self.tt("dve", G[:, j, :], sa[i][:], pb[:], ALU.mult, [f"sa{i}", kb], ["G"])
            for oc in range(KD):
                i = oc % 2
                p0, k0 = pp[4 + 2 * i], f"ppb{2 + i}"
                for j in range(NFF):
                    self.mm(p0[:], W2[:, j, oc * 128:(oc + 1) * 128], G[:, j, :], j == 0, j == NFF - 1, ["W2", "G"], [k0])
                self.tt("dve", xin[:, oc, :], xin[:, oc, :], p0[:], ALU.add, ["xin", k0], ["xin"])
            self.rmsnorm(xin, sq, hT, rstd, pss[:], l * 24 + 16, ("xin", "sq", "hT", "rstd", "pss"))
            for oc in range(KD):
                i = oc % 2
                p0, k0 = pp[4 + 2 * i], f"ppb{2 + i}"
                p1, k1 = pp[5 + 2 * i], f"ppb{2 + i}"
                for kc in range(KD):
                    self.mm(p0[:], Wpg[:, kc, oc * 128:(oc + 1) * 128], hT[:, kc, :], kc == 0, kc == KD - 1, ["Wpg", "hT"], [k0])
                for k2 in range(2):
                    self.mm(p1[:], Wpl[:, k2, oc * 128:(oc + 1) * 128], pTb[:, k2, :], k2 == 0, k2 == 1, ["Wpl", "pTb"], [k1])
                self.act(gt[i][:], p0[:], AF.Sigmoid, [k0], [f"gt{i}"])
                self.tt("dve", tp[i][:], gt[i][:], p1[:], ALU.mult, [f"gt{i}", k1], [f"tp{i}"])
                self.tt("pool", xin[:, oc, :], xin[:, oc, :], tp[i][:], ALU.add, ["xin", f"tp{i}"], ["xin"])
            if not last:
                self.dma("sp", self.xb[:, t0:t0 + TT].rearrange("(k p) t -> p k t", p=128), xin[:], ["xin"], [])
            else:
                self.act(sq[:], xin[:], AF.Square, ["xin"], ["sq"])
                for kc in range(KD):
                    self.mm(pss[:], self.onesb[:], sq[:, kc, :], kc == 0, kc == KD - 1, ["sq", "onesb"], ["pss"])
                self.act(rstd[:], pss[:], AF.Ln, ["pss"], ["rstd"], bias=float(D * EPS))
                self.act(rstd[:], rstd[:], AF.Exp, ["rstd"], ["rstd"], scale=-0.5)
                for kc in range(KD):
                    self.stt("dve", xin[:, kc, :], xin[:, kc, :], self.g32[:, gbase + kc:gbase + kc + 1], rstd[:],
                             ALU.mult, ALU.mult, ["xin", "rstd", "g32"], ["xin"])
                self.dma("sp", self.outT[:, t0:t0 + TT].rearrange("(k p) t -> p k t", p=128), xin[:], ["xin"], [], final=True)


def pack_vec(inp, L):
    v = np.zeros((128, L, NV), np.float32)
    for l in range(L):
        v[:, l, V_GMIX:V_GMIX + 8] = inp["g_mix"][l].reshape(8, 128).T
        v[:, l, V_GFFN:V_GFFN + 8] = inp["g_ffn"][l].reshape(8, 128).T
        v[:, l, V_GPLE:V_GPLE + 8] = inp["g_ple"][l].reshape(8, 128).T
        v[:, l, V_CONV:V_CONV + 48] = inp["conv_w"][l].reshape(4, 12, 128).transpose(2, 1, 0).reshape(128, 48)
        v[:, l, V_GDNN] = inp["gdn_norm"][l]
        v[:, l, V_MLN:V_MLN + 4] = inp["ml_norm"][l].T
        v[:, l, V_ALOG:V_ALOG + 4] = inp["a_log"][l][None, :]
        v[:, l, V_DTB:V_DTB + 4] = inp["dt_bias"][l][None, :]
        v[:, l, V_IB:V_IB + 4] = inp["ml_i_bias"][l][None, :]
        v[:, l, V_FB:V_FB + 4] = inp["ml_f_bias"][l][None, :]
    return np.ascontiguousarray(v.reshape(128, L * NV))


_CACHE = {}


def get_nc(S, L):
    key = (S, L)
    if key not in _CACHE:
        _CACHE[key] = Builder(S, L).build()
    return _CACHE[key]


def make_in_maps(inp, S, L, batches):
    f = lambda a: np.ascontiguousarray(np.asarray(a, dtype=np.float32))
    shared = {
        "vec": pack_vec({k: np.asarray(v) for k, v in inp.items()}, L),
        "gfin": np.ascontiguousarray(np.asarray(inp["g_final"], np.float32).reshape(8, 128).T),
        "cst": make_consts(),
        "w_in": f(inp["w_in"]), "w_branch_a": f(inp["w_branch_a"]), "w_branch_b": f(inp["w_branch_b"]),
        "w_out": f(inp["w_out"]), "w1": f(inp["w1"]), "w3": f(inp["w3"]), "w2": f(inp["w2"]),
        "w_ple_gate": f(inp["w_ple_gate"]), "w_ple": f(inp["w_ple"]),
    }
    maps = []
    for b in batches:
        m = dict(shared)
        m["xT"] = np.ascontiguousarray(np.asarray(inp["x"][b], np.float32).T)
        m["pT"] = np.ascontiguousarray(np.asarray(inp["p"][:, b], np.float32).transpose(0, 2, 1))
        maps.append(m)
    return maps


def kernel(**inputs):
    x = np.asarray(inputs["x"])
    B, S, _ = x.shape
    L = int(np.asarray(inputs["w_in"]).shape[0])
    nc = get_nc(S, L)
    batches = [i % B for i in range(8)]
    maps = make_in_maps(inputs, S, L, batches)
    res = run_bass_kernel_spmd(nc, maps, core_ids=list(range(8)))
    out = np.empty((B, S, D), np.float32)
    for b in range(B):
        out[b] = res.results[b]["outT"].T
    return out
```
